# Optimizing a Trainium2 kernel written in Bass

```python
import math
import jax, jax.numpy as jnp
from jax import lax
import numpy as np

D_MODEL = 1024
BATCH = 16
SEQ = 2048
DEPTH = 1

MLA_HEADS = 8
MLA_NOPE_DIM = 64
MLA_ROPE_DIM = 32
MLA_V_DIM = 64
MLA_Q_RANK = 384
MLA_KV_RANK = 256
ROPE_THETA = 10000.0
DIFF_HEADS = 8
DIFF_HEAD_DIM = D_MODEL // DIFF_HEADS // 2
DIFF_V_DIM = 2 * DIFF_HEAD_DIM
D_FF = 4 * D_MODEL
N_BRANCHES = 2
Q_BLOCK = 128
LN_EPS = 1e-5
RMS_EPS = 1e-6
NEG_INF = -1e30
MAX_POS_OFFSET = 4096
DEEPNORM_ALPHA = (2.0 * DEPTH) ** 0.25
DEEPNORM_BETA = (8.0 * DEPTH) ** -0.25

DIFF_Q_COLS = DIFF_HEADS * 2 * DIFF_HEAD_DIM
DIFF_K_COLS = DIFF_HEADS * 2 * DIFF_HEAD_DIM
DIFF_V_COLS = DIFF_HEADS * DIFF_V_DIM
GATE_COLS = N_BRANCHES * D_MODEL
_O1 = MLA_Q_RANK
_O2 = _O1 + MLA_KV_RANK
_O3 = _O2 + MLA_ROPE_DIM
_O4 = _O3 + DIFF_Q_COLS
_O5 = _O4 + DIFF_K_COLS
_O6 = _O5 + DIFF_V_COLS
D_IN = _O6 + GATE_COLS
SPLIT_POINTS = (_O1, _O2, _O3, _O4, _O5, _O6)

kernel_name = "hybrid_mla_diffattn_alibi_deepnorm"


def layer_norm(x, g, b):
    xf = x.astype(jnp.float32)
    mu = jnp.mean(xf, axis=-1, keepdims=True)
    var = jnp.mean(jnp.square(xf - mu), axis=-1, keepdims=True)
    return ((xf - mu) * lax.rsqrt(var + LN_EPS) * g.astype(jnp.float32) + b.astype(jnp.float32)).astype(x.dtype)


def rms_norm(x, g):
    xf = x.astype(jnp.float32)
    ms = jnp.mean(jnp.square(xf), axis=-1, keepdims=True)
    return (xf * lax.rsqrt(ms + RMS_EPS) * g.astype(jnp.float32)).astype(x.dtype)


def rope_angles(positions):
    inv_freq = 1.0 / (ROPE_THETA ** (jnp.arange(0, MLA_ROPE_DIM, 2, dtype=jnp.float32) / MLA_ROPE_DIM))
    ang = positions.astype(jnp.float32)[..., None] * inv_freq
    return jnp.cos(ang), jnp.sin(ang)


def apply_rope(t, cos, sin):
    t1, t2 = jnp.split(t.astype(jnp.float32), 2, axis=-1)
    return jnp.concatenate([t1 * cos - t2 * sin, t2 * cos + t1 * sin], axis=-1).astype(t.dtype)


def alibi_slopes(n):
    def pow2_slopes(k):
        start = 2.0 ** (-8.0 / k)
        return [start ** (i + 1) for i in range(k)]
    if math.log2(n).is_integer():
        s = pow2_slopes(n)
    else:
        c = 2 ** int(math.floor(math.log2(n)))
        s = pow2_slopes(c) + pow2_slopes(2 * c)[0::2][: n - c]
    return np.asarray(s, dtype=np.float32)


def _to_blocks(t):
    b, s = t.shape[:2]
    t = t.reshape((b, s // Q_BLOCK, Q_BLOCK) + t.shape[2:])
    return jnp.moveaxis(t, 1, 0)


def _from_blocks(t):
    t = jnp.moveaxis(t, 0, 1)
    return t.reshape((t.shape[0], t.shape[1] * t.shape[2]) + t.shape[3:])


def mla_attention(q_nope, q_pe, k_nope, k_pe, v):
    seq = q_nope.shape[1]
    scale = (MLA_NOPE_DIM + MLA_ROPE_DIM) ** -0.5
    k_idx = jnp.arange(seq)

    def one_block(args):
        qn, qp, blk = args
        q_idx = blk * Q_BLOCK + jnp.arange(Q_BLOCK)
        s = (jnp.einsum('bqhd,bkhd->bhqk', qn, k_nope, preferred_element_type=jnp.float32)
             + jnp.einsum('bqhr,bkr->bhqk', qp, k_pe, preferred_element_type=jnp.float32)) * scale
        s = jnp.where(k_idx[None, :] <= q_idx[:, None], s, NEG_INF)
        p = jax.nn.softmax(s, axis=-1).astype(v.dtype)
        return jnp.einsum('bhqk,bkhe->bqhe', p, v)

    out = lax.map(one_block, (_to_blocks(q_nope), _to_blocks(q_pe), jnp.arange(seq // Q_BLOCK)))
    return _from_blocks(out)


def diff_attention(q, k, v, positions, lam, slopes):
    seq = q.shape[1]
    scale = DIFF_HEAD_DIM ** -0.5
    k_idx = jnp.arange(seq)

    def one_block(args):
        qb, pq, blk = args
        q_idx = blk * Q_BLOCK + jnp.arange(Q_BLOCK)
        s = jnp.einsum('bqhcd,bkhcd->bhcqk', qb, k, preferred_element_type=jnp.float32) * scale
        dist = jnp.abs(pq[:, :, None] - positions[:, None, :]).astype(jnp.float32)
        s = s - slopes[None, :, None, None, None] * dist[:, None, None]
        s = jnp.where(k_idx[None, :] <= q_idx[:, None], s, NEG_INF)
        p = jax.nn.softmax(s, axis=-1)
        a = p[:, :, 0] - lam * p[:, :, 1]
        return jnp.einsum('bhqk,bkhe->bqhe', a.astype(v.dtype), v)

    out = lax.map(one_block, (_to_blocks(q), _to_blocks(positions), jnp.arange(seq // Q_BLOCK)))
    return _from_blocks(out)


def setup_inputs(seed: int = 0) -> dict:
    key = jax.random.key(seed)
    ks = jax.random.split(key, 24)
    L = DEPTH
    beta = DEEPNORM_BETA

    def dense(k, fan_in, fan_out, scale=1.0):
        return jax.random.normal(k, (L, fan_in, fan_out), jnp.float32) * (scale * fan_in ** -0.5)

    def gain(k, n):
        return 1.0 + 0.02 * jax.random.normal(k, (L, n), jnp.float32)

    def small(k, n, s=0.02):
        return s * jax.random.normal(k, (L, n), jnp.float32)

    x = jax.random.normal(ks[0], (BATCH, SEQ, D_MODEL), jnp.float32)
    offset = jax.random.randint(ks[1], (BATCH, 1), 0, MAX_POS_OFFSET, dtype=jnp.int32)
    positions = (offset + jnp.arange(SEQ, dtype=jnp.int32)[None, :]).astype(jnp.int32)

    in_scale = np.ones((D_IN,), np.float32)
    in_scale[_O5:_O6] = beta
    w_in = dense(ks[2], D_MODEL, D_IN) * jnp.asarray(in_scale)
    b_gate = small(ks[3], GATE_COLS)

    mla_q_norm = gain(ks[4], MLA_Q_RANK)
    mla_kv_norm = gain(ks[5], MLA_KV_RANK)
    w_uq = dense(ks[6], MLA_Q_RANK, MLA_HEADS * (MLA_NOPE_DIM + MLA_ROPE_DIM))
    ukv_scale = np.tile(np.concatenate([np.ones(MLA_NOPE_DIM, np.float32), np.full(MLA_V_DIM, beta, np.float32)]), MLA_HEADS)
    w_ukv = dense(ks[7], MLA_KV_RANK, MLA_HEADS * (MLA_NOPE_DIM + MLA_V_DIM)) * jnp.asarray(ukv_scale)
    w_o_mla = dense(ks[8], MLA_HEADS * MLA_V_DIM, D_MODEL, beta)

    diff_lambda_q1 = small(ks[9], DIFF_HEAD_DIM, 0.1)
    diff_lambda_k1 = small(ks[10], DIFF_HEAD_DIM, 0.1)
    diff_lambda_q2 = small(ks[11], DIFF_HEAD_DIM, 0.1)
    diff_lambda_k2 = small(ks[12], DIFF_HEAD_DIM, 0.1)
    diff_subln = gain(ks[13], DIFF_V_DIM)
    w_o_diff = dense(ks[14], DIFF_HEADS * DIFF_V_DIM, D_MODEL, beta)

    w_o = dense(ks[15], D_MODEL, D_MODEL, beta)
    ln1_g = gain(ks[16], D_MODEL)
    ln1_b = small(ks[17], D_MODEL)
    w_up = dense(ks[18], D_MODEL, D_FF, beta)
    w_down = dense(ks[19], D_FF, D_MODEL, beta)
    ln2_g = gain(ks[20], D_MODEL)
    ln2_b = small(ks[21], D_MODEL)
    return {"x": x, "positions": positions, "w_in": w_in, "b_gate": b_gate,
            "mla_q_norm": mla_q_norm, "mla_kv_norm": mla_kv_norm, "w_uq": w_uq, "w_ukv": w_ukv, "w_o_mla": w_o_mla,
            "diff_lambda_q1": diff_lambda_q1, "diff_lambda_k1": diff_lambda_k1,
            "diff_lambda_q2": diff_lambda_q2, "diff_lambda_k2": diff_lambda_k2,
            "diff_subln": diff_subln, "w_o_diff": w_o_diff, "w_o": w_o,
            "ln1_g": ln1_g, "ln1_b": ln1_b, "w_up": w_up, "w_down": w_down, "ln2_g": ln2_g, "ln2_b": ln2_b}


def reference(x, positions, w_in, b_gate, mla_q_norm, mla_kv_norm, w_uq, w_ukv, w_o_mla,
              diff_lambda_q1, diff_lambda_k1, diff_lambda_q2, diff_lambda_k2, diff_subln, w_o_diff, w_o,
              ln1_g, ln1_b, w_up, w_down, ln2_g, ln2_b):
    b, s, _ = x.shape
    cos, sin = rope_angles(positions)
    slopes = jnp.asarray(alibi_slopes(DIFF_HEADS))
    for layer in range(DEPTH):
        lambda_init = 0.8 - 0.6 * math.exp(-0.3 * layer)
        proj = x @ w_in[layer]
        c_q, c_kv, k_pe, dq, dk, dv, gate = jnp.split(proj, SPLIT_POINTS, axis=-1)

        c_q = rms_norm(c_q, mla_q_norm[layer])
        q = (c_q @ w_uq[layer]).reshape(b, s, MLA_HEADS, MLA_NOPE_DIM + MLA_ROPE_DIM)
        q_nope, q_pe = q[..., :MLA_NOPE_DIM], q[..., MLA_NOPE_DIM:]
        q_pe = apply_rope(q_pe, cos[:, :, None], sin[:, :, None])
        k_pe = apply_rope(k_pe, cos, sin)
        c_kv = rms_norm(c_kv, mla_kv_norm[layer])
        kv = (c_kv @ w_ukv[layer]).reshape(b, s, MLA_HEADS, MLA_NOPE_DIM + MLA_V_DIM)
        k_nope, v_mla = kv[..., :MLA_NOPE_DIM], kv[..., MLA_NOPE_DIM:]
        o_mla = mla_attention(q_nope, q_pe, k_nope, k_pe, v_mla).reshape(b, s, MLA_HEADS * MLA_V_DIM)
        y_mla = o_mla @ w_o_mla[layer]

        lam = (jnp.exp(jnp.sum(diff_lambda_q1[layer].astype(jnp.float32) * diff_lambda_k1[layer].astype(jnp.float32)))
               - jnp.exp(jnp.sum(diff_lambda_q2[layer].astype(jnp.float32) * diff_lambda_k2[layer].astype(jnp.float32)))
               + lambda_init)
        o_diff = diff_attention(dq.reshape(b, s, DIFF_HEADS, 2, DIFF_HEAD_DIM),
                                dk.reshape(b, s, DIFF_HEADS, 2, DIFF_HEAD_DIM),
                                dv.reshape(b, s, DIFF_HEADS, DIFF_V_DIM),
                                positions, lam, slopes)
        o_diff = rms_norm(o_diff, diff_subln[layer]) * (1.0 - lambda_init)
        y_diff = o_diff.reshape(b, s, DIFF_HEADS * DIFF_V_DIM) @ w_o_diff[layer]

        g = jax.nn.sigmoid(gate + b_gate[layer]).reshape(b, s, N_BRANCHES, D_MODEL)
        mixed = (g[:, :, 0] * y_mla + g[:, :, 1] * y_diff) @ w_o[layer]
        x = layer_norm(DEEPNORM_ALPHA * x + mixed, ln1_g[layer], ln1_b[layer])

        h = jnp.square(jax.nn.relu(x @ w_up[layer]))
        x = layer_norm(DEEPNORM_ALPHA * x + h @ w_down[layer], ln2_g[layer], ln2_b[layer])
    return x
```

```python
import math
import numpy as np
from contextlib import ExitStack
import concourse.bass as bass
import concourse.mybir as mybir
from concourse.bass_utils import run_bass_kernel_spmd

F32 = mybir.dt.float32
BF16 = mybir.dt.bfloat16
I32 = mybir.dt.int32
ALU = mybir.AluOpType
AF = mybir.ActivationFunctionType
AX = mybir.AxisListType

N_CORES = 8
SEQ = 2048
D = 1024
NT = 16
NCH = 4
O1_, O2_, O3_, O4_, O5_, O6_ = 384, 640, 672, 1696, 2720, 3744
D_IN = 5792
ALPHA = 2.0 ** 0.25
LAMBDA_INIT = 0.2
MLA_SCALE = 96.0 ** -0.5
DIFF_SCALE = 0.125
LN_EPS = 1e-5
RMS_EPS = 1e-6
TWO_PI = 2.0 * math.pi
CW1 = 6.28125
CW2 = TWO_PI - CW1
PI_SAFE = 3.1415925


def alibi_slopes(n):
    start = 2.0 ** (-8.0 / n)
    return [start ** (i + 1) for i in range(n)]


class Tok:
    __slots__ = ("w", "r")

    def __init__(self):
        self.w = None
        self.r = []


class Op:
    __slots__ = ("eng", "fn", "deps", "is_dma", "sem", "val", "signaled", "slot")

    def __init__(self, eng, fn, is_dma=False, slot=None):
        self.eng = eng
        self.fn = fn
        self.deps = []
        self.is_dma = is_dma
        self.sem = None
        self.val = None
        self.signaled = False
        self.slot = slot


ENGS = ("pe", "act", "dve", "pool", "sp")


def C(method, *a, **k):
    return lambda e: getattr(e, method)(*a, **k)


class Sched:
    def __init__(self, nc):
        self.nc = nc
        self.ops = {e: [] for e in ENGS}
        self.all_ops = []
        self.slot_last = {}

    def tok(self):
        return Tok()

    def _add(self, op, reads, writes):
        deps = {}
        for t in reads:
            if t.w is not None:
                deps[id(t.w)] = (t.w, True)
        for t in writes:
            if t.w is not None:
                deps[id(t.w)] = (t.w, True)
            for r in t.r:
                if id(r) not in deps:
                    deps[id(r)] = (r, False)
        for d, hard in deps.values():
            if d is op:
                continue
            if d.is_dma:
                if op.is_dma and d.slot == op.slot:
                    continue
                d = self.slot_last[d.slot]
            if (not d.is_dma) and d.eng == op.eng and not op.is_dma:
                if op.eng == "pe":
                    continue
            op.deps.append(d)
            d.signaled = True
        for t in reads:
            t.r.append(op)
        for t in writes:
            t.w = op
            t.r = []
        self.ops[op.eng].append(op)
        self.all_ops.append(op)
        return op

    def op(self, eng, fn, reads=(), writes=()):
        return self._add(Op(eng, fn), list(reads), list(writes))

    def dma(self, eng, fn, slot, reads=(), writes=()):
        o = Op(eng, fn, is_dma=True, slot=slot)
        o.signaled = True
        r = self._add(o, list(reads), list(writes))
        self.slot_last[slot] = o
        return r

    def barrier(self):
        last = {}
        for e in ENGS:
            for o in reversed(self.ops[e]):
                if not o.is_dma and o.fn is not None:
                    last[e] = o
                    break
        lastd = dict(self.slot_last)
        for e in ENGS:
            b = Op(e, None)
            for e2, o in last.items():
                if e2 != e:
                    b.deps.append(o)
                    o.signaled = True
            for o in lastd.values():
                b.deps.append(o)
            self.ops[e].append(b)
            self.all_ops.append(b)

    def emit(self, stack):
        nc = self.nc
        sems = {e: stack.enter_context(nc.semaphore("s_" + e)) for e in ENGS}
        slot_sem = {}
        slot_cnt = {}
        for o in self.all_ops:
            if o.is_dma:
                if o.slot not in slot_sem:
                    slot_sem[o.slot] = stack.enter_context(nc.semaphore("d_" + str(o.slot)))
                    slot_cnt[o.slot] = 0
                slot_cnt[o.slot] += 16
                o.sem = slot_sem[o.slot]
                o.val = slot_cnt[o.slot]
        for e in ENGS:
            c = 0
            for o in self.ops[e]:
                if not o.is_dma and o.signaled and o.fn is not None:
                    c += 1
                    o.sem = sems[e]
                    o.val = c
        block = stack.enter_context(nc.Block())
        engmap = {"pe": block.tensor, "act": block.scalar, "dve": block.vector,
                  "pool": block.gpsimd, "sp": block.sync}
        final_waits = [(slot_sem[s], slot_cnt[s]) for s in slot_sem]

        def make(ename):
            ops = self.ops[ename]

            def body(eng):
                waited = {}
                for o in ops:
                    for d in o.deps:
                        if d.sem is None:
                            continue
                        k = id(d.sem)
                        if waited.get(k, 0) >= d.val:
                            continue
                        eng.wait_ge(d.sem, d.val)
                        waited[k] = d.val
                    if o.fn is None:
                        continue
                    ins = o.fn(eng)
                    if o.is_dma:
                        ins.then_inc(o.sem, 16)
                    elif o.signaled:
                        ins.then_inc(o.sem, 1)
                if ename == "sp":
                    for s, v in final_waits:
                        if waited.get(id(s), 0) < v:
                            eng.wait_ge(s, v)
            return body

        for e in ENGS:
            engmap[e](make(e))


class Arena:
    def __init__(self, nc):
        self.nc = nc
        self.lo_ptr = (int(nc.sbuf_base) + 63) // 64 * 64
        self.hi_ptr = int(nc.sbuf_top) // 64 * 64
        self.n = 0

    @staticmethod
    def _bytes(shape, dt):
        n = 1
        for d in shape[1:]:
            n *= d
        return (n * mybir.dt.size(dt) + 63) // 64 * 64

    def lo(self, name, shape, dt):
        nb = self._bytes(shape, dt)
        off = self.lo_ptr
        self.lo_ptr += nb
        assert self.lo_ptr <= self.hi_ptr, "SBUF arena overflow at %s: lo=%d hi=%d" % (name, self.lo_ptr, self.hi_ptr)
        self.n += 1
        return self.nc.alloc_sbuf_tensor_at("%s_%d" % (name, self.n), list(shape), dt, offset=off)

    def hi(self, name, shape, dt):
        nb = self._bytes(shape, dt)
        self.hi_ptr -= nb
        assert self.lo_ptr <= self.hi_ptr, "SBUF arena overflow at %s: lo=%d hi=%d" % (name, self.lo_ptr, self.hi_ptr)
        self.n += 1
        return self.nc.alloc_sbuf_tensor_at("%s_%d" % (name, self.n), list(shape), dt, offset=self.hi_ptr)


def build(nseq, stage=99, dbg=False):
    nc = bass.Bass("TRN2", target_bir_lowering=False)

    def din(name, shape, dt=F32):
        return nc.dram_tensor(name, list(shape), dt, kind="ExternalInput").ap()

    x = din("x", [nseq, SEQ, D])
    pos = din("pos", [nseq, SEQ], I32)
    w_in = din("w_in", [D, D_IN])
    w_kpe = din("w_kpe", [D, 192])
    w_uq = din("w_uq", [384, 768])
    w_uqs = din("w_uqs", [384, 768])
    w_ukv = din("w_ukv", [256, 1024])
    w_omla = din("w_omla", [512, D])
    w_odiff = din("w_odiff", [D, D])
    w_o = din("w_o", [D, D])
    w_up = din("w_up", [D, 4 * D])
    w_down = din("w_down", [4 * D, D])
    b_gate = din("b_gate", [2048])
    g_q = din("g_q", [384])
    g_kv = din("g_kv", [256])
    g_sub = din("g_sub", [128])
    lam_in = [din("lam%d" % i, [1, 64]) for i in range(4)]
    ln_in = [din("ln%d" % i, [1, D]) for i in range(4)]
    cvec = din("cvec", [128, 2])
    out = nc.dram_tensor("out", [nseq, SEQ, D], F32, kind="ExternalOutput").ap()
    dbg_out = {}

    def scr(name, shape):
        return nc.dram_tensor(name, list(shape), BF16, kind="Internal").ap()

    wA_s = scr("wA_s", [128, 8, 832])
    wuq_s = scr("wuq_s", [128, 3, 1536])
    wukv_s = scr("wukv_s", [128, 2, 1024])
    wdv_s = scr("wdv_s", [128, 8, 1024])
    wdqk_s = scr("wdqk_s", [8, 128, 8, 256])
    wmg_s = scr("wmg_s", [8, 128, 28, 128])
    wo_s = scr("wo_s", [128, 8, 1024])
    wup_s = scr("wup_s", [8, 128, 8, 512])
    wdn_s = scr("wdn_s", [8, 128, 4, 1024])
    xT_s = scr("xT_s", [128, 8, SEQ])
    OdT_s = scr("OdT_s", [128, 8, SEQ])

    S = Sched(nc)
    slopes = alibi_slopes(8)

    with ExitStack() as top:
        A = Arena(nc)

        def sbuf(st, name, shape, dt):
            return A.lo(name, shape, dt) if st == "lo" else A.hi(name, shape, dt)

        PS = [top.enter_context(nc.psum_tensor("ps%d" % i, [128, 512], F32)) for i in range(8)]
        tPS = [S.tok() for _ in range(8)]

        ident = sbuf("lo", "ident", [128, 128], F32)
        ones_b = sbuf("lo", "ones_b", [128, 128], BF16)
        ones_f = sbuf("lo", "ones_f", [128, 128], F32)
        tri = sbuf("lo", "tri", [128, 128], BF16)
        t_const = S.tok()
        S.op("pool", C("memset", ident[:], 1.0), writes=[t_const])
        S.op("pool", C("affine_select", out=ident[:], in_=ident[:], pattern=[[-1, 128]], base=0,
                                               channel_multiplier=1, compare_op=ALU.is_equal, fill=0.0),
             reads=[t_const], writes=[t_const])
        S.op("pool", C("memset", tri[:], 1.0), writes=[t_const])
        S.op("pool", C("affine_select", out=tri[:], in_=tri[:], pattern=[[1, 128]], base=0,
                                               channel_multiplier=-1, compare_op=ALU.is_ge, fill=0.0),
             reads=[t_const], writes=[t_const])
        S.op("pool", C("memset", ones_b[:], 1.0), writes=[t_const])
        S.op("pool", C("memset", ones_f[:], 1.0), writes=[t_const])
        eps_rms = sbuf("lo", "eps_rms", [128, 1], F32)
        eps_ln = sbuf("lo", "eps_ln", [128, 1], F32)
        S.op("pool", C("memset", eps_rms[:], RMS_EPS), writes=[t_const])
        S.op("pool", C("memset", eps_ln[:], LN_EPS), writes=[t_const])

        bg = sbuf("lo", "bg", [128, 16, 1], F32)
        gq = sbuf("lo", "gq", [128, 3, 1], F32)
        gkv = sbuf("lo", "gkv", [128, 2, 1], F32)
        gsub = sbuf("lo", "gsub", [128, 1], F32)
        cv = sbuf("lo", "cv", [128, 2], F32)
        lamt = sbuf("lo", "lamt", [128, 4, 64], F32)
        lamw = sbuf("lo", "lamw", [128, 8], F32)
        t_small = S.tok()
        S.dma("sp", C("dma_start", out=bg[:], in_=b_gate.rearrange("(c p o) -> p c o", p=128, o=1), allow_slow_non_contiguous=True), "small", writes=[t_small])
        S.dma("sp", C("dma_start", out=gq[:], in_=g_q.rearrange("(c p o) -> p c o", p=128, o=1), allow_slow_non_contiguous=True), "small", writes=[t_small])
        S.dma("sp", C("dma_start", out=gkv[:], in_=g_kv.rearrange("(c p o) -> p c o", p=128, o=1), allow_slow_non_contiguous=True), "small", writes=[t_small])
        S.dma("sp", C("dma_start", out=gsub[:], in_=g_sub.rearrange("(p o) -> p o", o=1)), "small", writes=[t_small])
        S.dma("sp", C("dma_start", out=cv[:], in_=cvec), "small", writes=[t_small])
        for i in range(4):
            S.dma("sp", C("dma_start", out=lamt[:, i, :], in_=lam_in[i].partition_broadcast(128)), "small", writes=[t_small])
        t_lam = S.tok()
        S.op("dve", C("tensor_tensor", out=lamt[:, 0, :], in0=lamt[:, 0, :], in1=lamt[:, 1, :], op=ALU.mult), reads=[t_small], writes=[t_lam])
        S.op("dve", C("tensor_tensor", out=lamt[:, 2, :], in0=lamt[:, 2, :], in1=lamt[:, 3, :], op=ALU.mult), reads=[t_lam], writes=[t_lam])
        S.op("dve", C("reduce_sum", out=lamw[:, 0:1], in_=lamt[:, 0, :], axis=AX.X), reads=[t_lam], writes=[t_lam])
        S.op("dve", C("reduce_sum", out=lamw[:, 1:2], in_=lamt[:, 2, :], axis=AX.X), reads=[t_lam], writes=[t_lam])
        S.op("act", C("activation", out=lamw[:, 2:4], in_=lamw[:, 0:2], func=AF.Exp), reads=[t_lam], writes=[t_lam])
        S.op("dve", C("tensor_tensor", out=lamw[:, 4:5], in0=lamw[:, 3:4], in1=lamw[:, 2:3], op=ALU.subtract), reads=[t_lam], writes=[t_lam])
        S.op("dve", C("tensor_scalar", out=lamw[:, 5:6], in0=lamw[:, 4:5], scalar1=-LAMBDA_INIT, scalar2=None, op0=ALU.add), reads=[t_lam], writes=[t_lam])
        S.op("dve", C("tensor_scalar", out=gsub[:], in0=gsub[:], scalar1=1.0 - LAMBDA_INIT, scalar2=None, op0=ALU.mult), reads=[t_small], writes=[t_small])
        nlam = lamw[:, 5:6]

        tw = {k: S.tok() for k in ("A", "uq", "ukv", "dv", "dqk", "mg", "o", "up", "dn")}

        thr = [S.tok(), S.tok()]
        cast_n = [0]

        def cast(slot, dst, src, key):
            t = thr[cast_n[0] % 2]
            cast_n[0] += 1
            S.dma("pool", C("dma_start", out=dst, in_=src), slot, writes=[tw[key], t])

        def cp(ap, p=128):
            return ap.rearrange("(c p) n -> p c n", p=p)

        def casts(group):
            if group == 0:
                cast("cA", wA_s[:, :, 0:640], cp(w_in[:, 0:640]), "A")
                cast("cA", wA_s[:, :, 640:832], cp(w_kpe), "A")
                cast("cuq", wuq_s[:, :, 0:768], cp(w_uq), "uq")
                cast("cuq", wuq_s[:, :, 768:1536], cp(w_uqs), "uq")
                cast("cukv", wukv_s, cp(w_ukv), "ukv")
            elif group == 1:
                cast("cdv", wdv_s, cp(w_in[:, O5_:O6_]), "dv")
                for h in range(8):
                    cast("cdqk", wdqk_s[h, :, :, 0:128], cp(w_in[:, O3_ + h * 128:O3_ + (h + 1) * 128]), "dqk")
                    cast("cdqk", wdqk_s[h, :, :, 128:256], cp(w_in[:, O4_ + h * 128:O4_ + (h + 1) * 128]), "dqk")
            elif group == 2:
                for m in range(8):
                    cast("cmg", wmg_s[m, :, 0:4, :], cp(w_omla[:, m * 128:(m + 1) * 128]), "mg")
                    cast("cmg", wmg_s[m, :, 4:12, :], cp(w_odiff[:, m * 128:(m + 1) * 128]), "mg")
                    cast("cmg", wmg_s[m, :, 12:20, :], cp(w_in[:, O6_ + m * 128:O6_ + (m + 1) * 128]), "mg")
                    cast("cmg", wmg_s[m, :, 20:28, :], cp(w_in[:, O6_ + 1024 + m * 128:O6_ + 1024 + (m + 1) * 128]), "mg")
                cast("co", wo_s, cp(w_o), "o")
            else:
                for fb in range(8):
                    cast("cup", wup_s[fb], cp(w_up[:, fb * 512:(fb + 1) * 512]), "up")
                for fb in range(8):
                    cast("cdn", wdn_s[fb], cp(w_down[fb * 512:(fb + 1) * 512, :]), "dn")

        casts(0)
        NRING = 4
        t_xTs = S.tok()
        t_ods = S.tok()
        ring_ctr = [0]
        LO_BASE = A.lo_ptr
        HI_TOP = A.hi_ptr

        def dbg_dump(name, ap, shape, dt=F32):
            if not dbg:
                return
            o = nc.dram_tensor("dbg_" + name, list(shape), dt, kind="ExternalOutput").ap()
            dbg_out[name] = o
            S.barrier()
            S.dma("sp", C("dma_start", out=o, in_=ap), "dbg_" + name)
            S.barrier()

        for b in range(nseq):
            if True:
                A.lo_ptr = LO_BASE
                A.hi_ptr = HI_TOP
                XT_OFF = A.lo_ptr
                xT = sbuf("lo", "xT", [128, 8, SEQ], BF16)
                posf = sbuf("lo", "posf", [128, SEQ], F32)
                pk = sbuf("lo", "pk", [128, 16, 1], F32)
                npk = sbuf("lo", "npk", [128, 16, 1], F32)
                OmT = sbuf("lo", "OmT", [128, 4, SEQ], BF16)
                t_pos = S.tok()
                if True:
                    QT = sbuf("hi", "QT", [128, 8, SEQ], BF16)
                    KT = sbuf("hi", "KT", [128, 8, SEQ], BF16)
                    Vm = sbuf("hi", "Vm", [128, NT, 512], BF16)
                    if True:
                        mkA = A.hi_ptr
                        pki = sbuf("hi", "pki", [128, 16, 1], I32)
                        S.dma("sp", C("dma_start", out=posf[:].bitcast(I32), in_=pos[b:b + 1, :].partition_broadcast(128)), "pos", writes=[t_pos])
                        S.dma("sp", C("dma_start", out=pki[:], in_=pos[b, :].rearrange("(t p o) -> p t o", p=128, o=1), allow_slow_non_contiguous=True), "pos", writes=[t_pos])
                        S.op("dve", C("tensor_copy", out=posf[:], in_=posf[:].bitcast(I32)), reads=[t_pos], writes=[t_pos])
                        S.op("dve", C("tensor_copy", out=pk[:], in_=pki[:]), reads=[t_pos], writes=[t_pos])
                        S.op("dve", C("tensor_scalar", out=npk[:], in0=pk[:], scalar1=-1.0, scalar2=None, op0=ALU.mult), reads=[t_pos], writes=[t_pos])
                        wA = sbuf("hi", "wA", [128, 8, 832], BF16)
                        wuq = sbuf("hi", "wuq", [128, 3, 1536], BF16)
                        wukv = sbuf("hi", "wukv", [128, 2, 1024], BF16)
                        t_wA = S.tok()
                        S.dma("sp", C("dma_start", out=wA[:], in_=wA_s), "wA", reads=[tw["A"]], writes=[t_wA])
                        S.dma("sp", C("dma_start", out=wuq[:], in_=wuq_s), "wA", reads=[tw["uq"]], writes=[t_wA])
                        S.dma("sp", C("dma_start", out=wukv[:], in_=wukv_s), "wA", reads=[tw["ukv"]], writes=[t_wA])
                        xin = [sbuf("hi", "xin%d" % i, [128, D], F32) for i in range(2)]
                        t_xin = [S.tok(), S.tok()]
                        cqf = sbuf("hi", "cqf", [128, 3, 512], F32)
                        sqf = sbuf("hi", "sqf", [128, 3, 512], F32)
                        t_cqf, t_sqf = S.tok(), S.tok()
                        xin += [cqf[:].rearrange("p a b -> p (a b)")[:, 0:D], sqf[:].rearrange("p a b -> p (a b)")[:, 0:D]]
                        t_xin += [t_cqf, t_sqf]
                        NXB = 4
                        t_xT = [S.tok() for _ in range(NCH)]
                        t_xT2 = [S.tok() for _ in range(NCH)]
                        for t in range(NT):
                            xi = xin[t % NXB]
                            S.dma("sp", C("dma_start", out=xi[:, :], in_=x[b, t * 128:(t + 1) * 128, :]),
                                  "xin%d" % (t % NXB), writes=[t_xin[t % NXB]])
                            for half in range(2):
                                bank = (2 * t + half) % 4
                                for c4 in range(4):
                                    c = half * 4 + c4
                                    S.op("pe", C("transpose",
                                        out=PS[bank][:, c4 * 128:(c4 + 1) * 128], in_=xi[:, c * 128:(c + 1) * 128], identity=ident[:]),
                                        reads=[t_xin[t % NXB], t_const], writes=[tPS[bank]])
                                dst = xT[:, half * 4:(half + 1) * 4, t * 128:(t + 1) * 128]
                                src = PS[bank][:].rearrange("p (c n) -> p c n", c=4)
                                if half == 0:
                                    S.op("act", C("copy", out=dst, in_=src), reads=[tPS[bank]], writes=[t_xT[t // 4]])
                                else:
                                    S.op("dve", C("tensor_copy", out=dst, in_=src), reads=[tPS[bank]], writes=[t_xT2[t // 4]])
                        rstd = sbuf("hi", "rstd", [128, 512], F32)
                        cqn = sbuf("hi", "cqn", [128, 3, 512], BF16)
                        ckvn = sbuf("hi", "ckvn", [128, 2, 512], BF16)
                        ang = sbuf("hi", "ang", [128, 512], F32)
                        rr = sbuf("hi", "rr", [128, 512], F32)
                        rc = ang
                        cosT = sbuf("hi", "cosT", [128, 512], F32)
                        sinS = sbuf("hi", "sinS", [128, 512], F32)
                        rt1 = sbuf("hi", "rt1", [128, 512], F32)
                        rt2 = sbuf("hi", "rt2", [128, 512], F32)
                        angn = rt2
                        kper = sbuf("hi", "kper", [128, 512], BF16)
                        t_rstd, t_cqn, t_ckvn = S.tok(), S.tok(), S.tok()
                        t_trig = S.tok()
                        t_rt = t_trig
                        t_kper = S.tok()
                        t_QK = S.tok()
                        t_zero = S.tok()
                        S.op("pool", C("memset", QT[64:128, :, :], 0.0), writes=[t_zero])
                        S.op("pool", C("memset", KT[64:128, :, :], 0.0), writes=[t_zero])
                        R = slice(64, 96)
                        bk = [0]

                        def nb():
                            bk[0] = (bk[0] + 1) % 8
                            return bk[0]

                        for ch in range(NCH):
                            cs = slice(ch * 512, (ch + 1) * 512)
                            S.op("dve", C("tensor_scalar", out=ang[R, :], in0=posf[R, cs], scalar1=cv[R, 0:1], scalar2=None, op0=ALU.mult),
                                 reads=[t_pos, t_small], writes=[t_trig])
                            S.op("dve", C("tensor_scalar", out=rt1[R, :].bitcast(I32), in0=ang[R, :], scalar1=1.0 / TWO_PI, scalar2=None, op0=ALU.mult),
                                 reads=[t_trig], writes=[t_trig])
                            S.op("dve", C("tensor_copy", out=angn[R, :], in_=rt1[R, :].bitcast(I32)), reads=[t_trig], writes=[t_trig])
                            S.op("dve", C("scalar_tensor_tensor", out=rr[R, :], in0=angn[R, :], scalar=-CW1, in1=ang[R, :], op0=ALU.mult, op1=ALU.add),
                                 reads=[t_trig], writes=[t_trig])
                            S.op("dve", C("scalar_tensor_tensor", out=rr[R, :], in0=angn[R, :], scalar=-CW2, in1=rr[R, :], op0=ALU.mult, op1=ALU.add),
                                 reads=[t_trig], writes=[t_trig])
                            S.op("dve", C("tensor_scalar", out=rc[R, :], in0=rr[R, :], scalar1=math.pi / 2, scalar2=None, op0=ALU.add),
                                 reads=[t_trig], writes=[t_trig])
                            S.op("dve", C("tensor_scalar", out=angn[R, :], in0=rc[R, :], scalar1=math.pi, scalar2=-TWO_PI, op0=ALU.is_gt, op1=ALU.mult),
                                 reads=[t_trig], writes=[t_trig])
                            S.op("dve", C("tensor_tensor", out=rc[R, :], in0=rc[R, :], in1=angn[R, :], op=ALU.add), reads=[t_trig], writes=[t_trig])
                            S.op("dve", C("tensor_scalar", out=rc[R, :], in0=rc[R, :], scalar1=-PI_SAFE, scalar2=PI_SAFE, op0=ALU.max, op1=ALU.min),
                                 reads=[t_trig], writes=[t_trig])
                            S.op("dve", C("tensor_scalar", out=rr[R, :], in0=rr[R, :], scalar1=-PI_SAFE, scalar2=PI_SAFE, op0=ALU.max, op1=ALU.min),
                                 reads=[t_trig], writes=[t_trig])
                            S.op("act", C("activation", out=cosT[R, :], in_=rc[R, :], func=AF.Sin), reads=[t_trig], writes=[t_trig])
                            S.op("act", C("activation", out=sinS[R, :], in_=rr[R, :], func=AF.Sin, scale=cv[R, 1:2]), reads=[t_trig, t_small], writes=[t_trig])

                            def proj_norm(col0, nmt, dim, gvec, dstn, t_dst):
                                for mt in range(nmt):
                                    bank = nb()
                                    for c in range(8):
                                        S.op("pe", C("matmul",
                                            PS[bank][:], lhsT=wA[:, c, col0 + mt * 128:col0 + (mt + 1) * 128], rhs=xT[:, c, cs],
                                            start=(c == 0), stop=(c == 7)), reads=[t_wA, t_xT[ch], t_xT2[ch]], writes=[tPS[bank]])
                                    S.op("dve", C("tensor_copy", out=cqf[:, mt, :], in_=PS[bank][:]), reads=[tPS[bank]], writes=[t_cqf])
                                    S.op("act", C("activation", out=sqf[:, mt, :], in_=PS[bank][:], func=AF.Square),
                                         reads=[tPS[bank]], writes=[t_sqf])
                                bank = nb()
                                for mt in range(nmt):
                                    S.op("pe", C("matmul", PS[bank][:], lhsT=ones_f[:], rhs=sqf[:, mt, :],
                                                                                    start=(mt == 0), stop=(mt == nmt - 1)),
                                         reads=[t_sqf, t_const], writes=[tPS[bank]])
                                S.op("act", C("activation", out=rstd[:], in_=PS[bank][:], func=AF.Ln, scale=1.0 / dim, bias=eps_rms[:, 0:1]),
                                     reads=[tPS[bank], t_const], writes=[t_rstd])
                                S.op("act", C("activation", out=rstd[:], in_=rstd[:], func=AF.Exp, scale=-0.5), reads=[t_rstd], writes=[t_rstd])
                                for mt in range(nmt):
                                    S.op("dve", C("scalar_tensor_tensor", out=dstn[:, mt, :], in0=cqf[:, mt, :], scalar=gvec[:, mt, :],
                                                                                        in1=rstd[:], op0=ALU.mult, op1=ALU.mult),
                                         reads=[t_cqf, t_rstd, t_small], writes=[t_dst])

                            proj_norm(0, 3, 384.0, gq, cqn, t_cqn)
                            proj_norm(384, 2, 256.0, gkv, ckvn, t_ckvn)

                            def rope(bq, bs, dst, t_d):
                                S.op("dve", C("tensor_tensor", out=rt1[R, :], in0=PS[bq][R, :], in1=cosT[R, :], op=ALU.mult),
                                     reads=[tPS[bq], t_trig], writes=[t_rt])
                                S.op("dve", C("tensor_tensor", out=rt2[R, :], in0=PS[bs][R, :], in1=sinS[R, :], op=ALU.mult),
                                     reads=[tPS[bs], t_trig], writes=[t_rt])
                                S.op("dve", C("tensor_tensor", out=dst, in0=rt1[R, :], in1=rt2[R, :], op=ALU.add),
                                     reads=[t_rt] + ([t_zero] if t_d is None else []), writes=([] if t_d is None else [t_d]))

                            bq, bs = nb(), nb()
                            for (bank, col0) in ((bq, 640), (bs, 736)):
                                for c in range(8):
                                    S.op("pe", C("matmul",
                                        PS[bank][0:96, :], lhsT=wA[:, c, col0:col0 + 96], rhs=xT[:, c, cs], start=(c == 0), stop=(c == 7)),
                                        reads=[t_wA, t_xT[ch], t_xT2[ch]], writes=[tPS[bank]])
                            rope(bq, bs, kper[R, :], t_kper)
                            S.op("act", C("copy", out=KT[R, :, cs], in_=kper[R, :].unsqueeze(1).broadcast_to([32, 8, 512])),
                                 reads=[t_kper, t_zero], writes=[])
                            for h in range(8):
                                bq, bs = nb(), nb()
                                for (bank, col0) in ((bq, h * 96), (bs, 768 + h * 96)):
                                    for c in range(3):
                                        S.op("pe", C("matmul",
                                            PS[bank][0:96, :], lhsT=wuq[:, c, col0:col0 + 96], rhs=cqn[:, c, :], start=(c == 0), stop=(c == 2)),
                                            reads=[t_wA, t_cqn], writes=[tPS[bank]])
                                S.op("act", C("copy", out=QT[0:64, h, cs], in_=PS[bq][0:64, :]), reads=[tPS[bq]], writes=[])
                                rope(bq, bs, QT[R, h, cs], None)
                                bank = nb()
                                for c in range(2):
                                    S.op("pe", C("matmul",
                                        PS[bank][0:64, :], lhsT=wukv[:, c, h * 64:(h + 1) * 64], rhs=ckvn[:, c, :], start=(c == 0), stop=(c == 1)),
                                        reads=[t_wA, t_ckvn], writes=[tPS[bank]])
                                S.op("act", C("copy", out=KT[0:64, h, cs], in_=PS[bank][0:64, :]), reads=[tPS[bank]], writes=[])
                            for tt in range(4):
                                bank = nb()
                                for c in range(2):
                                    S.op("pe", C("matmul",
                                        PS[bank][:], lhsT=ckvn[:, c, tt * 128:(tt + 1) * 128], rhs=wukv[:, c, 512:1024], start=(c == 0), stop=(c == 1)),
                                        reads=[t_wA, t_ckvn], writes=[tPS[bank]])
                                S.op("dve", C("tensor_copy", out=Vm[:, ch * 4 + tt, :], in_=PS[bank][:]), reads=[tPS[bank]], writes=[])
                    A.hi_ptr = mkA
                    S.barrier()
                    if stage == 1 and dbg:
                        dbg_dump("cqf", cqf[:], [128, 3, 512], F32)
                        dbg_dump("wuq", wuq[:], [128, 3, 1536], BF16)
                        dbg_dump("wukv", wukv[:], [128, 2, 1024], BF16)
                        dbg_dump("sqf", sqf[:], [128, 3, 512], F32)
                        dbg_dump("rstd", rstd[:], [128, 512], F32)
                        dbg_dump("cqn", cqn[:], [128, 3, 512], BF16)
                        dbg_dump("ckvn", ckvn[:], [128, 2, 512], BF16)
                        dbg_dump("cosT", cosT[64:96, :], [32, 512], F32)
                        dbg_dump("sinS", sinS[64:96, :], [32, 512], F32)
                    if stage == 1:
                        dbg_dump("QT", QT[0:96, :, :], [96, 8, SEQ], BF16)
                        dbg_dump("KT", KT[0:96, :, :], [96, 8, SEQ], BF16)
                        dbg_dump("Vm", Vm[:], [128, NT, 512], BF16)
                        dbg_dump("xT", xT[:], [128, 8, SEQ], BF16)
                    if stage >= 2:
                        if True:
                            if b == 0:
                                casts(1)
                            S.dma("sp", C("dma_start", out=xT_s, in_=xT[:]), "xTs", reads=[t_xTs], writes=[t_xTs])
                            attention(S, nc, "hi", PS, tPS, mode="mla", QT=QT, KT=KT, V=Vm, OT=OmT, ones_b=ones_b, tri=tri,
                                      t_const=t_const, sbuf=sbuf)
                        S.barrier()
                A.hi_ptr = HI_TOP
                if stage == 2:
                    dbg_dump("OmT", OmT[:], [128, 4, SEQ], BF16)
                if stage < 3:
                    continue
                if True:
                    dv = sbuf("hi", "dv", [128, NT, 1024], BF16)
                    dqz = sbuf("hi", "dqz", [128, 8, 8, 512], BF16)
                    dkT = sbuf("hi", "dkT", [128, 8, SEQ], BF16)
                    if True:
                        mkC = A.hi_ptr
                        if b == 0:
                            casts(2)
                        wdv = sbuf("hi", "wdv", [128, 8, 1024], BF16)
                        t_wdv = S.tok()
                        S.dma("sp", C("dma_start", out=wdv[:], in_=wdv_s), "wdv", reads=[tw["dv"]], writes=[t_wdv])
                        t_dv = S.tok()
                        t_dqz = S.tok()
                        S.op("pool", C("memset", dqz[0:64, :, :, 256:512], 0.0), writes=[])
                        S.op("pool", C("memset", dqz[64:128, :, :, 0:256], 0.0), writes=[])
                        k = 0
                        for t in range(NT):
                            for j in range(2):
                                bank = k % 8
                                k += 1
                                for c in range(8):
                                    S.op("pe", C("matmul",
                                        PS[bank][:], lhsT=xT[:, c, t * 128:(t + 1) * 128], rhs=wdv[:, c, j * 512:(j + 1) * 512],
                                        start=(c == 0), stop=(c == 7)), reads=[t_wdv], writes=[tPS[bank]])
                                dst = dv[:, t, j * 512:(j + 1) * 512]
                                if k % 2:
                                    S.op("act", C("copy", out=dst, in_=PS[bank][:]), reads=[tPS[bank]], writes=[])
                                else:
                                    S.op("dve", C("tensor_copy", out=dst, in_=PS[bank][:]), reads=[tPS[bank]], writes=[])
                        A.hi_ptr = mkC
                        S.barrier()
                        wq = [sbuf("hi", "wq%d" % i, [128, 8, 256], BF16) for i in range(2)]
                        t_wq = [S.tok(), S.tok()]
                        for h in range(2):
                            S.dma("sp", C("dma_start", out=wq[h][:], in_=wdqk_s[h]), "wq%d" % h, reads=[tw["dqk"]], writes=[t_wq[h]])
                        for h in range(8):
                            hb = h % 2
                            for ch in range(NCH):
                                cs = slice(ch * 512, (ch + 1) * 512)
                                for which in range(2):
                                    bank = k % 8
                                    k += 1
                                    for c in range(8):
                                        S.op("pe", C("matmul", PS[bank][:], lhsT=wq[hb][:, c, which * 128:(which + 1) * 128], rhs=xT[:, c, cs],
                                                     start=(c == 0), stop=(c == 7)), reads=[t_wq[hb]], writes=[tPS[bank]])
                                    if which == 0:
                                        top_src = PS[bank][0:64, :].rearrange("p (b n) -> p b n", b=2)
                                        bot_src = PS[bank][64:128, :].rearrange("p (b n) -> p b n", b=2)
                                        S.op("act", C("copy", out=dqz[0:64, h, 2 * ch:2 * ch + 2, 0:256], in_=top_src), reads=[tPS[bank]], writes=[])
                                        S.op("dve", C("tensor_copy", out=dqz[64:128, h, 2 * ch:2 * ch + 2, 256:512], in_=bot_src), reads=[tPS[bank]], writes=[])
                                    elif k % 2:
                                        S.op("act", C("copy", out=dkT[:, h, cs], in_=PS[bank][:]), reads=[tPS[bank]], writes=[])
                                    else:
                                        S.op("dve", C("tensor_copy", out=dkT[:, h, cs], in_=PS[bank][:]), reads=[tPS[bank]], writes=[])
                            if h + 2 < 8:
                                S.dma("sp", C("dma_start", out=wq[hb][:], in_=wdqk_s[h + 2]), "wq%d" % hb, reads=[tw["dqk"]], writes=[t_wq[hb]])
                    A.hi_ptr = mkC
                    S.barrier()
                    if b == 0:
                        casts(3)
                    save = (A.lo_ptr, A.hi_ptr)
                    A.lo_ptr, A.hi_ptr = XT_OFF, XT_OFF + 32768
                    xt_tmps = {"distc": sbuf("lo", "distc", [128, 16, 256], mybir.dt.int16),
                               "sbq": [sbuf("lo", "sbq%d" % i, [128, 512], F32) for i in range(6)],
                               "pt": [sbuf("lo", "pt%d" % i, [128, 512], BF16) for i in range(8)],
                               "e_r": [sbuf("lo", "e_r0", [128, 512], F32)],
                               "ost": [sbuf("lo", "ost%d" % i, [128, 256], BF16) for i in range(2)]}
                    A.lo_ptr, A.hi_ptr = save
                    xt_tmps["e_r"].append(sbuf("hi", "e_r1", [128, 512], F32))
                    xt_tmps["e_t"] = [sbuf("hi", "e_t%d" % i, [128, 512], F32) for i in range(2)]
                    xt_tmps["e_od"] = [sbuf("hi", "e_od%d" % i, [128, 256], F32) for i in range(2)]
                    xt_tmps["e_sq"] = [sbuf("hi", "e_sq%d" % i, [128, 256], F32) for i in range(2)]
                    xt_tmps["e_rs"] = [sbuf("hi", "e_rs%d" % i, [128, 256], F32) for i in range(2)]
                    attention(S, nc, "hi", PS, tPS, mode="diff", dqz=dqz, dkT=dkT, V=dv, OT=OdT_s, ones_b=ones_b, ones_f=ones_f, tri=tri,
                              t_const=t_const, sbuf=sbuf, posf=posf, npk=npk, nlam=nlam, gsub=gsub,
                              slopes=slopes, eps_rms=eps_rms, tmps=xt_tmps, t_ods=t_ods)
                    S.barrier()
                A.hi_ptr = HI_TOP
                if stage == 3 and dbg:
                    dbg_dump("dqz", dqz[:], [128, 8, 8, 512], BF16)
                    dbg_dump("dkT", dkT[:], [128, 8, SEQ], BF16)
                    dbg_dump("distc", xt_tmps["distc"][:], [128, 16, 256], mybir.dt.int16)
                OdT = sbuf("lo", "OdT", [128, 8, SEQ], BF16)
                t_odr = S.tok()
                S.dma("sp", C("dma_start", out=OdT[:], in_=OdT_s), "odr", reads=[t_ods], writes=[t_odr])
                if stage == 3:
                    dbg_dump("OdT", OdT[:], [128, 8, SEQ], BF16)
                if stage < 4:
                    continue
                x1 = None
                if True:
                    mixT = sbuf("hi", "mixT", [128, 8, SEQ], BF16)
                    if True:
                        mkD = A.hi_ptr
                        wm = [sbuf("hi", "wm%d" % i, [128, 28, 128], BF16) for i in range(2)]
                        t_wm = [S.tok(), S.tok()]
                        g0 = sbuf("hi", "g0", [128, 512], F32)
                        g1 = sbuf("hi", "g1", [128, 512], F32)
                        u0 = sbuf("hi", "u0", [128, 512], F32)
                        u1 = sbuf("hi", "u1", [128, 512], F32)
                        t_g0, t_g1, t_u0, t_u1, t_mix = S.tok(), S.tok(), S.tok(), S.tok(), S.tok()
                        S.dma("sp", C("dma_start", out=wm[0][:], in_=wmg_s[0]), "wm0", reads=[tw["mg"]], writes=[t_wm[0]])
                        t_xTr = S.tok()
                        S.dma("sp", C("dma_start", out=xT[:], in_=xT_s), "xTr", reads=[t_xTs], writes=[t_xTr])
                        k = 0
                        for m in range(8):
                            w = wm[m % 2]
                            if m + 1 < 8:
                                S.dma("sp", C("dma_start", out=wm[(m + 1) % 2][:], in_=wmg_s[m + 1]), "wm%d" % ((m + 1) % 2),
                                      reads=[tw["mg"]], writes=[t_wm[(m + 1) % 2]])
                            for ch in range(NCH):
                                cs = slice(ch * 512, (ch + 1) * 512)
                                b_ym, b_yd, b_g0, b_g1 = [(k * 4 + i) % 8 for i in range(4)]
                                k += 1
                                for c in range(4):
                                    S.op("pe", C("matmul", PS[b_ym][:], lhsT=w[:, c, :], rhs=OmT[:, c, cs],
                                                                                         start=(c == 0), stop=(c == 3)),
                                         reads=[t_wm[m % 2]], writes=[tPS[b_ym]])
                                for c in range(8):
                                    S.op("pe", C("matmul", PS[b_yd][:], lhsT=w[:, 4 + c, :], rhs=OdT[:, c, cs],
                                                                                         start=(c == 0), stop=(c == 7)),
                                         reads=[t_wm[m % 2], t_odr], writes=[tPS[b_yd]])
                                for c in range(8):
                                    S.op("pe", C("matmul", PS[b_g0][:], lhsT=w[:, 12 + c, :], rhs=xT[:, c, cs],
                                                                                         start=(c == 0), stop=(c == 7)),
                                         reads=[t_wm[m % 2], t_xTr], writes=[tPS[b_g0]])
                                for c in range(8):
                                    S.op("pe", C("matmul", PS[b_g1][:], lhsT=w[:, 20 + c, :], rhs=xT[:, c, cs],
                                                                                         start=(c == 0), stop=(c == 7)),
                                         reads=[t_wm[m % 2], t_xTr], writes=[tPS[b_g1]])
                                S.op("act", C("activation", out=g0[:], in_=PS[b_g0][:], func=AF.Sigmoid, bias=bg[:, m, :]),
                                     reads=[tPS[b_g0], t_small], writes=[t_g0])
                                S.op("act", C("activation", out=g1[:], in_=PS[b_g1][:], func=AF.Sigmoid, bias=bg[:, 8 + m, :]),
                                     reads=[tPS[b_g1], t_small], writes=[t_g1])
                                S.op("dve", C("tensor_tensor", out=u0[:], in0=g0[:], in1=PS[b_ym][:], op=ALU.mult),
                                     reads=[t_g0, tPS[b_ym]], writes=[t_u0])
                                S.op("dve", C("tensor_tensor", out=u1[:], in0=g1[:], in1=PS[b_yd][:], op=ALU.mult),
                                     reads=[t_g1, tPS[b_yd]], writes=[t_u1])
                                S.op("pool", C("tensor_tensor", out=mixT[:, m, cs], in0=u0[:], in1=u1[:], op=ALU.add),
                                     reads=[t_u0, t_u1], writes=[t_mix])
                    A.hi_ptr = mkD
                    A.lo_ptr = LO_BASE
                    S.barrier()
                    if stage == 4:
                        dbg_dump("mixT", mixT[:], [128, 8, SEQ], BF16)
                        continue
                    x1 = sbuf("lo", "x1", [128, NT, D], F32)
                    lnp = sbuf("lo", "lnp", [128, 4, D], F32)
                    for i in range(4):
                        S.dma("sp", C("dma_start", out=lnp[:, i, :], in_=ln_in[i].partition_broadcast(128)), "lnp", writes=[t_small])
                    if True:
                        wo = sbuf("hi", "wo", [128, 8, 1024], BF16)
                        t_wo = S.tok()
                        S.dma("sp", C("dma_start", out=wo[:], in_=wo_s), "wo", reads=[tw["o"]], writes=[t_wo])
                        xin = [sbuf("hi", "xin2_%d" % i, [128, D], F32) for i in range(4)]
                        t_xin = [S.tok() for _ in range(4)]
                        t_x1 = [S.tok() for _ in range(NT)]
                        lnw = ln_work(S, nc, "hi", sbuf)
                        prev_st2 = None
                        for t in range(NT):
                            xi = xin[t % 4]
                            S.dma("sp", C("dma_start", out=xi[:], in_=x[b, t * 128:(t + 1) * 128, :]),
                                  "xin2_%d" % (t % 4), writes=[t_xin[t % 4]])
                            banks = [(2 * t) % 8, (2 * t + 1) % 8]
                            for j in range(2):
                                for c in range(8):
                                    S.op("pe", C("matmul",
                                        PS[banks[j]][:], lhsT=mixT[:, c, t * 128:(t + 1) * 128], rhs=wo[:, c, j * 512:(j + 1) * 512],
                                        start=(c == 0), stop=(c == 7)), reads=[t_wo], writes=[tPS[banks[j]]])
                            st2 = resid_ln(S, lnw, xi, t_xin[t % 4], [PS[banks[0]], PS[banks[1]]], [tPS[banks[0]], tPS[banks[1]]],
                                           lnp, 0, x1[:, t, :], t_x1[t], t_small, eps_ln)
                            if prev_st2 is not None:
                                prev_st2()
                            prev_st2 = st2
                        prev_st2()
                    A.hi_ptr = HI_TOP
                    S.barrier()
                    if stage == 5:
                        dbg_dump("x1", x1[:], [128, NT, D], F32)
                        continue
                    if True:
                        ring = [sbuf("hi", "ring%d" % i, [128, 4096], BF16) for i in range(NRING)]
                        t_ring = [S.tok() for _ in range(NRING)]
                        x1T = sbuf("hi", "x1T", [128, 8, 512], BF16)
                        hT = sbuf("hi", "hT", [128, 32, 512], BF16)
                        rl = [sbuf("hi", "rl%d" % i, [128, 512], F32) for i in range(4)]
                        ost = [sbuf("hi", "ost%d" % i, [128, D], F32) for i in range(2)]
                        t_x1T, t_hT, t_hT2 = S.tok(), S.tok(), S.tok()
                        t_rl = [S.tok() for _ in range(4)]
                        t_ost = [S.tok(), S.tok()]
                        lnw = ln_work(S, nc, "hi", sbuf)
                        t_x1c = S.tok()

                        def ring_load(src, key):
                            i = ring_ctr[0] % NRING
                            ring_ctr[0] += 1
                            S.dma("sp", C("dma_start", out=ring[i][:], in_=src), "ring%d" % i, reads=[tw[key]], writes=[t_ring[i]])
                            return i

                        oc = 0
                        PD = 3
                        blocks = []
                        for _ch in range(NCH):
                            blocks += [(wup_s[fb].rearrange("p c n -> p (c n)"), "up") for fb in range(8)]
                            blocks += [(wdn_s[fb].rearrange("p c n -> p (c n)"), "dn") for fb in range(8)]
                        loaded = []
                        nxt = [0]

                        def next_block():
                            while nxt[0] < len(blocks) and len(loaded) < PD:
                                loaded.append(ring_load(*blocks[nxt[0]]))
                                nxt[0] += 1
                            ri = loaded.pop(0)
                            while nxt[0] < len(blocks) and len(loaded) < PD:
                                loaded.append(ring_load(*blocks[nxt[0]]))
                                nxt[0] += 1
                            return ri

                        for ch in range(NCH):
                            for tt in range(4):
                                t = ch * 4 + tt
                                for half in range(2):
                                    bank = (2 * tt + half) % 8
                                    for c4 in range(4):
                                        c = half * 4 + c4
                                        S.op("pe", C("transpose",
                                            out=PS[bank][:, c4 * 128:(c4 + 1) * 128], in_=x1[:, t, c * 128:(c + 1) * 128], identity=ident[:]),
                                            reads=[t_x1c, t_const], writes=[tPS[bank]])
                                    dst = x1T[:, half * 4:(half + 1) * 4, tt * 128:(tt + 1) * 128]
                                    src = PS[bank][:].rearrange("p (c n) -> p c n", c=4)
                                    if half == 0:
                                        S.op("act", C("copy", out=dst, in_=src), reads=[tPS[bank]], writes=[t_x1T])
                                    else:
                                        S.op("dve", C("tensor_copy", out=dst, in_=src), reads=[tPS[bank]], writes=[t_x1T])
                            k = 0
                            for fb in range(8):
                                ri = next_block()
                                wv = ring[ri][:].rearrange("p (c n) -> p c n", c=8)
                                for f4 in range(4):
                                    f = fb * 4 + f4
                                    bank = k % 8
                                    k += 1
                                    for c in range(8):
                                        S.op("pe", C("matmul",
                                            PS[bank][:], lhsT=wv[:, c, f4 * 128:(f4 + 1) * 128], rhs=x1T[:, c, :], start=(c == 0), stop=(c == 7)),
                                            reads=[t_ring[ri], t_x1T], writes=[tPS[bank]])
                                    r = rl[f % 4]
                                    S.op("act", C("activation", out=r[:], in_=PS[bank][:], func=AF.Relu),
                                         reads=[tPS[bank]], writes=[t_rl[f % 4]])
                                    if f % 4 != 3:
                                        S.op("dve", C("tensor_tensor", out=hT[:, f, :], in0=r[:], in1=r[:], op=ALU.mult),
                                             reads=[t_rl[f % 4]], writes=[t_hT])
                                    else:
                                        S.op("pool", C("tensor_tensor", out=hT[:, f, :], in0=r[:], in1=r[:], op=ALU.mult),
                                             reads=[t_rl[f % 4]], writes=[t_hT2])
                            for fb in range(8):
                                ri = next_block()
                                wv = ring[ri][:].rearrange("p (c n) -> p c n", c=4)
                                for tt in range(4):
                                    for j in range(2):
                                        bank = tt * 2 + j
                                        for fi in range(4):
                                            f = fb * 4 + fi
                                            S.op("pe", C("matmul",
                                                PS[bank][:], lhsT=hT[:, f, tt * 128:(tt + 1) * 128], rhs=wv[:, fi, j * 512:(j + 1) * 512],
                                                start=(f == 0), stop=(f == 31)), reads=[t_ring[ri], t_hT, t_hT2], writes=[tPS[bank]])
                            prev_fin = None
                            for tt in range(4):
                                t = ch * 4 + tt
                                o = ost[oc % 2]
                                t_o = t_ost[oc % 2]
                                slot = "ost%d" % (oc % 2)
                                oc += 1
                                st2 = resid_ln(S, lnw, x1[:, t, :], t_x1c, [PS[tt * 2], PS[tt * 2 + 1]], [tPS[tt * 2], tPS[tt * 2 + 1]],
                                               lnp, 2, o[:], t_o, t_small, eps_ln, src_is_ap=True)

                                def fin(st2=st2, o=o, t=t, t_o=t_o, slot=slot):
                                    st2()
                                    S.dma("sp", C("dma_start", out=out[b, t * 128:(t + 1) * 128, :], in_=o[:]), slot, reads=[t_o])
                                if prev_fin is not None:
                                    prev_fin()
                                prev_fin = fin
                            prev_fin()
                    S.barrier()
        S.emit(top)
    return nc, dbg_out


def ln_work(S, nc, st, sbuf):
    w = {"i": 0, "sets": []}
    for i in range(2):
        d = {
            "r": sbuf(st, "ln_r%d" % i, [128, D], F32),
            "st": sbuf(st, "ln_st%d" % i, [128, 2, 6], F32),
            "mv": sbuf(st, "ln_mv%d" % i, [128, 2], F32),
            "rs": sbuf(st, "ln_rs%d" % i, [128, 1], F32),
            "nm": sbuf(st, "ln_nm%d" % i, [128, 1], F32),
            "t": sbuf(st, "ln_t%d" % i, [128, D], F32),
            "tok": S.tok(), "tok2": S.tok(),
        }
        w["sets"].append(d)
    return w


def resid_ln(S, lnw, xsrc, t_x, banks, t_banks, lnp, gi, dst, t_dst, t_small, eps_ln, src_is_ap=False):
    d = lnw["sets"][lnw["i"] % 2]
    lnw["i"] += 1
    r, stt, mv, rs, tmp, tk, tk2 = d["r"], d["st"], d["mv"], d["rs"], d["t"], d["tok"], d["tok2"]
    for j in range(2):
        xs = xsrc[:, j * 512:(j + 1) * 512]
        S.op("dve", C("scalar_tensor_tensor", out=r[:, j * 512:(j + 1) * 512], in0=xs, scalar=ALPHA,
                                                                in1=banks[j][:], op0=ALU.mult, op1=ALU.add),
             reads=[t_x, t_banks[j]], writes=[tk])
        S.op("dve", C("bn_stats", out=stt[:, j, :], in_=r[:, j * 512:(j + 1) * 512]), reads=[tk], writes=[tk])
    S.op("dve", C("bn_aggr", out=mv[:], in_=stt[:].rearrange("p a b -> p (a b)")), reads=[tk], writes=[tk])
    S.op("act", C("activation", out=rs[:], in_=mv[:, 1:2], func=AF.Sqrt, bias=eps_ln[:, 0:1]), reads=[tk], writes=[tk])
    S.op("dve", C("reciprocal", out=rs[:], in_=rs[:]), reads=[tk], writes=[tk])
    S.op("dve", C("scalar_tensor_tensor", out=d["nm"][:], in0=mv[:, 0:1], scalar=-1.0, in1=rs[:], op0=ALU.mult, op1=ALU.mult), reads=[tk], writes=[tk])
    S.op("act", C("activation", out=tmp[:], in_=r[:], func=AF.Identity, scale=rs[:, 0:1], bias=d["nm"][:, 0:1]), reads=[tk], writes=[tk2])
    def stage2():
        S.op("dve", C("tensor_tensor", out=tmp[:], in0=tmp[:], in1=lnp[:, gi, :], op=ALU.mult), reads=[tk2, t_small], writes=[tk2])
        S.op("pool", C("tensor_tensor", out=dst, in0=tmp[:], in1=lnp[:, gi + 1, :], op=ALU.add), reads=[tk2, t_small], writes=[t_dst])
    return stage2


def attention(S, nc, st, PS, tPS, mode, sbuf, ones_b, tri, t_const, V, OT, **kw):
    NPT = 4 if mode == "mla" else 8
    LAG = 3 if mode == "mla" else 5
    NSBQ = 6
    sctr = [0]
    ictr = [0]

    def sbank():
        sctr[0] += 1
        return sctr[0] % 4

    if mode == "mla":
        QT, KT = kw["QT"], kw["KT"]
        pt = [sbuf(st, "pt%d" % i, [128, 512], BF16) for i in range(NPT)]
        rlt = [sbuf(st, "rlt%d" % i, [128, 512], F32) for i in range(2)]
        t_rlt = [S.tok(), S.tok()]
    else:
        dqz, dkT = kw["dqz"], kw["dkT"]
        posf, npk, nlam, gsub, slopes = kw["posf"], kw["npk"], kw["nlam"], kw["gsub"], kw["slopes"]
        ones_f, eps_rms, t_ods = kw["ones_f"], kw["eps_rms"], kw["t_ods"]
        tm = kw["tmps"]
        pt, sbq, distc = tm["pt"], tm["sbq"], tm["distc"]
        e_r2, e_t2, e_od2, e_sq2, e_rs2, ost = tm["e_r"], tm["e_t"], tm["e_od"], tm["e_sq"], tm["e_rs"], tm["ost"]
        t_e2 = [S.tok(), S.tok()]
        t_er2 = [S.tok(), S.tok()]
        t_et2 = [S.tok(), S.tok()]
        t_sbq = [S.tok() for _ in range(NSBQ)]
        t_dist = [S.tok() for _ in range(16)]
        t_ost = [S.tok(), S.tok()]
        octr = [0]
    t_pt = [S.tok() for _ in range(NPT)]
    pending = []

    def issue_S(it):
        i = ictr[0]
        ictr[0] += 1
        it["pi"] = i % NPT
        bk = sbank()
        n0 = it["n0"]
        S.op("pe", C("matmul", PS[bk][:, n0:512], lhsT=it["lhsT"], rhs=it["rhs"], start=True, stop=True),
             reads=it["s_reads"], writes=[tPS[bk]])
        p = pt[it["pi"]]
        tp = t_pt[it["pi"]]
        if mode == "mla":
            S.op("act", C("activation", out=p[:, n0:512], in_=PS[bk][:, n0:512], func=AF.Exp, scale=MLA_SCALE),
                 reads=[tPS[bk]], writes=[tp])
        else:
            q = i % NSBQ
            sb_, tsb = sbq[q], t_sbq[q]
            kt = it["kt"]
            v3 = lambda ap: ap.rearrange("p (c n) -> p c n", c=2)
            dbc = distc[:, kt, :].unsqueeze(1).broadcast_to([128, 2, 256])
            S.op("dve", C("scalar_tensor_tensor", out=v3(sb_[:]), in0=dbc, scalar=it["bscale"], in1=v3(PS[bk][:]),
                          op0=ALU.mult, op1=ALU.add), reads=[t_dist[kt], tPS[bk]], writes=[tsb])
            S.op("act", C("activation", out=p[:], in_=sb_[:], func=AF.Exp, scale=DIFF_SCALE), reads=[tsb], writes=[tp])
            if it["diag"]:
                p3 = v3(p[:])
                tri_bc = tri[:].unsqueeze(1).broadcast_to([128, 2, 128])
                d0 = it["d0"]
                if d0:
                    S.op("pool", C("memset", p3[:, :, 0:128], 0.0), reads=[tp], writes=[tp])
                S.op("pool", C("tensor_tensor", out=p3[:, :, d0:d0 + 128], in0=p3[:, :, d0:d0 + 128], in1=tri_bc, op=ALU.mult),
                     reads=[tp, t_const], writes=[tp])
        if mode == "mla" and it["diag"]:
            S.op("pool", C("tensor_tensor", out=p[:, n0:n0 + 128], in0=p[:, n0:n0 + 128], in1=tri[:], op=ALU.mult),
                 reads=[tp, t_const], writes=[tp])
        if it.get("after_S") is not None:
            it["after_S"]()

    deferred = []
    POST_LAG = 3
    STAGE_LAG = 2

    def defer(cnt, fn, tag):
        deferred.append({"cnt": cnt, "fn": fn, "tag": tag})

    def run_deferred():
        for d in deferred:
            d["cnt"] -= 1
        while deferred and deferred[0]["cnt"] <= 0:
            deferred.pop(0)["fn"]()

    def force_tag(tag):
        while any(d["tag"] == tag for d in deferred):
            deferred.pop(0)["fn"]()

    def issue_PV(it):
        p = pt[it["pi"]]
        tp = t_pt[it["pi"]]
        n0 = it["n0"]
        aO, aL = it["accO"], it["accL"]
        r0, r1 = it["rows"]
        if it["first"]:
            force_tag(("acc", aO))
        S.op("pe", C("matmul", PS[aO][r0:r1, n0:512], lhsT=it["vT"], rhs=p[:, n0:512], start=it["first"], stop=it["last"]),
             reads=[tp], writes=[tPS[aO]])
        S.op("pe", C("matmul", PS[aL][r0:r1, n0:512], lhsT=ones_b[:, 0:r1 - r0], rhs=p[:, n0:512], start=it["first"], stop=it["last"]),
             reads=[tp, t_const], writes=[tPS[aL]])
        if it["post"] is not None:
            defer(POST_LAG, it["post"], ("acc", aO))

    def push(it):
        issue_S(it)
        pending.append(it)
        if len(pending) > LAG:
            issue_PV(pending.pop(0))
        run_deferred()

    def flush():
        while pending:
            issue_PV(pending.pop(0))
        while deferred:
            deferred.pop(0)["fn"]()

    grp = 0
    if mode == "mla":
        for h in range(8):
            ho = (h % 2) * 64
            for qc in range(NCH):
                aO, aL = (4, 5) if grp % 2 == 0 else (6, 7)
                rl_, trl = rlt[grp % 2], t_rlt[grp % 2]
                grp += 1
                nk = 4 * qc + 4

                def post(h=h, qc=qc, aO=aO, aL=aL, ho=ho, rl_=rl_, trl=trl):
                    force_tag(("acc", aO))
                    S.op("act", C("activation", out=rl_[ho:ho + 64, :], in_=PS[aL][ho:ho + 64, :], func=AF.Ln), reads=[tPS[aL]], writes=[trl])
                    S.op("act", C("activation", out=rl_[ho:ho + 64, :], in_=rl_[ho:ho + 64, :], func=AF.Exp, scale=-1.0), reads=[trl], writes=[trl])

                    def post_b():
                        S.op("dve", C("tensor_tensor", out=OT[ho:ho + 64, h // 2, qc * 512:(qc + 1) * 512], in0=PS[aO][ho:ho + 64, :],
                                      in1=rl_[ho:ho + 64, :], op=ALU.mult), reads=[tPS[aO], trl], writes=[])
                    defer(STAGE_LAG, post_b, ("acc", aO))
                for kt in range(nk):
                    j = kt - 4 * qc
                    n0 = max(0, j) * 128
                    push({"n0": n0, "diag": j >= 0,
                          "lhsT": KT[:, h, kt * 128:(kt + 1) * 128], "rhs": QT[:, h, qc * 512 + n0:(qc + 1) * 512],
                          "s_reads": [], "vT": V[:, kt, h * 64:(h + 1) * 64],
                          "accO": aO, "accL": aL, "rows": (ho, ho + 64), "first": kt == 0, "last": kt == nk - 1,
                          "post": post if kt == nk - 1 else None})
        flush()
        return

    def abs_tile(qb, kt):
        S.op("act", C("activation", out=distc[:, kt, :], in_=posf[:, qb * 256:(qb + 1) * 256], func=AF.Abs, bias=npk[:, kt, :]),
             reads=[], writes=[t_dist[kt]])

    for kt in range(2):
        abs_tile(0, kt)
    NQB = SEQ // 256
    for qb in range(NQB):
        nk = 2 * qb + 2
        for h in range(8):
            bscale = -slopes[h] / DIFF_SCALE
            aO, aL = (4, 5) if grp % 2 == 0 else (6, 7)
            grp += 1

            def post(h=h, qb=qb, aO=aO, aL=aL, gi=grp % 2):
                e_r, e_t = e_r2[gi], e_t2[gi]
                e_od, e_sq, e_rs, t_e, t_er, t_et = e_od2[gi], e_sq2[gi], e_rs2[gi], t_e2[gi], t_er2[gi], t_et2[gi]
                force_tag(("acc", aO))
                force_tag(("tmp", gi))
                S.op("act", C("activation", out=e_r[:], in_=PS[aL][:], func=AF.Ln), reads=[tPS[aL]], writes=[t_er])
                S.op("act", C("activation", out=e_r[:], in_=e_r[:], func=AF.Exp, scale=-1.0), reads=[t_er], writes=[t_er])

                def post_b():
                    S.op("dve", C("tensor_tensor", out=e_t[:], in0=PS[aO][:], in1=e_r[:], op=ALU.mult), reads=[tPS[aO], t_er], writes=[t_et])
                    S.op("dve", C("scalar_tensor_tensor", out=e_od[:], in0=e_t[:, 256:512], scalar=nlam, in1=e_t[:, 0:256], op0=ALU.mult, op1=ALU.add),
                         reads=[t_et], writes=[t_e])
                    S.op("pool", C("tensor_tensor", out=e_sq[:], in0=e_od[:], in1=e_od[:], op=ALU.mult), reads=[t_e], writes=[t_e])
                    defer(STAGE_LAG + 1, tail_a, ("tmp", gi))

                def tail_a():
                    bk = sbank()
                    S.op("pe", C("matmul", PS[bk][:, 0:256], lhsT=ones_f[:], rhs=e_sq[:], start=True, stop=True), reads=[t_e, t_const], writes=[tPS[bk]])
                    defer(STAGE_LAG, lambda: tail_b(bk), ("tmp", gi))

                def tail_b(bk):
                    S.op("act", C("activation", out=e_rs[:], in_=PS[bk][:, 0:256], func=AF.Ln, scale=1.0 / 128.0, bias=eps_rms[:, 0:1]),
                         reads=[tPS[bk], t_const], writes=[t_e])
                    S.op("act", C("activation", out=e_rs[:], in_=e_rs[:], func=AF.Exp, scale=-0.5), reads=[t_e], writes=[t_e])
                    defer(STAGE_LAG, tail_c, ("tmp", gi))

                def tail_c():
                    oi = octr[0] % 2
                    octr[0] += 1
                    S.op("dve", C("scalar_tensor_tensor", out=ost[oi][:], in0=e_od[:], scalar=gsub[:, 0:1], in1=e_rs[:],
                                  op0=ALU.mult, op1=ALU.mult), reads=[t_e], writes=[t_ost[oi]])
                    S.dma("sp", C("dma_start", out=OT[:, h, qb * 256:(qb + 1) * 256], in_=ost[oi][:]), "ods%d" % oi, reads=[t_ost[oi]], writes=[t_ods])

                defer(STAGE_LAG, post_b, ("acc", aO))

            for kt in range(nk):
                j = kt - 2 * qb
                after = None
                if h == 7 and qb + 1 < NQB:
                    after = (lambda qb=qb, kt=kt: abs_tile(qb + 1, kt))
                push({"n0": 0, "diag": j >= 0, "d0": max(0, j) * 128, "kt": kt, "bscale": bscale, "after_S": after,
                      "lhsT": dkT[:, h, kt * 128:(kt + 1) * 128], "rhs": dqz[:, h, qb, :],
                      "s_reads": [], "vT": V[:, kt, h * 128:(h + 1) * 128],
                      "accO": aO, "accL": aL, "rows": (0, 128), "first": kt == 0, "last": kt == nk - 1,
                      "post": post if kt == nk - 1 else None})
        if qb + 1 < NQB:
            for kt in range(nk, nk + 2):
                abs_tile(qb + 1, kt)
    flush()


def _host_inputs(inputs):
    f = lambda a: np.ascontiguousarray(np.asarray(a, dtype=np.float32))
    w_in = f(inputs["w_in"][0])
    kpe = w_in[:, O2_:O3_]
    kpe_sw = np.concatenate([kpe[:, 16:32], kpe[:, 0:16]], axis=1)
    w_kpe = np.concatenate([kpe, kpe, kpe, kpe_sw, kpe_sw, kpe_sw], axis=1)
    w_uq = f(inputs["w_uq"][0])
    w_uqs = w_uq.copy()
    for h in range(8):
        b0 = h * 96 + 64
        w_uqs[:, b0:b0 + 16] = w_uq[:, b0 + 16:b0 + 32]
        w_uqs[:, b0 + 16:b0 + 32] = w_uq[:, b0:b0 + 16]
    w_ukv = f(inputs["w_ukv"][0]).reshape(256, 8, 128)
    w_ukv_r = np.concatenate([w_ukv[:, :, 0:64].reshape(256, 512), w_ukv[:, :, 64:128].reshape(256, 512)], axis=1)
    inv_freq = (1.0 / (10000.0 ** (np.arange(0, 32, 2, dtype=np.float32) / 32.0))).astype(np.float32)
    cvec = np.zeros((128, 2), np.float32)
    cvec[64:80, 0] = inv_freq
    cvec[80:96, 0] = inv_freq
    cvec[64:80, 1] = -1.0
    cvec[80:96, 1] = 1.0
    shared = {
        "w_in": w_in, "w_kpe": np.ascontiguousarray(w_kpe), "w_uq": w_uq, "w_uqs": np.ascontiguousarray(w_uqs),
        "w_ukv": np.ascontiguousarray(w_ukv_r), "w_omla": f(inputs["w_o_mla"][0]), "w_odiff": f(inputs["w_o_diff"][0]),
        "w_o": f(inputs["w_o"][0]), "w_up": f(inputs["w_up"][0]), "w_down": f(inputs["w_down"][0]),
        "b_gate": f(inputs["b_gate"][0]), "g_q": f(inputs["mla_q_norm"][0]), "g_kv": f(inputs["mla_kv_norm"][0]),
        "g_sub": f(inputs["diff_subln"][0]),
        "lam0": f(inputs["diff_lambda_q1"]), "lam1": f(inputs["diff_lambda_k1"]),
        "lam2": f(inputs["diff_lambda_q2"]), "lam3": f(inputs["diff_lambda_k2"]),
        "ln0": f(inputs["ln1_g"]), "ln1": f(inputs["ln1_b"]), "ln2": f(inputs["ln2_g"]), "ln3": f(inputs["ln2_b"]),
        "cvec": cvec,
    }
    return shared


_NC_CACHE = {}


def kernel(**inputs):
    x = np.ascontiguousarray(np.asarray(inputs["x"], dtype=np.float32))
    positions = np.ascontiguousarray(np.asarray(inputs["positions"], dtype=np.int32))
    nseq = x.shape[0] // N_CORES
    shared = _host_inputs(inputs)
    if "nc" not in _NC_CACHE:
        _NC_CACHE["nc"] = build(nseq)[0]
    nc = _NC_CACHE["nc"]
    in_maps = []
    for i in range(N_CORES):
        m = dict(shared)
        m["x"] = x[i * nseq:(i + 1) * nseq]
        m["pos"] = positions[i * nseq:(i + 1) * nseq]
        in_maps.append(m)
    res = run_bass_kernel_spmd(nc, in_maps, core_ids=list(range(N_CORES)))
    return np.concatenate([r["out"] for r in res.results], axis=0)
```

```python
import math
import numpy as np
from contextlib import ExitStack
import concourse.bass as bass
import concourse.mybir as mybir
from concourse.bass_utils import run_bass_kernel_spmd

F32 = mybir.dt.float32
BF16 = mybir.dt.bfloat16
I32 = mybir.dt.int32
ALU = mybir.AluOpType
AF = mybir.ActivationFunctionType
AX = mybir.AxisListType

N_CORES = 8
SEQ = 2048
D = 1024
NT = 16
NCH = 4
O1_, O2_, O3_, O4_, O5_, O6_ = 384, 640, 672, 1696, 2720, 3744
D_IN = 5792
ALPHA = 2.0 ** 0.25
LAMBDA_INIT = 0.2
MLA_SCALE = 96.0 ** -0.5
DIFF_SCALE = 0.125
LN_EPS = 1e-5
RMS_EPS = 1e-6
TWO_PI = 2.0 * math.pi
CW1 = 6.28125
CW2 = TWO_PI - CW1
PI_SAFE = 3.1415925


def alibi_slopes(n):
    start = 2.0 ** (-8.0 / n)
    return [start ** (i + 1) for i in range(n)]


class Tok:
    __slots__ = ("w", "r")

    def __init__(self):
        self.w = None
        self.r = []


class Op:
    __slots__ = ("eng", "fn", "deps", "is_dma", "sem", "val", "signaled", "slot")

    def __init__(self, eng, fn, is_dma=False, slot=None):
        self.eng = eng
        self.fn = fn
        self.deps = []
        self.is_dma = is_dma
        self.sem = None
        self.val = None
        self.signaled = False
        self.slot = slot


ENGS = ("pe", "act", "dve", "pool", "sp")


def C(method, *a, **k):
    return lambda e: getattr(e, method)(*a, **k)


class Sched:
    def __init__(self, nc):
        self.nc = nc
        self.ops = {e: [] for e in ENGS}
        self.all_ops = []
        self.slot_last = {}

    def tok(self):
        return Tok()

    def _add(self, op, reads, writes):
        deps = {}
        for t in reads:
            if t.w is not None:
                deps[id(t.w)] = (t.w, True)
        for t in writes:
            if t.w is not None:
                deps[id(t.w)] = (t.w, True)
            for r in t.r:
                if id(r) not in deps:
                    deps[id(r)] = (r, False)
        for d, hard in deps.values():
            if d is op:
                continue
            if d.is_dma:
                if op.is_dma and d.slot == op.slot:
                    continue
                d = self.slot_last[d.slot]
            if (not d.is_dma) and d.eng == op.eng and not op.is_dma:
                if op.eng == "pe":
                    continue
            op.deps.append(d)
            d.signaled = True
        for t in reads:
            t.r.append(op)
        for t in writes:
            t.w = op
            t.r = []
        self.ops[op.eng].append(op)
        self.all_ops.append(op)
        return op

    def op(self, eng, fn, reads=(), writes=()):
        return self._add(Op(eng, fn), list(reads), list(writes))

    def dma(self, eng, fn, slot, reads=(), writes=()):
        o = Op(eng, fn, is_dma=True, slot=slot)
        o.signaled = True
        r = self._add(o, list(reads), list(writes))
        self.slot_last[slot] = o
        return r

    def barrier(self):
        last = {}
        for e in ENGS:
            for o in reversed(self.ops[e]):
                if not o.is_dma and o.fn is not None:
                    last[e] = o
                    break
        lastd = dict(self.slot_last)
        for e in ENGS:
            b = Op(e, None)
            for e2, o in last.items():
                if e2 != e:
                    b.deps.append(o)
                    o.signaled = True
            for o in lastd.values():
                b.deps.append(o)
            self.ops[e].append(b)
            self.all_ops.append(b)

    def emit(self, stack):
        nc = self.nc
        sems = {e: stack.enter_context(nc.semaphore("s_" + e)) for e in ENGS}
        slot_sem = {}
        slot_cnt = {}
        for o in self.all_ops:
            if o.is_dma:
                if o.slot not in slot_sem:
                    slot_sem[o.slot] = stack.enter_context(nc.semaphore("d_" + str(o.slot)))
                    slot_cnt[o.slot] = 0
                slot_cnt[o.slot] += 16
                o.sem = slot_sem[o.slot]
                o.val = slot_cnt[o.slot]
        for e in ENGS:
            c = 0
            for o in self.ops[e]:
                if not o.is_dma and o.signaled and o.fn is not None:
                    c += 1
                    o.sem = sems[e]
                    o.val = c
        block = stack.enter_context(nc.Block())
        engmap = {"pe": block.tensor, "act": block.scalar, "dve": block.vector,
                  "pool": block.gpsimd, "sp": block.sync}
        final_waits = [(slot_sem[s], slot_cnt[s]) for s in slot_sem]

        def make(ename):
            ops = self.ops[ename]

            def body(eng):
                waited = {}
                for o in ops:
                    for d in o.deps:
                        if d.sem is None:
                            continue
                        k = id(d.sem)
                        if waited.get(k, 0) >= d.val:
                            continue
                        eng.wait_ge(d.sem, d.val)
                        waited[k] = d.val
                    if o.fn is None:
                        continue
                    ins = o.fn(eng)
                    if o.is_dma:
                        ins.then_inc(o.sem, 16)
                    elif o.signaled:
                        ins.then_inc(o.sem, 1)
                if ename == "sp":
                    for s, v in final_waits:
                        if waited.get(id(s), 0) < v:
                            eng.wait_ge(s, v)
            return body

        for e in ENGS:
            engmap[e](make(e))


class Arena:
    def __init__(self, nc):
        self.nc = nc
        self.lo_ptr = (int(nc.sbuf_base) + 63) // 64 * 64
        self.hi_ptr = int(nc.sbuf_top) // 64 * 64
        self.n = 0

    @staticmethod
    def _bytes(shape, dt):
        n = 1
        for d in shape[1:]:
            n *= d
        return (n * mybir.dt.size(dt) + 63) // 64 * 64

    def lo(self, name, shape, dt):
        nb = self._bytes(shape, dt)
        off = self.lo_ptr
        self.lo_ptr += nb
        assert self.lo_ptr <= self.hi_ptr, "SBUF arena overflow at %s: lo=%d hi=%d" % (name, self.lo_ptr, self.hi_ptr)
        self.n += 1
        return self.nc.alloc_sbuf_tensor_at("%s_%d" % (name, self.n), list(shape), dt, offset=off)

    def hi(self, name, shape, dt):
        nb = self._bytes(shape, dt)
        self.hi_ptr -= nb
        assert self.lo_ptr <= self.hi_ptr, "SBUF arena overflow at %s: lo=%d hi=%d" % (name, self.lo_ptr, self.hi_ptr)
        self.n += 1
        return self.nc.alloc_sbuf_tensor_at("%s_%d" % (name, self.n), list(shape), dt, offset=self.hi_ptr)


def build(nseq, stage=99, dbg=False):
    nc = bass.Bass("TRN2", target_bir_lowering=False)

    def din(name, shape, dt=F32):
        return nc.dram_tensor(name, list(shape), dt, kind="ExternalInput").ap()

    x = din("x", [nseq, SEQ, D])
    pos = din("pos", [nseq, SEQ], I32)
    w_in = din("w_in", [D, D_IN])
    w_kpe = din("w_kpe", [D, 192])
    w_uq = din("w_uq", [384, 768])
    w_uqs = din("w_uqs", [384, 768])
    w_ukv = din("w_ukv", [256, 1024])
    w_omla = din("w_omla", [512, D])
    w_odiff = din("w_odiff", [D, D])
    w_o = din("w_o", [D, D])
    w_up = din("w_up", [D, 4 * D])
    w_down = din("w_down", [4 * D, D])
    b_gate = din("b_gate", [2048])
    g_q = din("g_q", [384])
    g_kv = din("g_kv", [256])
    g_sub = din("g_sub", [128])
    lam_in = [din("lam%d" % i, [1, 64]) for i in range(4)]
    ln_in = [din("ln%d" % i, [1, D]) for i in range(4)]
    cvec = din("cvec", [128, 2])
    out = nc.dram_tensor("out", [nseq, SEQ, D], F32, kind="ExternalOutput").ap()
    dbg_out = {}

    def scr(name, shape):
        return nc.dram_tensor(name, list(shape), BF16, kind="Internal").ap()

    wA_s = scr("wA_s", [128, 8, 832])
    wuq_s = scr("wuq_s", [128, 3, 1536])
    wukv_s = scr("wukv_s", [128, 2, 1024])
    wdv_s = scr("wdv_s", [128, 8, 1024])
    wdqk_s = scr("wdqk_s", [8, 128, 8, 256])
    wmg_s = scr("wmg_s", [8, 128, 28, 128])
    wo_s = scr("wo_s", [128, 8, 1024])
    wup_s = scr("wup_s", [8, 128, 8, 512])
    wdn_s = scr("wdn_s", [8, 128, 4, 1024])
    xT_s = scr("xT_s", [128, 8, SEQ])
    OdT_s = scr("OdT_s", [128, 8, SEQ])

    S = Sched(nc)
    slopes = alibi_slopes(8)

    with ExitStack() as top:
        A = Arena(nc)

        def sbuf(st, name, shape, dt):
            return A.lo(name, shape, dt) if st == "lo" else A.hi(name, shape, dt)

        PS = [top.enter_context(nc.psum_tensor("ps%d" % i, [128, 512], F32)) for i in range(8)]
        tPS = [S.tok() for _ in range(8)]

        ident = sbuf("lo", "ident", [128, 128], F32)
        ones_b = sbuf("lo", "ones_b", [128, 128], BF16)
        ones_f = sbuf("lo", "ones_f", [128, 128], F32)
        tri = sbuf("lo", "tri", [128, 128], BF16)
        t_const = S.tok()
        S.op("pool", C("memset", ident[:], 1.0), writes=[t_const])
        S.op("pool", C("affine_select", out=ident[:], in_=ident[:], pattern=[[-1, 128]], base=0,
                                               channel_multiplier=1, compare_op=ALU.is_equal, fill=0.0),
             reads=[t_const], writes=[t_const])
        S.op("pool", C("memset", tri[:], 1.0), writes=[t_const])
        S.op("pool", C("affine_select", out=tri[:], in_=tri[:], pattern=[[1, 128]], base=0,
                                               channel_multiplier=-1, compare_op=ALU.is_ge, fill=0.0),
             reads=[t_const], writes=[t_const])
        S.op("pool", C("memset", ones_b[:], 1.0), writes=[t_const])
        S.op("pool", C("memset", ones_f[:], 1.0), writes=[t_const])
        eps_rms = sbuf("lo", "eps_rms", [128, 1], F32)
        eps_ln = sbuf("lo", "eps_ln", [128, 1], F32)
        S.op("pool", C("memset", eps_rms[:], RMS_EPS), writes=[t_const])
        S.op("pool", C("memset", eps_ln[:], LN_EPS), writes=[t_const])

        bg = sbuf("lo", "bg", [128, 16, 1], F32)
        gq = sbuf("lo", "gq", [128, 3, 1], F32)
        gkv = sbuf("lo", "gkv", [128, 2, 1], F32)
        gsub = sbuf("lo", "gsub", [128, 1], F32)
        cv = sbuf("lo", "cv", [128, 2], F32)
        lamt = sbuf("lo", "lamt", [128, 4, 64], F32)
        lamw = sbuf("lo", "lamw", [128, 8], F32)
        t_small = S.tok()
        S.dma("sp", C("dma_start", out=bg[:], in_=b_gate.rearrange("(c p o) -> p c o", p=128, o=1), allow_slow_non_contiguous=True), "small", writes=[t_small])
        S.dma("sp", C("dma_start", out=gq[:], in_=g_q.rearrange("(c p o) -> p c o", p=128, o=1), allow_slow_non_contiguous=True), "small", writes=[t_small])
        S.dma("sp", C("dma_start", out=gkv[:], in_=g_kv.rearrange("(c p o) -> p c o", p=128, o=1), allow_slow_non_contiguous=True), "small", writes=[t_small])
        S.dma("sp", C("dma_start", out=gsub[:], in_=g_sub.rearrange("(p o) -> p o", o=1)), "small", writes=[t_small])
        S.dma("sp", C("dma_start", out=cv[:], in_=cvec), "small", writes=[t_small])
        for i in range(4):
            S.dma("sp", C("dma_start", out=lamt[:, i, :], in_=lam_in[i].partition_broadcast(128)), "small", writes=[t_small])
        t_lam = S.tok()
        S.op("dve", C("tensor_tensor", out=lamt[:, 0, :], in0=lamt[:, 0, :], in1=lamt[:, 1, :], op=ALU.mult), reads=[t_small], writes=[t_lam])
        S.op("dve", C("tensor_tensor", out=lamt[:, 2, :], in0=lamt[:, 2, :], in1=lamt[:, 3, :], op=ALU.mult), reads=[t_lam], writes=[t_lam])
        S.op("dve", C("reduce_sum", out=lamw[:, 0:1], in_=lamt[:, 0, :], axis=AX.X), reads=[t_lam], writes=[t_lam])
        S.op("dve", C("reduce_sum", out=lamw[:, 1:2], in_=lamt[:, 2, :], axis=AX.X), reads=[t_lam], writes=[t_lam])
        S.op("act", C("activation", out=lamw[:, 2:4], in_=lamw[:, 0:2], func=AF.Exp), reads=[t_lam], writes=[t_lam])
        S.op("dve", C("tensor_tensor", out=lamw[:, 4:5], in0=lamw[:, 3:4], in1=lamw[:, 2:3], op=ALU.subtract), reads=[t_lam], writes=[t_lam])
        S.op("dve", C("tensor_scalar", out=lamw[:, 5:6], in0=lamw[:, 4:5], scalar1=-LAMBDA_INIT, scalar2=None, op0=ALU.add), reads=[t_lam], writes=[t_lam])
        S.op("dve", C("tensor_scalar", out=gsub[:], in0=gsub[:], scalar1=1.0 - LAMBDA_INIT, scalar2=None, op0=ALU.mult), reads=[t_small], writes=[t_small])
        nlam = lamw[:, 5:6]

        tw = {k: S.tok() for k in ("A", "uq", "ukv", "dv", "dqk", "mg", "o", "up", "dn")}

        thr = [S.tok(), S.tok()]
        cast_n = [0]

        def cast(slot, dst, src, key):
            t = thr[cast_n[0] % 2]
            cast_n[0] += 1
            S.dma("pool", C("dma_start", out=dst, in_=src), slot, writes=[tw[key], t])

        def cp(ap, p=128):
            return ap.rearrange("(c p) n -> p c n", p=p)

        def casts(group):
            if group == 0:
                cast("cA", wA_s[:, :, 0:640], cp(w_in[:, 0:640]), "A")
                cast("cA", wA_s[:, :, 640:832], cp(w_kpe), "A")
                cast("cuq", wuq_s[:, :, 0:768], cp(w_uq), "uq")
                cast("cuq", wuq_s[:, :, 768:1536], cp(w_uqs), "uq")
                cast("cukv", wukv_s, cp(w_ukv), "ukv")
            elif group == 1:
                cast("cdv", wdv_s, cp(w_in[:, O5_:O6_]), "dv")
                for h in range(8):
                    cast("cdqk", wdqk_s[h, :, :, 0:128], cp(w_in[:, O3_ + h * 128:O3_ + (h + 1) * 128]), "dqk")
                    cast("cdqk", wdqk_s[h, :, :, 128:256], cp(w_in[:, O4_ + h * 128:O4_ + (h + 1) * 128]), "dqk")
            elif group == 2:
                for m in range(8):
                    cast("cmg", wmg_s[m, :, 0:4, :], cp(w_omla[:, m * 128:(m + 1) * 128]), "mg")
                    cast("cmg", wmg_s[m, :, 4:12, :], cp(w_odiff[:, m * 128:(m + 1) * 128]), "mg")
                    cast("cmg", wmg_s[m, :, 12:20, :], cp(w_in[:, O6_ + m * 128:O6_ + (m + 1) * 128]), "mg")
                    cast("cmg", wmg_s[m, :, 20:28, :], cp(w_in[:, O6_ + 1024 + m * 128:O6_ + 1024 + (m + 1) * 128]), "mg")
                cast("co", wo_s, cp(w_o), "o")
            else:
                for fb in range(8):
                    cast("cup", wup_s[fb], cp(w_up[:, fb * 512:(fb + 1) * 512]), "up")
                for fb in range(8):
                    cast("cdn", wdn_s[fb], cp(w_down[fb * 512:(fb + 1) * 512, :]), "dn")

        casts(0)
        NRING = 4
        t_xTs = S.tok()
        t_ods = S.tok()
        ring_ctr = [0]
        LO_BASE = A.lo_ptr
        HI_TOP = A.hi_ptr

        def dbg_dump(name, ap, shape, dt=F32):
            if not dbg:
                return
            o = nc.dram_tensor("dbg_" + name, list(shape), dt, kind="ExternalOutput").ap()
            dbg_out[name] = o
            S.barrier()
            S.dma("sp", C("dma_start", out=o, in_=ap), "dbg_" + name)
            S.barrier()

        for b in range(nseq):
            if True:
                A.lo_ptr = LO_BASE
                A.hi_ptr = HI_TOP
                XT_OFF = A.lo_ptr
                xT = sbuf("lo", "xT", [128, 8, SEQ], BF16)
                posf = sbuf("lo", "posf", [128, SEQ], F32)
                pk = sbuf("lo", "pk", [128, 16, 1], F32)
                npk = sbuf("lo", "npk", [128, 16, 1], F32)
                OmT = sbuf("lo", "OmT", [128, 4, SEQ], BF16)
                t_pos = S.tok()
                if True:
                    QT = sbuf("hi", "QT", [128, 8, SEQ], BF16)
                    KT = sbuf("hi", "KT", [128, 8, SEQ], BF16)
                    Vm = sbuf("hi", "Vm", [128, NT, 512], BF16)
                    if True:
                        mkA = A.hi_ptr
                        pki = sbuf("hi", "pki", [128, 16, 1], I32)
                        S.dma("sp", C("dma_start", out=posf[:].bitcast(I32), in_=pos[b:b + 1, :].partition_broadcast(128)), "pos", writes=[t_pos])
                        S.dma("sp", C("dma_start", out=pki[:], in_=pos[b, :].rearrange("(t p o) -> p t o", p=128, o=1), allow_slow_non_contiguous=True), "pos", writes=[t_pos])
                        S.op("dve", C("tensor_copy", out=posf[:], in_=posf[:].bitcast(I32)), reads=[t_pos], writes=[t_pos])
                        S.op("dve", C("tensor_copy", out=pk[:], in_=pki[:]), reads=[t_pos], writes=[t_pos])
                        S.op("dve", C("tensor_scalar", out=npk[:], in0=pk[:], scalar1=-1.0, scalar2=None, op0=ALU.mult), reads=[t_pos], writes=[t_pos])
                        wA = sbuf("hi", "wA", [128, 8, 832], BF16)
                        wuq = sbuf("hi", "wuq", [128, 3, 1536], BF16)
                        wukv = sbuf("hi", "wukv", [128, 2, 1024], BF16)
                        t_wA = S.tok()
                        S.dma("sp", C("dma_start", out=wA[:], in_=wA_s), "wA", reads=[tw["A"]], writes=[t_wA])
                        S.dma("sp", C("dma_start", out=wuq[:], in_=wuq_s), "wA", reads=[tw["uq"]], writes=[t_wA])
                        S.dma("sp", C("dma_start", out=wukv[:], in_=wukv_s), "wA", reads=[tw["ukv"]], writes=[t_wA])
                        xin = [sbuf("hi", "xin%d" % i, [128, D], F32) for i in range(2)]
                        t_xin = [S.tok(), S.tok()]
                        cqf = sbuf("hi", "cqf", [128, 3, 512], F32)
                        sqf = sbuf("hi", "sqf", [128, 3, 512], F32)
                        t_cqf, t_sqf = S.tok(), S.tok()
                        xin += [cqf[:].rearrange("p a b -> p (a b)")[:, 0:D], sqf[:].rearrange("p a b -> p (a b)")[:, 0:D]]
                        t_xin += [t_cqf, t_sqf]
                        NXB = 4
                        t_xT = [S.tok() for _ in range(NCH)]
                        t_xT2 = [S.tok() for _ in range(NCH)]
                        for t in range(NT):
                            xi = xin[t % NXB]
                            S.dma("sp", C("dma_start", out=xi[:, :], in_=x[b, t * 128:(t + 1) * 128, :]),
                                  "xin%d" % (t % NXB), writes=[t_xin[t % NXB]])
                            for half in range(2):
                                bank = (2 * t + half) % 4
                                for c4 in range(4):
                                    c = half * 4 + c4
                                    S.op("pe", C("transpose",
                                        out=PS[bank][:, c4 * 128:(c4 + 1) * 128], in_=xi[:, c * 128:(c + 1) * 128], identity=ident[:]),
                                        reads=[t_xin[t % NXB], t_const], writes=[tPS[bank]])
                                dst = xT[:, half * 4:(half + 1) * 4, t * 128:(t + 1) * 128]
                                src = PS[bank][:].rearrange("p (c n) -> p c n", c=4)
                                if half == 0:
                                    S.op("act", C("copy", out=dst, in_=src), reads=[tPS[bank]], writes=[t_xT[t // 4]])
                                else:
                                    S.op("dve", C("tensor_copy", out=dst, in_=src), reads=[tPS[bank]], writes=[t_xT2[t // 4]])
                        rstd = sbuf("hi", "rstd", [128, 512], F32)
                        cqn = sbuf("hi", "cqn", [128, 3, 512], BF16)
                        ckvn = sbuf("hi", "ckvn", [128, 2, 512], BF16)
                        ang = sbuf("hi", "ang", [128, 512], F32)
                        rr = sbuf("hi", "rr", [128, 512], F32)
                        rc = ang
                        cosT = sbuf("hi", "cosT", [128, 512], F32)
                        sinS = sbuf("hi", "sinS", [128, 512], F32)
                        rt1 = sbuf("hi", "rt1", [128, 512], F32)
                        rt2 = sbuf("hi", "rt2", [128, 512], F32)
                        angn = rt2
                        kper = sbuf("hi", "kper", [128, 512], BF16)
                        t_rstd, t_cqn, t_ckvn = S.tok(), S.tok(), S.tok()
                        t_trig = S.tok()
                        t_rt = t_trig
                        t_kper = S.tok()
                        t_QK = S.tok()
                        t_zero = S.tok()
                        S.op("pool", C("memset", QT[64:128, :, :], 0.0), writes=[t_zero])
                        S.op("pool", C("memset", KT[64:128, :, :], 0.0), writes=[t_zero])
                        R = slice(64, 96)
                        bk = [0]

                        def nb():
                            bk[0] = (bk[0] + 1) % 8
                            return bk[0]

                        for ch in range(NCH):
                            cs = slice(ch * 512, (ch + 1) * 512)
                            S.op("dve", C("tensor_scalar", out=ang[R, :], in0=posf[R, cs], scalar1=cv[R, 0:1], scalar2=None, op0=ALU.mult),
                                 reads=[t_pos, t_small], writes=[t_trig])
                            S.op("dve", C("tensor_scalar", out=rt1[R, :].bitcast(I32), in0=ang[R, :], scalar1=1.0 / TWO_PI, scalar2=None, op0=ALU.mult),
                                 reads=[t_trig], writes=[t_trig])
                            S.op("dve", C("tensor_copy", out=angn[R, :], in_=rt1[R, :].bitcast(I32)), reads=[t_trig], writes=[t_trig])
                            S.op("dve", C("scalar_tensor_tensor", out=rr[R, :], in0=angn[R, :], scalar=-CW1, in1=ang[R, :], op0=ALU.mult, op1=ALU.add),
                                 reads=[t_trig], writes=[t_trig])
                            S.op("dve", C("scalar_tensor_tensor", out=rr[R, :], in0=angn[R, :], scalar=-CW2, in1=rr[R, :], op0=ALU.mult, op1=ALU.add),
                                 reads=[t_trig], writes=[t_trig])
                            S.op("dve", C("tensor_scalar", out=rc[R, :], in0=rr[R, :], scalar1=math.pi / 2, scalar2=None, op0=ALU.add),
                                 reads=[t_trig], writes=[t_trig])
                            S.op("dve", C("tensor_scalar", out=angn[R, :], in0=rc[R, :], scalar1=math.pi, scalar2=-TWO_PI, op0=ALU.is_gt, op1=ALU.mult),
                                 reads=[t_trig], writes=[t_trig])
                            S.op("dve", C("tensor_tensor", out=rc[R, :], in0=rc[R, :], in1=angn[R, :], op=ALU.add), reads=[t_trig], writes=[t_trig])
                            S.op("dve", C("tensor_scalar", out=rc[R, :], in0=rc[R, :], scalar1=-PI_SAFE, scalar2=PI_SAFE, op0=ALU.max, op1=ALU.min),
                                 reads=[t_trig], writes=[t_trig])
                            S.op("dve", C("tensor_scalar", out=rr[R, :], in0=rr[R, :], scalar1=-PI_SAFE, scalar2=PI_SAFE, op0=ALU.max, op1=ALU.min),
                                 reads=[t_trig], writes=[t_trig])
                            S.op("act", C("activation", out=cosT[R, :], in_=rc[R, :], func=AF.Sin), reads=[t_trig], writes=[t_trig])
                            S.op("act", C("activation", out=sinS[R, :], in_=rr[R, :], func=AF.Sin, scale=cv[R, 1:2]), reads=[t_trig, t_small], writes=[t_trig])

                            def proj_norm(col0, nmt, dim, gvec, dstn, t_dst):
                                for mt in range(nmt):
                                    bank = nb()
                                    for c in range(8):
                                        S.op("pe", C("matmul",
                                            PS[bank][:], lhsT=wA[:, c, col0 + mt * 128:col0 + (mt + 1) * 128], rhs=xT[:, c, cs],
                                            start=(c == 0), stop=(c == 7)), reads=[t_wA, t_xT[ch], t_xT2[ch]], writes=[tPS[bank]])
                                    S.op("dve", C("tensor_copy", out=cqf[:, mt, :], in_=PS[bank][:]), reads=[tPS[bank]], writes=[t_cqf])
                                    S.op("act", C("activation", out=sqf[:, mt, :], in_=PS[bank][:], func=AF.Square),
                                         reads=[tPS[bank]], writes=[t_sqf])
                                bank = nb()
                                for mt in range(nmt):
                                    S.op("pe", C("matmul", PS[bank][:], lhsT=ones_f[:], rhs=sqf[:, mt, :],
                                                                                    start=(mt == 0), stop=(mt == nmt - 1)),
                                         reads=[t_sqf, t_const], writes=[tPS[bank]])
                                S.op("act", C("activation", out=rstd[:], in_=PS[bank][:], func=AF.Ln, scale=1.0 / dim, bias=eps_rms[:, 0:1]),
                                     reads=[tPS[bank], t_const], writes=[t_rstd])
                                S.op("act", C("activation", out=rstd[:], in_=rstd[:], func=AF.Exp, scale=-0.5), reads=[t_rstd], writes=[t_rstd])
                                for mt in range(nmt):
                                    S.op("dve", C("scalar_tensor_tensor", out=dstn[:, mt, :], in0=cqf[:, mt, :], scalar=gvec[:, mt, :],
                                                                                        in1=rstd[:], op0=ALU.mult, op1=ALU.mult),
                                         reads=[t_cqf, t_rstd, t_small], writes=[t_dst])

                            proj_norm(0, 3, 384.0, gq, cqn, t_cqn)
                            proj_norm(384, 2, 256.0, gkv, ckvn, t_ckvn)

                            def rope(bq, bs, dst, t_d):
                                S.op("dve", C("tensor_tensor", out=rt1[R, :], in0=PS[bq][R, :], in1=cosT[R, :], op=ALU.mult),
                                     reads=[tPS[bq], t_trig], writes=[t_rt])
                                S.op("dve", C("tensor_tensor", out=rt2[R, :], in0=PS[bs][R, :], in1=sinS[R, :], op=ALU.mult),
                                     reads=[tPS[bs], t_trig], writes=[t_rt])
                                S.op("dve", C("tensor_tensor", out=dst, in0=rt1[R, :], in1=rt2[R, :], op=ALU.add),
                                     reads=[t_rt] + ([t_zero] if t_d is None else []), writes=([] if t_d is None else [t_d]))

                            bq, bs = nb(), nb()
                            for (bank, col0) in ((bq, 640), (bs, 736)):
                                for c in range(8):
                                    S.op("pe", C("matmul",
                                        PS[bank][0:96, :], lhsT=wA[:, c, col0:col0 + 96], rhs=xT[:, c, cs], start=(c == 0), stop=(c == 7)),
                                        reads=[t_wA, t_xT[ch], t_xT2[ch]], writes=[tPS[bank]])
                            rope(bq, bs, kper[R, :], t_kper)
                            S.op("act", C("copy", out=KT[R, :, cs], in_=kper[R, :].unsqueeze(1).broadcast_to([32, 8, 512])),
                                 reads=[t_kper, t_zero], writes=[])
                            for h in range(8):
                                bq, bs = nb(), nb()
                                for (bank, col0) in ((bq, h * 96), (bs, 768 + h * 96)):
                                    for c in range(3):
                                        S.op("pe", C("matmul",
                                            PS[bank][0:96, :], lhsT=wuq[:, c, col0:col0 + 96], rhs=cqn[:, c, :], start=(c == 0), stop=(c == 2)),
                                            reads=[t_wA, t_cqn], writes=[tPS[bank]])
                                S.op("act", C("copy", out=QT[0:64, h, cs], in_=PS[bq][0:64, :]), reads=[tPS[bq]], writes=[])
                                rope(bq, bs, QT[R, h, cs], None)
                                bank = nb()
                                for c in range(2):
                                    S.op("pe", C("matmul",
                                        PS[bank][0:64, :], lhsT=wukv[:, c, h * 64:(h + 1) * 64], rhs=ckvn[:, c, :], start=(c == 0), stop=(c == 1)),
                                        reads=[t_wA, t_ckvn], writes=[tPS[bank]])
                                S.op("act", C("copy", out=KT[0:64, h, cs], in_=PS[bank][0:64, :]), reads=[tPS[bank]], writes=[])
                            for tt in range(4):
                                bank = nb()
                                for c in range(2):
                                    S.op("pe", C("matmul",
                                        PS[bank][:], lhsT=ckvn[:, c, tt * 128:(tt + 1) * 128], rhs=wukv[:, c, 512:1024], start=(c == 0), stop=(c == 1)),
                                        reads=[t_wA, t_ckvn], writes=[tPS[bank]])
                                S.op("dve", C("tensor_copy", out=Vm[:, ch * 4 + tt, :], in_=PS[bank][:]), reads=[tPS[bank]], writes=[])
                    A.hi_ptr = mkA
                    S.barrier()
                    if stage == 1 and dbg:
                        dbg_dump("cqf", cqf[:], [128, 3, 512], F32)
                        dbg_dump("wuq", wuq[:], [128, 3, 1536], BF16)
                        dbg_dump("wukv", wukv[:], [128, 2, 1024], BF16)
                        dbg_dump("sqf", sqf[:], [128, 3, 512], F32)
                        dbg_dump("rstd", rstd[:], [128, 512], F32)
                        dbg_dump("cqn", cqn[:], [128, 3, 512], BF16)
                        dbg_dump("ckvn", ckvn[:], [128, 2, 512], BF16)
                        dbg_dump("cosT", cosT[64:96, :], [32, 512], F32)
                        dbg_dump("sinS", sinS[64:96, :], [32, 512], F32)
                    if stage == 1:
                        dbg_dump("QT", QT[0:96, :, :], [96, 8, SEQ], BF16)
                        dbg_dump("KT", KT[0:96, :, :], [96, 8, SEQ], BF16)
                        dbg_dump("Vm", Vm[:], [128, NT, 512], BF16)
                        dbg_dump("xT", xT[:], [128, 8, SEQ], BF16)
                    if stage >= 2:
                        if True:
                            if b == 0:
                                casts(1)
                            S.dma("sp", C("dma_start", out=xT_s, in_=xT[:]), "xTs", reads=[t_xTs], writes=[t_xTs])
                            attention(S, nc, "hi", PS, tPS, mode="mla", QT=QT, KT=KT, V=Vm, OT=OmT, ones_b=ones_b, tri=tri,
                                      t_const=t_const, sbuf=sbuf)
                        S.barrier()
                A.hi_ptr = HI_TOP
                if stage == 2:
                    dbg_dump("OmT", OmT[:], [128, 4, SEQ], BF16)
                if stage < 3:
                    continue
                if True:
                    dv = sbuf("hi", "dv", [128, NT, 1024], BF16)
                    dqz = sbuf("hi", "dqz", [128, 8, 8, 512], BF16)
                    dkT = sbuf("hi", "dkT", [128, 8, SEQ], BF16)
                    if True:
                        mkC = A.hi_ptr
                        if b == 0:
                            casts(2)
                        wdv = sbuf("hi", "wdv", [128, 8, 1024], BF16)
                        t_wdv = S.tok()
                        S.dma("sp", C("dma_start", out=wdv[:], in_=wdv_s), "wdv", reads=[tw["dv"]], writes=[t_wdv])
                        t_dv = S.tok()
                        t_dqz = S.tok()
                        S.op("pool", C("memset", dqz[0:64, :, :, 256:512], 0.0), writes=[])
                        S.op("pool", C("memset", dqz[64:128, :, :, 0:256], 0.0), writes=[])
                        k = 0
                        for t in range(NT):
                            for j in range(2):
                                bank = k % 8
                                k += 1
                                for c in range(8):
                                    S.op("pe", C("matmul",
                                        PS[bank][:], lhsT=xT[:, c, t * 128:(t + 1) * 128], rhs=wdv[:, c, j * 512:(j + 1) * 512],
                                        start=(c == 0), stop=(c == 7)), reads=[t_wdv], writes=[tPS[bank]])
                                dst = dv[:, t, j * 512:(j + 1) * 512]
                                if k % 2:
                                    S.op("act", C("copy", out=dst, in_=PS[bank][:]), reads=[tPS[bank]], writes=[])
                                else:
                                    S.op("dve", C("tensor_copy", out=dst, in_=PS[bank][:]), reads=[tPS[bank]], writes=[])
                        A.hi_ptr = mkC
                        S.barrier()
                        wq = [sbuf("hi", "wq%d" % i, [128, 8, 256], BF16) for i in range(2)]
                        t_wq = [S.tok(), S.tok()]
                        for h in range(2):
                            S.dma("sp", C("dma_start", out=wq[h][:], in_=wdqk_s[h]), "wq%d" % h, reads=[tw["dqk"]], writes=[t_wq[h]])
                        for h in range(8):
                            hb = h % 2
                            for ch in range(NCH):
                                cs = slice(ch * 512, (ch + 1) * 512)
                                for which in range(2):
                                    bank = k % 8
                                    k += 1
                                    for c in range(8):
                                        S.op("pe", C("matmul", PS[bank][:], lhsT=wq[hb][:, c, which * 128:(which + 1) * 128], rhs=xT[:, c, cs],
                                                     start=(c == 0), stop=(c == 7)), reads=[t_wq[hb]], writes=[tPS[bank]])
                                    if which == 0:
                                        top_src = PS[bank][0:64, :].rearrange("p (b n) -> p b n", b=2)
                                        bot_src = PS[bank][64:128, :].rearrange("p (b n) -> p b n", b=2)
                                        S.op("act", C("copy", out=dqz[0:64, h, 2 * ch:2 * ch + 2, 0:256], in_=top_src), reads=[tPS[bank]], writes=[])
                                        S.op("dve", C("tensor_copy", out=dqz[64:128, h, 2 * ch:2 * ch + 2, 256:512], in_=bot_src), reads=[tPS[bank]], writes=[])
                                    elif k % 2:
                                        S.op("act", C("copy", out=dkT[:, h, cs], in_=PS[bank][:]), reads=[tPS[bank]], writes=[])
                                    else:
                                        S.op("dve", C("tensor_copy", out=dkT[:, h, cs], in_=PS[bank][:]), reads=[tPS[bank]], writes=[])
                            if h + 2 < 8:
                                S.dma("sp", C("dma_start", out=wq[hb][:], in_=wdqk_s[h + 2]), "wq%d" % hb, reads=[tw["dqk"]], writes=[t_wq[hb]])
                    A.hi_ptr = mkC
                    S.barrier()
                    if b == 0:
                        casts(3)
                    save = (A.lo_ptr, A.hi_ptr)
                    A.lo_ptr, A.hi_ptr = XT_OFF, XT_OFF + 32768
                    xt_tmps = {"distc": sbuf("lo", "distc", [128, 16, 256], mybir.dt.int16),
                               "sbq": [sbuf("lo", "sbq%d" % i, [128, 512], F32) for i in range(6)],
                               "pt": [sbuf("lo", "pt%d" % i, [128, 512], BF16) for i in range(8)],
                               "e_r": [sbuf("lo", "e_r0", [128, 512], F32)],
                               "ost": [sbuf("lo", "ost%d" % i, [128, 256], BF16) for i in range(2)]}
                    A.lo_ptr, A.hi_ptr = save
                    xt_tmps["e_r"].append(sbuf("hi", "e_r1", [128, 512], F32))
                    xt_tmps["e_t"] = [sbuf("hi", "e_t%d" % i, [128, 512], F32) for i in range(2)]
                    xt_tmps["e_od"] = [sbuf("hi", "e_od%d" % i, [128, 256], F32) for i in range(2)]
                    xt_tmps["e_sq"] = [sbuf("hi", "e_sq%d" % i, [128, 256], F32) for i in range(2)]
                    xt_tmps["e_rs"] = [sbuf("hi", "e_rs%d" % i, [128, 256], F32) for i in range(2)]
                    attention(S, nc, "hi", PS, tPS, mode="diff", dqz=dqz, dkT=dkT, V=dv, OT=OdT_s, ones_b=ones_b, ones_f=ones_f, tri=tri,
                              t_const=t_const, sbuf=sbuf, posf=posf, npk=npk, nlam=nlam, gsub=gsub,
                              slopes=slopes, eps_rms=eps_rms, tmps=xt_tmps, t_ods=t_ods)
                    S.barrier()
                A.hi_ptr = HI_TOP
                if stage == 3 and dbg:
                    dbg_dump("dqz", dqz[:], [128, 8, 8, 512], BF16)
                    dbg_dump("dkT", dkT[:], [128, 8, SEQ], BF16)
                    dbg_dump("distc", xt_tmps["distc"][:], [128, 16, 256], mybir.dt.int16)
                OdT = sbuf("lo", "OdT", [128, 8, SEQ], BF16)
                t_odr = S.tok()
                S.dma("sp", C("dma_start", out=OdT[:], in_=OdT_s), "odr", reads=[t_ods], writes=[t_odr])
                if stage == 3:
                    dbg_dump("OdT", OdT[:], [128, 8, SEQ], BF16)
                if stage < 4:
                    continue
                x1 = None
                if True:
                    mixT = sbuf("hi", "mixT", [128, 8, SEQ], BF16)
                    if True:
                        mkD = A.hi_ptr
                        wm = [sbuf("hi", "wm%d" % i, [128, 28, 128], BF16) for i in range(2)]
                        t_wm = [S.tok(), S.tok()]
                        g0 = sbuf("hi", "g0", [128, 512], F32)
                        g1 = sbuf("hi", "g1", [128, 512], F32)
                        u0 = sbuf("hi", "u0", [128, 512], F32)
                        u1 = sbuf("hi", "u1", [128, 512], F32)
                        t_g0, t_g1, t_u0, t_u1, t_mix = S.tok(), S.tok(), S.tok(), S.tok(), S.tok()
                        S.dma("sp", C("dma_start", out=wm[0][:], in_=wmg_s[0]), "wm0", reads=[tw["mg"]], writes=[t_wm[0]])
                        t_xTr = S.tok()
                        S.dma("sp", C("dma_start", out=xT[:], in_=xT_s), "xTr", reads=[t_xTs], writes=[t_xTr])
                        k = 0
                        for m in range(8):
                            w = wm[m % 2]
                            if m + 1 < 8:
                                S.dma("sp", C("dma_start", out=wm[(m + 1) % 2][:], in_=wmg_s[m + 1]), "wm%d" % ((m + 1) % 2),
                                      reads=[tw["mg"]], writes=[t_wm[(m + 1) % 2]])
                            for ch in range(NCH):
                                cs = slice(ch * 512, (ch + 1) * 512)
                                b_ym, b_yd, b_g0, b_g1 = [(k * 4 + i) % 8 for i in range(4)]
                                k += 1
                                for c in range(4):
                                    S.op("pe", C("matmul", PS[b_ym][:], lhsT=w[:, c, :], rhs=OmT[:, c, cs],
                                                                                         start=(c == 0), stop=(c == 3)),
                                         reads=[t_wm[m % 2]], writes=[tPS[b_ym]])
                                for c in range(8):
                                    S.op("pe", C("matmul", PS[b_yd][:], lhsT=w[:, 4 + c, :], rhs=OdT[:, c, cs],
                                                                                         start=(c == 0), stop=(c == 7)),
                                         reads=[t_wm[m % 2], t_odr], writes=[tPS[b_yd]])
                                for c in range(8):
                                    S.op("pe", C("matmul", PS[b_g0][:], lhsT=w[:, 12 + c, :], rhs=xT[:, c, cs],
                                                                                         start=(c == 0), stop=(c == 7)),
                                         reads=[t_wm[m % 2], t_xTr], writes=[tPS[b_g0]])
                                for c in range(8):
                                    S.op("pe", C("matmul", PS[b_g1][:], lhsT=w[:, 20 + c, :], rhs=xT[:, c, cs],
                                                                                         start=(c == 0), stop=(c == 7)),
                                         reads=[t_wm[m % 2], t_xTr], writes=[tPS[b_g1]])
                                S.op("act", C("activation", out=g0[:], in_=PS[b_g0][:], func=AF.Sigmoid, bias=bg[:, m, :]),
                                     reads=[tPS[b_g0], t_small], writes=[t_g0])
                                S.op("act", C("activation", out=g1[:], in_=PS[b_g1][:], func=AF.Sigmoid, bias=bg[:, 8 + m, :]),
                                     reads=[tPS[b_g1], t_small], writes=[t_g1])
                                S.op("dve", C("tensor_tensor", out=u0[:], in0=g0[:], in1=PS[b_ym][:], op=ALU.mult),
                                     reads=[t_g0, tPS[b_ym]], writes=[t_u0])
                                S.op("dve", C("tensor_tensor", out=u1[:], in0=g1[:], in1=PS[b_yd][:], op=ALU.mult),
                                     reads=[t_g1, tPS[b_yd]], writes=[t_u1])
                                S.op("pool", C("tensor_tensor", out=mixT[:, m, cs], in0=u0[:], in1=u1[:], op=ALU.add),
                                     reads=[t_u0, t_u1], writes=[t_mix])
                    A.hi_ptr = mkD
                    A.lo_ptr = LO_BASE
                    S.barrier()
                    if stage == 4:
                        dbg_dump("mixT", mixT[:], [128, 8, SEQ], BF16)
                        continue
                    x1 = sbuf("lo", "x1", [128, NT, D], F32)
                    lnp = sbuf("lo", "lnp", [128, 4, D], F32)
                    for i in range(4):
                        S.dma("sp", C("dma_start", out=lnp[:, i, :], in_=ln_in[i].partition_broadcast(128)), "lnp", writes=[t_small])
                    if True:
                        wo = sbuf("hi", "wo", [128, 8, 1024], BF16)
                        t_wo = S.tok()
                        S.dma("sp", C("dma_start", out=wo[:], in_=wo_s), "wo", reads=[tw["o"]], writes=[t_wo])
                        xin = [sbuf("hi", "xin2_%d" % i, [128, D], F32) for i in range(4)]
                        t_xin = [S.tok() for _ in range(4)]
                        t_x1 = [S.tok() for _ in range(NT)]
                        lnw = ln_work(S, nc, "hi", sbuf)
                        prev_st2 = None
                        for t in range(NT):
                            xi = xin[t % 4]
                            S.dma("sp", C("dma_start", out=xi[:], in_=x[b, t * 128:(t + 1) * 128, :]),
                                  "xin2_%d" % (t % 4), writes=[t_xin[t % 4]])
                            banks = [(2 * t) % 8, (2 * t + 1) % 8]
                            for j in range(2):
                                for c in range(8):
                                    S.op("pe", C("matmul",
                                        PS[banks[j]][:], lhsT=mixT[:, c, t * 128:(t + 1) * 128], rhs=wo[:, c, j * 512:(j + 1) * 512],
                                        start=(c == 0), stop=(c == 7)), reads=[t_wo], writes=[tPS[banks[j]]])
                            st2 = resid_ln(S, lnw, xi, t_xin[t % 4], [PS[banks[0]], PS[banks[1]]], [tPS[banks[0]], tPS[banks[1]]],
                                           lnp, 0, x1[:, t, :], t_x1[t], t_small, eps_ln)
                            if prev_st2 is not None:
                                prev_st2()
                            prev_st2 = st2
                        prev_st2()
                    A.hi_ptr = HI_TOP
                    S.barrier()
                    if stage == 5:
                        dbg_dump("x1", x1[:], [128, NT, D], F32)
                        continue
                    if True:
                        ring = [sbuf("hi", "ring%d" % i, [128, 4096], BF16) for i in range(NRING)]
                        t_ring = [S.tok() for _ in range(NRING)]
                        x1T = sbuf("hi", "x1T", [128, 8, 512], BF16)
                        hT = sbuf("hi", "hT", [128, 32, 512], BF16)
                        rl = [sbuf("hi", "rl%d" % i, [128, 512], F32) for i in range(4)]
                        ost = [sbuf("hi", "ost%d" % i, [128, D], F32) for i in range(2)]
                        t_x1T, t_hT, t_hT2 = S.tok(), S.tok(), S.tok()
                        t_rl = [S.tok() for _ in range(4)]
                        t_ost = [S.tok(), S.tok()]
                        lnw = ln_work(S, nc, "hi", sbuf)
                        t_x1c = S.tok()

                        def ring_load(src, key):
                            i = ring_ctr[0] % NRING
                            ring_ctr[0] += 1
                            S.dma("sp", C("dma_start", out=ring[i][:], in_=src), "ring%d" % i, reads=[tw[key]], writes=[t_ring[i]])
                            return i

                        oc = 0
                        PD = 3
                        blocks = []
                        for _ch in range(NCH):
                            blocks += [(wup_s[fb].rearrange("p c n -> p (c n)"), "up") for fb in range(8)]
                            blocks += [(wdn_s[fb].rearrange("p c n -> p (c n)"), "dn") for fb in range(8)]
                        loaded = []
                        nxt = [0]

                        def next_block():
                            while nxt[0] < len(blocks) and len(loaded) < PD:
                                loaded.append(ring_load(*blocks[nxt[0]]))
                                nxt[0] += 1
                            ri = loaded.pop(0)
                            while nxt[0] < len(blocks) and len(loaded) < PD:
                                loaded.append(ring_load(*blocks[nxt[0]]))
                                nxt[0] += 1
                            return ri

                        for ch in range(NCH):
                            for tt in range(4):
                                t = ch * 4 + tt
                                for half in range(2):
                                    bank = (2 * tt + half) % 8
                                    for c4 in range(4):
                                        c = half * 4 + c4
                                        S.op("pe", C("transpose",
                                            out=PS[bank][:, c4 * 128:(c4 + 1) * 128], in_=x1[:, t, c * 128:(c + 1) * 128], identity=ident[:]),
                                            reads=[t_x1c, t_const], writes=[tPS[bank]])
                                    dst = x1T[:, half * 4:(half + 1) * 4, tt * 128:(tt + 1) * 128]
                                    src = PS[bank][:].rearrange("p (c n) -> p c n", c=4)
                                    if half == 0:
                                        S.op("act", C("copy", out=dst, in_=src), reads=[tPS[bank]], writes=[t_x1T])
                                    else:
                                        S.op("dve", C("tensor_copy", out=dst, in_=src), reads=[tPS[bank]], writes=[t_x1T])
                            k = 0
                            for fb in range(8):
                                ri = next_block()
                                wv = ring[ri][:].rearrange("p (c n) -> p c n", c=8)
                                for f4 in range(4):
                                    f = fb * 4 + f4
                                    bank = k % 8
                                    k += 1
                                    for c in range(8):
                                        S.op("pe", C("matmul",
                                            PS[bank][:], lhsT=wv[:, c, f4 * 128:(f4 + 1) * 128], rhs=x1T[:, c, :], start=(c == 0), stop=(c == 7)),
                                            reads=[t_ring[ri], t_x1T], writes=[tPS[bank]])
                                    r = rl[f % 4]
                                    S.op("act", C("activation", out=r[:], in_=PS[bank][:], func=AF.Relu),
                                         reads=[tPS[bank]], writes=[t_rl[f % 4]])
                                    if f % 4 != 3:
                                        S.op("dve", C("tensor_tensor", out=hT[:, f, :], in0=r[:], in1=r[:], op=ALU.mult),
                                             reads=[t_rl[f % 4]], writes=[t_hT])
                                    else:
                                        S.op("pool", C("tensor_tensor", out=hT[:, f, :], in0=r[:], in1=r[:], op=ALU.mult),
                                             reads=[t_rl[f % 4]], writes=[t_hT2])
                            for fb in range(8):
                                ri = next_block()
                                wv = ring[ri][:].rearrange("p (c n) -> p c n", c=4)
                                for tt in range(4):
                                    for j in range(2):
                                        bank = tt * 2 + j
                                        for fi in range(4):
                                            f = fb * 4 + fi
                                            S.op("pe", C("matmul",
                                                PS[bank][:], lhsT=hT[:, f, tt * 128:(tt + 1) * 128], rhs=wv[:, fi, j * 512:(j + 1) * 512],
                                                start=(f == 0), stop=(f == 31)), reads=[t_ring[ri], t_hT, t_hT2], writes=[tPS[bank]])
                            prev_fin = None
                            for tt in range(4):
                                t = ch * 4 + tt
                                o = ost[oc % 2]
                                t_o = t_ost[oc % 2]
                                slot = "ost%d" % (oc % 2)
                                oc += 1
                                st2 = resid_ln(S, lnw, x1[:, t, :], t_x1c, [PS[tt * 2], PS[tt * 2 + 1]], [tPS[tt * 2], tPS[tt * 2 + 1]],
                                               lnp, 2, o[:], t_o, t_small, eps_ln, src_is_ap=True)

                                def fin(st2=st2, o=o, t=t, t_o=t_o, slot=slot):
                                    st2()
                                    S.dma("sp", C("dma_start", out=out[b, t * 128:(t + 1) * 128, :], in_=o[:]), slot, reads=[t_o])
                                if prev_fin is not None:
                                    prev_fin()
                                prev_fin = fin
                            prev_fin()
                    S.barrier()
        S.emit(top)
    return nc, dbg_out


def ln_work(S, nc, st, sbuf):
    w = {"i": 0, "sets": []}
    for i in range(2):
        d = {
            "r": sbuf(st, "ln_r%d" % i, [128, D], F32),
            "st": sbuf(st, "ln_st%d" % i, [128, 2, 6], F32),
            "mv": sbuf(st, "ln_mv%d" % i, [128, 2], F32),
            "rs": sbuf(st, "ln_rs%d" % i, [128, 1], F32),
            "nm": sbuf(st, "ln_nm%d" % i, [128, 1], F32),
            "t": sbuf(st, "ln_t%d" % i, [128, D], F32),
            "tok": S.tok(), "tok2": S.tok(),
        }
        w["sets"].append(d)
    return w


def resid_ln(S, lnw, xsrc, t_x, banks, t_banks, lnp, gi, dst, t_dst, t_small, eps_ln, src_is_ap=False):
    d = lnw["sets"][lnw["i"] % 2]
    lnw["i"] += 1
    r, stt, mv, rs, tmp, tk, tk2 = d["r"], d["st"], d["mv"], d["rs"], d["t"], d["tok"], d["tok2"]
    for j in range(2):
        xs = xsrc[:, j * 512:(j + 1) * 512]
        S.op("dve", C("scalar_tensor_tensor", out=r[:, j * 512:(j + 1) * 512], in0=xs, scalar=ALPHA,
                                                                in1=banks[j][:], op0=ALU.mult, op1=ALU.add),
             reads=[t_x, t_banks[j]], writes=[tk])
        S.op("dve", C("bn_stats", out=stt[:, j, :], in_=r[:, j * 512:(j + 1) * 512]), reads=[tk], writes=[tk])
    S.op("dve", C("bn_aggr", out=mv[:], in_=stt[:].rearrange("p a b -> p (a b)")), reads=[tk], writes=[tk])
    S.op("act", C("activation", out=rs[:], in_=mv[:, 1:2], func=AF.Sqrt, bias=eps_ln[:, 0:1]), reads=[tk], writes=[tk])
    S.op("dve", C("reciprocal", out=rs[:], in_=rs[:]), reads=[tk], writes=[tk])
    S.op("dve", C("scalar_tensor_tensor", out=d["nm"][:], in0=mv[:, 0:1], scalar=-1.0, in1=rs[:], op0=ALU.mult, op1=ALU.mult), reads=[tk], writes=[tk])
    S.op("act", C("activation", out=tmp[:], in_=r[:], func=AF.Identity, scale=rs[:, 0:1], bias=d["nm"][:, 0:1]), reads=[tk], writes=[tk2])
    def stage2():
        S.op("dve", C("tensor_tensor", out=tmp[:], in0=tmp[:], in1=lnp[:, gi, :], op=ALU.mult), reads=[tk2, t_small], writes=[tk2])
        S.op("pool", C("tensor_tensor", out=dst, in0=tmp[:], in1=lnp[:, gi + 1, :], op=ALU.add), reads=[tk2, t_small], writes=[t_dst])
    return stage2


def attention(S, nc, st, PS, tPS, mode, sbuf, ones_b, tri, t_const, V, OT, **kw):
    NPT = 6 if mode == "mla" else 8
    LAG = 4 if mode == "mla" else 5
    NSBQ = 6
    sctr = [0]
    ictr = [0]

    def sbank():
        sctr[0] += 1
        return sctr[0] % 4

    if mode == "mla":
        QT, KT = kw["QT"], kw["KT"]
        pt = [sbuf(st, "pt%d" % i, [128, 512], BF16) for i in range(NPT)]
        rlt = [sbuf(st, "rlt%d" % i, [128, 512], F32) for i in range(2)]
        t_rlt = [S.tok(), S.tok()]
    else:
        dqz, dkT = kw["dqz"], kw["dkT"]
        posf, npk, nlam, gsub, slopes = kw["posf"], kw["npk"], kw["nlam"], kw["gsub"], kw["slopes"]
        ones_f, eps_rms, t_ods = kw["ones_f"], kw["eps_rms"], kw["t_ods"]
        tm = kw["tmps"]
        pt, sbq, distc = tm["pt"], tm["sbq"], tm["distc"]
        e_r2, e_t2, e_od2, e_sq2, e_rs2, ost = tm["e_r"], tm["e_t"], tm["e_od"], tm["e_sq"], tm["e_rs"], tm["ost"]
        t_e2 = [S.tok(), S.tok()]
        t_er2 = [S.tok(), S.tok()]
        t_et2 = [S.tok(), S.tok()]
        t_sbq = [S.tok() for _ in range(NSBQ)]
        t_dist = [S.tok() for _ in range(16)]
        t_ost = [S.tok(), S.tok()]
        octr = [0]
    t_pt = [S.tok() for _ in range(NPT)]
    pending = []

    def issue_S(it):
        i = ictr[0]
        ictr[0] += 1
        it["pi"] = i % NPT
        bk = sbank()
        n0 = it["n0"]
        S.op("pe", C("matmul", PS[bk][:, n0:512], lhsT=it["lhsT"], rhs=it["rhs"], start=True, stop=True),
             reads=it["s_reads"], writes=[tPS[bk]])
        p = pt[it["pi"]]
        tp = t_pt[it["pi"]]
        if mode == "mla":
            S.op("act", C("activation", out=p[:, n0:512], in_=PS[bk][:, n0:512], func=AF.Exp, scale=MLA_SCALE),
                 reads=[tPS[bk]], writes=[tp])
        else:
            q = i % NSBQ
            sb_, tsb = sbq[q], t_sbq[q]
            kt = it["kt"]
            v3 = lambda ap: ap.rearrange("p (c n) -> p c n", c=2)
            dbc = distc[:, kt, :].unsqueeze(1).broadcast_to([128, 2, 256])
            S.op("dve", C("scalar_tensor_tensor", out=v3(sb_[:]), in0=dbc, scalar=it["bscale"], in1=v3(PS[bk][:]),
                          op0=ALU.mult, op1=ALU.add), reads=[t_dist[kt], tPS[bk]], writes=[tsb])
            S.op("act", C("activation", out=p[:], in_=sb_[:], func=AF.Exp, scale=DIFF_SCALE), reads=[tsb], writes=[tp])
            if it["diag"]:
                p3 = v3(p[:])
                tri_bc = tri[:].unsqueeze(1).broadcast_to([128, 2, 128])
                d0 = it["d0"]
                if d0:
                    S.op("pool", C("memset", p3[:, :, 0:128], 0.0), reads=[tp], writes=[tp])
                S.op("pool", C("tensor_tensor", out=p3[:, :, d0:d0 + 128], in0=p3[:, :, d0:d0 + 128], in1=tri_bc, op=ALU.mult),
                     reads=[tp, t_const], writes=[tp])
        if mode == "mla" and it["diag"]:
            S.op("pool", C("tensor_tensor", out=p[:, n0:n0 + 128], in0=p[:, n0:n0 + 128], in1=tri[:], op=ALU.mult),
                 reads=[tp, t_const], writes=[tp])
        if it.get("after_S") is not None:
            it["after_S"]()

    deferred = []
    POST_LAG = 3
    STAGE_LAG = 2

    def defer(cnt, fn, tag):
        deferred.append({"cnt": cnt, "fn": fn, "tag": tag})

    def run_deferred():
        for d in deferred:
            d["cnt"] -= 1
        while deferred and deferred[0]["cnt"] <= 0:
            deferred.pop(0)["fn"]()

    def force_tag(tag):
        while any(d["tag"] == tag for d in deferred):
            deferred.pop(0)["fn"]()

    def issue_PV(it):
        p = pt[it["pi"]]
        tp = t_pt[it["pi"]]
        n0 = it["n0"]
        aO, aL = it["accO"], it["accL"]
        r0, r1 = it["rows"]
        if it["first"]:
            force_tag(("acc", aO))
        S.op("pe", C("matmul", PS[aO][r0:r1, n0:512], lhsT=it["vT"], rhs=p[:, n0:512], start=it["first"], stop=it["last"]),
             reads=[tp], writes=[tPS[aO]])
        S.op("pe", C("matmul", PS[aL][r0:r1, n0:512], lhsT=ones_b[:, 0:r1 - r0], rhs=p[:, n0:512], start=it["first"], stop=it["last"]),
             reads=[tp, t_const], writes=[tPS[aL]])
        if it["post"] is not None:
            defer(POST_LAG, it["post"], ("acc", aO))

    def push(it):
        issue_S(it)
        pending.append(it)
        if len(pending) > LAG:
            issue_PV(pending.pop(0))
        run_deferred()

    def flush():
        while pending:
            issue_PV(pending.pop(0))
        while deferred:
            deferred.pop(0)["fn"]()

    grp = 0
    if mode == "mla":
        for h in range(8):
            ho = (h % 2) * 64
            for qc in range(NCH):
                aO, aL = (4, 5) if grp % 2 == 0 else (6, 7)
                rl_, trl = rlt[grp % 2], t_rlt[grp % 2]
                grp += 1
                nk = 4 * qc + 4

                def post(h=h, qc=qc, aO=aO, aL=aL, ho=ho, rl_=rl_, trl=trl):
                    force_tag(("acc", aO))
                    S.op("act", C("activation", out=rl_[ho:ho + 64, :], in_=PS[aL][ho:ho + 64, :], func=AF.Ln), reads=[tPS[aL]], writes=[trl])
                    S.op("act", C("activation", out=rl_[ho:ho + 64, :], in_=rl_[ho:ho + 64, :], func=AF.Exp, scale=-1.0), reads=[trl], writes=[trl])

                    def post_b():
                        S.op("dve", C("tensor_tensor", out=OT[ho:ho + 64, h // 2, qc * 512:(qc + 1) * 512], in0=PS[aO][ho:ho + 64, :],
                                      in1=rl_[ho:ho + 64, :], op=ALU.mult), reads=[tPS[aO], trl], writes=[])
                    defer(STAGE_LAG, post_b, ("acc", aO))
                for kt in range(nk):
                    j = kt - 4 * qc
                    n0 = max(0, j) * 128
                    push({"n0": n0, "diag": j >= 0,
                          "lhsT": KT[:, h, kt * 128:(kt + 1) * 128], "rhs": QT[:, h, qc * 512 + n0:(qc + 1) * 512],
                          "s_reads": [], "vT": V[:, kt, h * 64:(h + 1) * 64],
                          "accO": aO, "accL": aL, "rows": (ho, ho + 64), "first": kt == 0, "last": kt == nk - 1,
                          "post": post if kt == nk - 1 else None})
        flush()
        return

    def abs_tile(qb, kt):
        S.op("act", C("activation", out=distc[:, kt, :], in_=posf[:, qb * 256:(qb + 1) * 256], func=AF.Abs, bias=npk[:, kt, :]),
             reads=[], writes=[t_dist[kt]])

    for kt in range(2):
        abs_tile(0, kt)
    NQB = SEQ // 256
    for qb in range(NQB):
        nk = 2 * qb + 2
        for h in range(8):
            bscale = -slopes[h] / DIFF_SCALE
            aO, aL = (4, 5) if grp % 2 == 0 else (6, 7)
            grp += 1

            def post(h=h, qb=qb, aO=aO, aL=aL, gi=grp % 2):
                e_r, e_t = e_r2[gi], e_t2[gi]
                e_od, e_sq, e_rs, t_e, t_er, t_et = e_od2[gi], e_sq2[gi], e_rs2[gi], t_e2[gi], t_er2[gi], t_et2[gi]
                force_tag(("acc", aO))
                force_tag(("tmp", gi))
                S.op("act", C("activation", out=e_r[:], in_=PS[aL][:], func=AF.Ln), reads=[tPS[aL]], writes=[t_er])
                S.op("act", C("activation", out=e_r[:], in_=e_r[:], func=AF.Exp, scale=-1.0), reads=[t_er], writes=[t_er])

                def post_b():
                    S.op("dve", C("tensor_tensor", out=e_t[:], in0=PS[aO][:], in1=e_r[:], op=ALU.mult), reads=[tPS[aO], t_er], writes=[t_et])
                    S.op("dve", C("scalar_tensor_tensor", out=e_od[:], in0=e_t[:, 256:512], scalar=nlam, in1=e_t[:, 0:256], op0=ALU.mult, op1=ALU.add),
                         reads=[t_et], writes=[t_e])
                    S.op("pool", C("tensor_tensor", out=e_sq[:], in0=e_od[:], in1=e_od[:], op=ALU.mult), reads=[t_e], writes=[t_e])
                    defer(STAGE_LAG + 1, tail_a, ("tmp", gi))

                def tail_a():
                    bk = sbank()
                    S.op("pe", C("matmul", PS[bk][:, 0:256], lhsT=ones_f[:], rhs=e_sq[:], start=True, stop=True), reads=[t_e, t_const], writes=[tPS[bk]])
                    defer(STAGE_LAG, lambda: tail_b(bk), ("tmp", gi))

                def tail_b(bk):
                    S.op("act", C("activation", out=e_rs[:], in_=PS[bk][:, 0:256], func=AF.Ln, scale=1.0 / 128.0, bias=eps_rms[:, 0:1]),
                         reads=[tPS[bk], t_const], writes=[t_e])
                    S.op("act", C("activation", out=e_rs[:], in_=e_rs[:], func=AF.Exp, scale=-0.5), reads=[t_e], writes=[t_e])
                    defer(STAGE_LAG, tail_c, ("tmp", gi))

                def tail_c():
                    oi = octr[0] % 2
                    octr[0] += 1
                    S.op("dve", C("scalar_tensor_tensor", out=ost[oi][:], in0=e_od[:], scalar=gsub[:, 0:1], in1=e_rs[:],
                                  op0=ALU.mult, op1=ALU.mult), reads=[t_e], writes=[t_ost[oi]])
                    S.dma("sp", C("dma_start", out=OT[:, h, qb * 256:(qb + 1) * 256], in_=ost[oi][:]), "ods%d" % oi, reads=[t_ost[oi]], writes=[t_ods])

                defer(STAGE_LAG, post_b, ("acc", aO))

            for kt in range(nk):
                j = kt - 2 * qb
                after = None
                if h == 7 and qb + 1 < NQB:
                    after = (lambda qb=qb, kt=kt: abs_tile(qb + 1, kt))
                push({"n0": 0, "diag": j >= 0, "d0": max(0, j) * 128, "kt": kt, "bscale": bscale, "after_S": after,
                      "lhsT": dkT[:, h, kt * 128:(kt + 1) * 128], "rhs": dqz[:, h, qb, :],
                      "s_reads": [], "vT": V[:, kt, h * 128:(h + 1) * 128],
                      "accO": aO, "accL": aL, "rows": (0, 128), "first": kt == 0, "last": kt == nk - 1,
                      "post": post if kt == nk - 1 else None})
        if qb + 1 < NQB:
            for kt in range(nk, nk + 2):
                abs_tile(qb + 1, kt)
    flush()


def _host_inputs(inputs):
    f = lambda a: np.ascontiguousarray(np.asarray(a, dtype=np.float32))
    w_in = f(inputs["w_in"][0])
    kpe = w_in[:, O2_:O3_]
    kpe_sw = np.concatenate([kpe[:, 16:32], kpe[:, 0:16]], axis=1)
    w_kpe = np.concatenate([kpe, kpe, kpe, kpe_sw, kpe_sw, kpe_sw], axis=1)
    w_uq = f(inputs["w_uq"][0])
    w_uqs = w_uq.copy()
    for h in range(8):
        b0 = h * 96 + 64
        w_uqs[:, b0:b0 + 16] = w_uq[:, b0 + 16:b0 + 32]
        w_uqs[:, b0 + 16:b0 + 32] = w_uq[:, b0:b0 + 16]
    w_ukv = f(inputs["w_ukv"][0]).reshape(256, 8, 128)
    w_ukv_r = np.concatenate([w_ukv[:, :, 0:64].reshape(256, 512), w_ukv[:, :, 64:128].reshape(256, 512)], axis=1)
    inv_freq = (1.0 / (10000.0 ** (np.arange(0, 32, 2, dtype=np.float32) / 32.0))).astype(np.float32)
    cvec = np.zeros((128, 2), np.float32)
    cvec[64:80, 0] = inv_freq
    cvec[80:96, 0] = inv_freq
    cvec[64:80, 1] = -1.0
    cvec[80:96, 1] = 1.0
    shared = {
        "w_in": w_in, "w_kpe": np.ascontiguousarray(w_kpe), "w_uq": w_uq, "w_uqs": np.ascontiguousarray(w_uqs),
        "w_ukv": np.ascontiguousarray(w_ukv_r), "w_omla": f(inputs["w_o_mla"][0]), "w_odiff": f(inputs["w_o_diff"][0]),
        "w_o": f(inputs["w_o"][0]), "w_up": f(inputs["w_up"][0]), "w_down": f(inputs["w_down"][0]),
        "b_gate": f(inputs["b_gate"][0]), "g_q": f(inputs["mla_q_norm"][0]), "g_kv": f(inputs["mla_kv_norm"][0]),
        "g_sub": f(inputs["diff_subln"][0]),
        "lam0": f(inputs["diff_lambda_q1"]), "lam1": f(inputs["diff_lambda_k1"]),
        "lam2": f(inputs["diff_lambda_q2"]), "lam3": f(inputs["diff_lambda_k2"]),
        "ln0": f(inputs["ln1_g"]), "ln1": f(inputs["ln1_b"]), "ln2": f(inputs["ln2_g"]), "ln3": f(inputs["ln2_b"]),
        "cvec": cvec,
    }
    return shared


_NC_CACHE = {}


def kernel(**inputs):
    x = np.ascontiguousarray(np.asarray(inputs["x"], dtype=np.float32))
    positions = np.ascontiguousarray(np.asarray(inputs["positions"], dtype=np.int32))
    nseq = x.shape[0] // N_CORES
    shared = _host_inputs(inputs)
    if "nc" not in _NC_CACHE:
        _NC_CACHE["nc"] = build(nseq)[0]
    nc = _NC_CACHE["nc"]
    in_maps = []
    for i in range(N_CORES):
        m = dict(shared)
        m["x"] = x[i * nseq:(i + 1) * nseq]
        m["pos"] = positions[i * nseq:(i + 1) * nseq]
        in_maps.append(m)
    res = run_bass_kernel_spmd(nc, in_maps, core_ids=list(range(N_CORES)))
    return np.concatenate([r["out"] for r in res.results], axis=0)
```

```python
import math
import numpy as np
from contextlib import ExitStack
import concourse.bass as bass
import concourse.mybir as mybir
from concourse.bass_utils import run_bass_kernel_spmd

F32 = mybir.dt.float32
BF16 = mybir.dt.bfloat16
I32 = mybir.dt.int32
ALU = mybir.AluOpType
AF = mybir.ActivationFunctionType
AX = mybir.AxisListType

N_CORES = 8
SEQ = 2048
D = 1024
NT = 16
NCH = 4
O1_, O2_, O3_, O4_, O5_, O6_ = 384, 640, 672, 1696, 2720, 3744
D_IN = 5792
ALPHA = 2.0 ** 0.25
LAMBDA_INIT = 0.2
MLA_SCALE = 96.0 ** -0.5
DIFF_SCALE = 0.125
LN_EPS = 1e-5
RMS_EPS = 1e-6
TWO_PI = 2.0 * math.pi
CW1 = 6.28125
CW2 = TWO_PI - CW1
PI_SAFE = 3.1415925


def alibi_slopes(n):
    start = 2.0 ** (-8.0 / n)
    return [start ** (i + 1) for i in range(n)]


class Tok:
    __slots__ = ("w", "r")

    def __init__(self):
        self.w = None
        self.r = []


class Op:
    __slots__ = ("eng", "fn", "deps", "is_dma", "sem", "val", "signaled", "slot")

    def __init__(self, eng, fn, is_dma=False, slot=None):
        self.eng = eng
        self.fn = fn
        self.deps = []
        self.is_dma = is_dma
        self.sem = None
        self.val = None
        self.signaled = False
        self.slot = slot


ENGS = ("pe", "act", "dve", "pool", "sp")


def C(method, *a, **k):
    return lambda e: getattr(e, method)(*a, **k)


class Sched:
    def __init__(self, nc):
        self.nc = nc
        self.ops = {e: [] for e in ENGS}
        self.all_ops = []
        self.slot_last = {}

    def tok(self):
        return Tok()

    def _add(self, op, reads, writes):
        deps = {}
        for t in reads:
            if t.w is not None:
                deps[id(t.w)] = (t.w, True)
        for t in writes:
            if t.w is not None:
                deps[id(t.w)] = (t.w, True)
            for r in t.r:
                if id(r) not in deps:
                    deps[id(r)] = (r, False)
        for d, hard in deps.values():
            if d is op:
                continue
            if d.is_dma:
                if op.is_dma and d.slot == op.slot:
                    continue
                d = self.slot_last[d.slot]
            if (not d.is_dma) and d.eng == op.eng and not op.is_dma:
                if op.eng == "pe":
                    continue
            op.deps.append(d)
            d.signaled = True
        for t in reads:
            t.r.append(op)
        for t in writes:
            t.w = op
            t.r = []
        self.ops[op.eng].append(op)
        self.all_ops.append(op)
        return op

    def op(self, eng, fn, reads=(), writes=()):
        return self._add(Op(eng, fn), list(reads), list(writes))

    def dma(self, eng, fn, slot, reads=(), writes=()):
        o = Op(eng, fn, is_dma=True, slot=slot)
        o.signaled = True
        r = self._add(o, list(reads), list(writes))
        self.slot_last[slot] = o
        return r

    def barrier(self):
        last = {}
        for e in ENGS:
            for o in reversed(self.ops[e]):
                if not o.is_dma and o.fn is not None:
                    last[e] = o
                    break
        lastd = dict(self.slot_last)
        for e in ENGS:
            b = Op(e, None)
            for e2, o in last.items():
                if e2 != e:
                    b.deps.append(o)
                    o.signaled = True
            for o in lastd.values():
                b.deps.append(o)
            self.ops[e].append(b)
            self.all_ops.append(b)

    def emit(self, stack):
        nc = self.nc
        sems = {e: stack.enter_context(nc.semaphore("s_" + e)) for e in ENGS}
        slot_sem = {}
        slot_cnt = {}
        for o in self.all_ops:
            if o.is_dma:
                if o.slot not in slot_sem:
                    slot_sem[o.slot] = stack.enter_context(nc.semaphore("d_" + str(o.slot)))
                    slot_cnt[o.slot] = 0
                slot_cnt[o.slot] += 16
                o.sem = slot_sem[o.slot]
                o.val = slot_cnt[o.slot]
        for e in ENGS:
            c = 0
            for o in self.ops[e]:
                if not o.is_dma and o.signaled and o.fn is not None:
                    c += 1
                    o.sem = sems[e]
                    o.val = c
        block = stack.enter_context(nc.Block())
        engmap = {"pe": block.tensor, "act": block.scalar, "dve": block.vector,
                  "pool": block.gpsimd, "sp": block.sync}
        final_waits = [(slot_sem[s], slot_cnt[s]) for s in slot_sem]

        def make(ename):
            ops = self.ops[ename]

            def body(eng):
                waited = {}
                for o in ops:
                    for d in o.deps:
                        if d.sem is None:
                            continue
                        k = id(d.sem)
                        if waited.get(k, 0) >= d.val:
                            continue
                        eng.wait_ge(d.sem, d.val)
                        waited[k] = d.val
                    if o.fn is None:
                        continue
                    ins = o.fn(eng)
                    if o.is_dma:
                        ins.then_inc(o.sem, 16)
                    elif o.signaled:
                        ins.then_inc(o.sem, 1)
                if ename == "sp":
                    for s, v in final_waits:
                        if waited.get(id(s), 0) < v:
                            eng.wait_ge(s, v)
            return body

        for e in ENGS:
            engmap[e](make(e))


class Arena:
    def __init__(self, nc):
        self.nc = nc
        self.lo_ptr = (int(nc.sbuf_base) + 63) // 64 * 64
        self.hi_ptr = int(nc.sbuf_top) // 64 * 64
        self.n = 0

    @staticmethod
    def _bytes(shape, dt):
        n = 1
        for d in shape[1:]:
            n *= d
        return (n * mybir.dt.size(dt) + 63) // 64 * 64

    def lo(self, name, shape, dt):
        nb = self._bytes(shape, dt)
        off = self.lo_ptr
        self.lo_ptr += nb
        assert self.lo_ptr <= self.hi_ptr, "SBUF arena overflow at %s: lo=%d hi=%d" % (name, self.lo_ptr, self.hi_ptr)
        self.n += 1
        return self.nc.alloc_sbuf_tensor_at("%s_%d" % (name, self.n), list(shape), dt, offset=off)

    def hi(self, name, shape, dt):
        nb = self._bytes(shape, dt)
        self.hi_ptr -= nb
        assert self.lo_ptr <= self.hi_ptr, "SBUF arena overflow at %s: lo=%d hi=%d" % (name, self.lo_ptr, self.hi_ptr)
        self.n += 1
        return self.nc.alloc_sbuf_tensor_at("%s_%d" % (name, self.n), list(shape), dt, offset=self.hi_ptr)


def build(nseq, stage=99, dbg=False):
    nc = bass.Bass("TRN2", target_bir_lowering=False)

    def din(name, shape, dt=F32):
        return nc.dram_tensor(name, list(shape), dt, kind="ExternalInput").ap()

    x = din("x", [nseq, SEQ, D])
    pos = din("pos", [nseq, SEQ], I32)
    w_in = din("w_in", [D, D_IN])
    w_kpe = din("w_kpe", [D, 192])
    w_uq = din("w_uq", [384, 768])
    w_uqs = din("w_uqs", [384, 768])
    w_ukv = din("w_ukv", [256, 1024])
    w_omla = din("w_omla", [512, D])
    w_odiff = din("w_odiff", [D, D])
    w_o = din("w_o", [D, D])
    w_up = din("w_up", [D, 4 * D])
    w_down = din("w_down", [4 * D, D])
    b_gate = din("b_gate", [2048])
    g_q = din("g_q", [384])
    g_kv = din("g_kv", [256])
    g_sub = din("g_sub", [128])
    lam_in = [din("lam%d" % i, [1, 64]) for i in range(4)]
    ln_in = [din("ln%d" % i, [1, D]) for i in range(4)]
    cvec = din("cvec", [128, 2])
    out = nc.dram_tensor("out", [nseq, SEQ, D], F32, kind="ExternalOutput").ap()
    dbg_out = {}

    def scr(name, shape):
        return nc.dram_tensor(name, list(shape), BF16, kind="Internal").ap()

    wA_s = scr("wA_s", [128, 8, 832])
    wuq_s = scr("wuq_s", [128, 3, 1536])
    wukv_s = scr("wukv_s", [128, 2, 1024])
    wdv_s = scr("wdv_s", [128, 8, 1024])
    wdqk_s = scr("wdqk_s", [8, 128, 8, 256])
    wmg_s = scr("wmg_s", [8, 128, 28, 128])
    wo_s = scr("wo_s", [128, 8, 1024])
    wup_s = scr("wup_s", [8, 128, 8, 512])
    wdn_s = scr("wdn_s", [8, 128, 4, 1024])
    xT_s = scr("xT_s", [128, 8, SEQ])
    OdT_s = scr("OdT_s", [128, 8, SEQ])

    S = Sched(nc)
    slopes = alibi_slopes(8)

    with ExitStack() as top:
        A = Arena(nc)

        def sbuf(st, name, shape, dt):
            return A.lo(name, shape, dt) if st == "lo" else A.hi(name, shape, dt)

        PS = [top.enter_context(nc.psum_tensor("ps%d" % i, [128, 512], F32)) for i in range(8)]
        tPS = [S.tok() for _ in range(8)]

        ident = sbuf("lo", "ident", [128, 128], F32)
        ones_b = sbuf("lo", "ones_b", [128, 128], BF16)
        ones_f = sbuf("lo", "ones_f", [128, 128], F32)
        tri = sbuf("lo", "tri", [128, 128], BF16)
        t_const = S.tok()
        S.op("pool", C("memset", ident[:], 1.0), writes=[t_const])
        S.op("pool", C("affine_select", out=ident[:], in_=ident[:], pattern=[[-1, 128]], base=0,
                                               channel_multiplier=1, compare_op=ALU.is_equal, fill=0.0),
             reads=[t_const], writes=[t_const])
        S.op("pool", C("memset", tri[:], 1.0), writes=[t_const])
        S.op("pool", C("affine_select", out=tri[:], in_=tri[:], pattern=[[1, 128]], base=0,
                                               channel_multiplier=-1, compare_op=ALU.is_ge, fill=0.0),
             reads=[t_const], writes=[t_const])
        S.op("pool", C("memset", ones_b[:], 1.0), writes=[t_const])
        S.op("pool", C("memset", ones_f[:], 1.0), writes=[t_const])
        eps_rms = sbuf("lo", "eps_rms", [128, 1], F32)
        eps_ln = sbuf("lo", "eps_ln", [128, 1], F32)
        S.op("pool", C("memset", eps_rms[:], RMS_EPS), writes=[t_const])
        S.op("pool", C("memset", eps_ln[:], LN_EPS), writes=[t_const])

        bg = sbuf("lo", "bg", [128, 16, 1], F32)
        gq = sbuf("lo", "gq", [128, 3, 1], F32)
        gkv = sbuf("lo", "gkv", [128, 2, 1], F32)
        gsub = sbuf("lo", "gsub", [128, 1], F32)
        cv = sbuf("lo", "cv", [128, 2], F32)
        lamt = sbuf("lo", "lamt", [128, 4, 64], F32)
        lamw = sbuf("lo", "lamw", [128, 8], F32)
        t_small = S.tok()
        S.dma("sp", C("dma_start", out=bg[:], in_=b_gate.rearrange("(c p o) -> p c o", p=128, o=1), allow_slow_non_contiguous=True), "small", writes=[t_small])
        S.dma("sp", C("dma_start", out=gq[:], in_=g_q.rearrange("(c p o) -> p c o", p=128, o=1), allow_slow_non_contiguous=True), "small", writes=[t_small])
        S.dma("sp", C("dma_start", out=gkv[:], in_=g_kv.rearrange("(c p o) -> p c o", p=128, o=1), allow_slow_non_contiguous=True), "small", writes=[t_small])
        S.dma("sp", C("dma_start", out=gsub[:], in_=g_sub.rearrange("(p o) -> p o", o=1)), "small", writes=[t_small])
        S.dma("sp", C("dma_start", out=cv[:], in_=cvec), "small", writes=[t_small])
        for i in range(4):
            S.dma("sp", C("dma_start", out=lamt[:, i, :], in_=lam_in[i].partition_broadcast(128)), "small", writes=[t_small])
        t_lam = S.tok()
        S.op("dve", C("tensor_tensor", out=lamt[:, 0, :], in0=lamt[:, 0, :], in1=lamt[:, 1, :], op=ALU.mult), reads=[t_small], writes=[t_lam])
        S.op("dve", C("tensor_tensor", out=lamt[:, 2, :], in0=lamt[:, 2, :], in1=lamt[:, 3, :], op=ALU.mult), reads=[t_lam], writes=[t_lam])
        S.op("dve", C("reduce_sum", out=lamw[:, 0:1], in_=lamt[:, 0, :], axis=AX.X), reads=[t_lam], writes=[t_lam])
        S.op("dve", C("reduce_sum", out=lamw[:, 1:2], in_=lamt[:, 2, :], axis=AX.X), reads=[t_lam], writes=[t_lam])
        S.op("act", C("activation", out=lamw[:, 2:4], in_=lamw[:, 0:2], func=AF.Exp), reads=[t_lam], writes=[t_lam])
        S.op("dve", C("tensor_tensor", out=lamw[:, 4:5], in0=lamw[:, 3:4], in1=lamw[:, 2:3], op=ALU.subtract), reads=[t_lam], writes=[t_lam])
        S.op("dve", C("tensor_scalar", out=lamw[:, 5:6], in0=lamw[:, 4:5], scalar1=-LAMBDA_INIT, scalar2=None, op0=ALU.add), reads=[t_lam], writes=[t_lam])
        S.op("dve", C("tensor_scalar", out=gsub[:], in0=gsub[:], scalar1=1.0 - LAMBDA_INIT, scalar2=None, op0=ALU.mult), reads=[t_small], writes=[t_small])
        nlam = lamw[:, 5:6]

        tw = {k: S.tok() for k in ("A", "uq", "ukv", "dv", "dqk", "mg", "o", "up", "dn")}

        thr = [S.tok(), S.tok()]
        cast_n = [0]

        def cast(slot, dst, src, key):
            t = thr[cast_n[0] % 2]
            cast_n[0] += 1
            S.dma("pool", C("dma_start", out=dst, in_=src), slot, writes=[tw[key], t])

        def cp(ap, p=128):
            return ap.rearrange("(c p) n -> p c n", p=p)

        def casts(group):
            if group == 0:
                cast("cA", wA_s[:, :, 0:640], cp(w_in[:, 0:640]), "A")
                cast("cA", wA_s[:, :, 640:832], cp(w_kpe), "A")
                cast("cuq", wuq_s[:, :, 0:768], cp(w_uq), "uq")
                cast("cuq", wuq_s[:, :, 768:1536], cp(w_uqs), "uq")
                cast("cukv", wukv_s, cp(w_ukv), "ukv")
            elif group == 1:
                cast("cdv", wdv_s, cp(w_in[:, O5_:O6_]), "dv")
                for h in range(8):
                    cast("cdqk", wdqk_s[h, :, :, 0:128], cp(w_in[:, O3_ + h * 128:O3_ + (h + 1) * 128]), "dqk")
                    cast("cdqk", wdqk_s[h, :, :, 128:256], cp(w_in[:, O4_ + h * 128:O4_ + (h + 1) * 128]), "dqk")
            elif group == 2:
                for m in range(8):
                    cast("cmg", wmg_s[m, :, 0:4, :], cp(w_omla[:, m * 128:(m + 1) * 128]), "mg")
                    cast("cmg", wmg_s[m, :, 4:12, :], cp(w_odiff[:, m * 128:(m + 1) * 128]), "mg")
                    cast("cmg", wmg_s[m, :, 12:20, :], cp(w_in[:, O6_ + m * 128:O6_ + (m + 1) * 128]), "mg")
                    cast("cmg", wmg_s[m, :, 20:28, :], cp(w_in[:, O6_ + 1024 + m * 128:O6_ + 1024 + (m + 1) * 128]), "mg")
                cast("co", wo_s, cp(w_o), "o")
            else:
                for fb in range(8):
                    cast("cup", wup_s[fb], cp(w_up[:, fb * 512:(fb + 1) * 512]), "up")
                for fb in range(8):
                    cast("cdn", wdn_s[fb], cp(w_down[fb * 512:(fb + 1) * 512, :]), "dn")

        casts(0)
        NRING = 4
        t_xTs = S.tok()
        t_ods = S.tok()
        ring_ctr = [0]
        LO_BASE = A.lo_ptr
        HI_TOP = A.hi_ptr

        def dbg_dump(name, ap, shape, dt=F32):
            if not dbg:
                return
            o = nc.dram_tensor("dbg_" + name, list(shape), dt, kind="ExternalOutput").ap()
            dbg_out[name] = o
            S.barrier()
            S.dma("sp", C("dma_start", out=o, in_=ap), "dbg_" + name)
            S.barrier()

        for b in range(nseq):
            if True:
                A.lo_ptr = LO_BASE
                A.hi_ptr = HI_TOP
                XT_OFF = A.lo_ptr
                xT = sbuf("lo", "xT", [128, 8, SEQ], BF16)
                posf = sbuf("lo", "posf", [128, SEQ], F32)
                pk = sbuf("lo", "pk", [128, 16, 1], F32)
                npk = sbuf("lo", "npk", [128, 16, 1], F32)
                OmT = sbuf("lo", "OmT", [128, 4, SEQ], BF16)
                t_pos = S.tok()
                if True:
                    QT = sbuf("hi", "QT", [128, 8, SEQ], BF16)
                    KT = sbuf("hi", "KT", [128, 8, SEQ], BF16)
                    Vm = sbuf("hi", "Vm", [128, NT, 512], BF16)
                    if True:
                        mkA = A.hi_ptr
                        pki = sbuf("hi", "pki", [128, 16, 1], I32)
                        S.dma("sp", C("dma_start", out=posf[:].bitcast(I32), in_=pos[b:b + 1, :].partition_broadcast(128)), "pos", writes=[t_pos])
                        S.dma("sp", C("dma_start", out=pki[:], in_=pos[b, :].rearrange("(t p o) -> p t o", p=128, o=1), allow_slow_non_contiguous=True), "pos", writes=[t_pos])
                        S.op("dve", C("tensor_copy", out=posf[:], in_=posf[:].bitcast(I32)), reads=[t_pos], writes=[t_pos])
                        S.op("dve", C("tensor_copy", out=pk[:], in_=pki[:]), reads=[t_pos], writes=[t_pos])
                        S.op("dve", C("tensor_scalar", out=npk[:], in0=pk[:], scalar1=-1.0, scalar2=None, op0=ALU.mult), reads=[t_pos], writes=[t_pos])
                        wA = sbuf("hi", "wA", [128, 8, 832], BF16)
                        wuq = sbuf("hi", "wuq", [128, 3, 1536], BF16)
                        wukv = sbuf("hi", "wukv", [128, 2, 1024], BF16)
                        t_wA = S.tok()
                        S.dma("sp", C("dma_start", out=wA[:], in_=wA_s), "wA", reads=[tw["A"]], writes=[t_wA])
                        S.dma("sp", C("dma_start", out=wuq[:], in_=wuq_s), "wA", reads=[tw["uq"]], writes=[t_wA])
                        S.dma("sp", C("dma_start", out=wukv[:], in_=wukv_s), "wA", reads=[tw["ukv"]], writes=[t_wA])
                        xin = [sbuf("hi", "xin%d" % i, [128, D], F32) for i in range(2)]
                        t_xin = [S.tok(), S.tok()]
                        cqf = sbuf("hi", "cqf", [128, 3, 512], F32)
                        sqf = sbuf("hi", "sqf", [128, 3, 512], F32)
                        t_cqf, t_sqf = S.tok(), S.tok()
                        xin += [cqf[:].rearrange("p a b -> p (a b)")[:, 0:D], sqf[:].rearrange("p a b -> p (a b)")[:, 0:D]]
                        t_xin += [t_cqf, t_sqf]
                        NXB = 4
                        t_xT = [S.tok() for _ in range(NCH)]
                        t_xT2 = [S.tok() for _ in range(NCH)]
                        for t in range(NT):
                            xi = xin[t % NXB]
                            S.dma("sp", C("dma_start", out=xi[:, :], in_=x[b, t * 128:(t + 1) * 128, :]),
                                  "xin%d" % (t % NXB), writes=[t_xin[t % NXB]])
                            for half in range(2):
                                bank = (2 * t + half) % 4
                                for c4 in range(4):
                                    c = half * 4 + c4
                                    S.op("pe", C("transpose",
                                        out=PS[bank][:, c4 * 128:(c4 + 1) * 128], in_=xi[:, c * 128:(c + 1) * 128], identity=ident[:]),
                                        reads=[t_xin[t % NXB], t_const], writes=[tPS[bank]])
                                dst = xT[:, half * 4:(half + 1) * 4, t * 128:(t + 1) * 128]
                                src = PS[bank][:].rearrange("p (c n) -> p c n", c=4)
                                if half == 0:
                                    S.op("act", C("copy", out=dst, in_=src), reads=[tPS[bank]], writes=[t_xT[t // 4]])
                                else:
                                    S.op("dve", C("tensor_copy", out=dst, in_=src), reads=[tPS[bank]], writes=[t_xT2[t // 4]])
                        rstd = sbuf("hi", "rstd", [128, 512], F32)
                        cqn = sbuf("hi", "cqn", [128, 3, 512], BF16)
                        ckvn = sbuf("hi", "ckvn", [128, 2, 512], BF16)
                        ang = sbuf("hi", "ang", [128, 512], F32)
                        rr = sbuf("hi", "rr", [128, 512], F32)
                        rc = ang
                        cosT = sbuf("hi", "cosT", [128, 512], F32)
                        sinS = sbuf("hi", "sinS", [128, 512], F32)
                        rt1 = sbuf("hi", "rt1", [128, 512], F32)
                        rt2 = sbuf("hi", "rt2", [128, 512], F32)
                        angn = rt2
                        kper = sbuf("hi", "kper", [128, 512], BF16)
                        t_rstd, t_cqn, t_ckvn = S.tok(), S.tok(), S.tok()
                        t_trig = S.tok()
                        t_rt = t_trig
                        t_kper = S.tok()
                        t_QK = S.tok()
                        t_zero = S.tok()
                        S.op("pool", C("memset", QT[64:128, :, :], 0.0), writes=[t_zero])
                        S.op("pool", C("memset", KT[64:128, :, :], 0.0), writes=[t_zero])
                        R = slice(64, 96)
                        bk = [0]

                        def nb():
                            bk[0] = (bk[0] + 1) % 8
                            return bk[0]

                        for ch in range(NCH):
                            cs = slice(ch * 512, (ch + 1) * 512)
                            S.op("dve", C("tensor_scalar", out=ang[R, :], in0=posf[R, cs], scalar1=cv[R, 0:1], scalar2=None, op0=ALU.mult),
                                 reads=[t_pos, t_small], writes=[t_trig])
                            S.op("dve", C("tensor_scalar", out=rt1[R, :].bitcast(I32), in0=ang[R, :], scalar1=1.0 / TWO_PI, scalar2=None, op0=ALU.mult),
                                 reads=[t_trig], writes=[t_trig])
                            S.op("dve", C("tensor_copy", out=angn[R, :], in_=rt1[R, :].bitcast(I32)), reads=[t_trig], writes=[t_trig])
                            S.op("dve", C("scalar_tensor_tensor", out=rr[R, :], in0=angn[R, :], scalar=-CW1, in1=ang[R, :], op0=ALU.mult, op1=ALU.add),
                                 reads=[t_trig], writes=[t_trig])
                            S.op("dve", C("scalar_tensor_tensor", out=rr[R, :], in0=angn[R, :], scalar=-CW2, in1=rr[R, :], op0=ALU.mult, op1=ALU.add),
                                 reads=[t_trig], writes=[t_trig])
                            S.op("dve", C("tensor_scalar", out=rc[R, :], in0=rr[R, :], scalar1=math.pi / 2, scalar2=None, op0=ALU.add),
                                 reads=[t_trig], writes=[t_trig])
                            S.op("dve", C("tensor_scalar", out=angn[R, :], in0=rc[R, :], scalar1=math.pi, scalar2=-TWO_PI, op0=ALU.is_gt, op1=ALU.mult),
                                 reads=[t_trig], writes=[t_trig])
                            S.op("dve", C("tensor_tensor", out=rc[R, :], in0=rc[R, :], in1=angn[R, :], op=ALU.add), reads=[t_trig], writes=[t_trig])
                            S.op("dve", C("tensor_scalar", out=rc[R, :], in0=rc[R, :], scalar1=-PI_SAFE, scalar2=PI_SAFE, op0=ALU.max, op1=ALU.min),
                                 reads=[t_trig], writes=[t_trig])
                            S.op("dve", C("tensor_scalar", out=rr[R, :], in0=rr[R, :], scalar1=-PI_SAFE, scalar2=PI_SAFE, op0=ALU.max, op1=ALU.min),
                                 reads=[t_trig], writes=[t_trig])
                            S.op("act", C("activation", out=cosT[R, :], in_=rc[R, :], func=AF.Sin), reads=[t_trig], writes=[t_trig])
                            S.op("act", C("activation", out=sinS[R, :], in_=rr[R, :], func=AF.Sin, scale=cv[R, 1:2]), reads=[t_trig, t_small], writes=[t_trig])

                            def proj_norm(col0, nmt, dim, gvec, dstn, t_dst):
                                for mt in range(nmt):
                                    bank = nb()
                                    for c in range(8):
                                        S.op("pe", C("matmul",
                                            PS[bank][:], lhsT=wA[:, c, col0 + mt * 128:col0 + (mt + 1) * 128], rhs=xT[:, c, cs],
                                            start=(c == 0), stop=(c == 7)), reads=[t_wA, t_xT[ch], t_xT2[ch]], writes=[tPS[bank]])
                                    S.op("dve", C("tensor_copy", out=cqf[:, mt, :], in_=PS[bank][:]), reads=[tPS[bank]], writes=[t_cqf])
                                    S.op("act", C("activation", out=sqf[:, mt, :], in_=PS[bank][:], func=AF.Square),
                                         reads=[tPS[bank]], writes=[t_sqf])
                                bank = nb()
                                for mt in range(nmt):
                                    S.op("pe", C("matmul", PS[bank][:], lhsT=ones_f[:], rhs=sqf[:, mt, :],
                                                                                    start=(mt == 0), stop=(mt == nmt - 1)),
                                         reads=[t_sqf, t_const], writes=[tPS[bank]])
                                S.op("act", C("activation", out=rstd[:], in_=PS[bank][:], func=AF.Ln, scale=1.0 / dim, bias=eps_rms[:, 0:1]),
                                     reads=[tPS[bank], t_const], writes=[t_rstd])
                                S.op("act", C("activation", out=rstd[:], in_=rstd[:], func=AF.Exp, scale=-0.5), reads=[t_rstd], writes=[t_rstd])
                                for mt in range(nmt):
                                    S.op("dve", C("scalar_tensor_tensor", out=dstn[:, mt, :], in0=cqf[:, mt, :], scalar=gvec[:, mt, :],
                                                                                        in1=rstd[:], op0=ALU.mult, op1=ALU.mult),
                                         reads=[t_cqf, t_rstd, t_small], writes=[t_dst])

                            proj_norm(0, 3, 384.0, gq, cqn, t_cqn)
                            proj_norm(384, 2, 256.0, gkv, ckvn, t_ckvn)

                            def rope(bq, bs, dst, t_d):
                                S.op("dve", C("tensor_tensor", out=rt1[R, :], in0=PS[bq][R, :], in1=cosT[R, :], op=ALU.mult),
                                     reads=[tPS[bq], t_trig], writes=[t_rt])
                                S.op("dve", C("tensor_tensor", out=rt2[R, :], in0=PS[bs][R, :], in1=sinS[R, :], op=ALU.mult),
                                     reads=[tPS[bs], t_trig], writes=[t_rt])
                                S.op("dve", C("tensor_tensor", out=dst, in0=rt1[R, :], in1=rt2[R, :], op=ALU.add),
                                     reads=[t_rt] + ([t_zero] if t_d is None else []), writes=([] if t_d is None else [t_d]))

                            bq, bs = nb(), nb()
                            for (bank, col0) in ((bq, 640), (bs, 736)):
                                for c in range(8):
                                    S.op("pe", C("matmul",
                                        PS[bank][0:96, :], lhsT=wA[:, c, col0:col0 + 96], rhs=xT[:, c, cs], start=(c == 0), stop=(c == 7)),
                                        reads=[t_wA, t_xT[ch], t_xT2[ch]], writes=[tPS[bank]])
                            rope(bq, bs, kper[R, :], t_kper)
                            S.op("act", C("copy", out=KT[R, :, cs], in_=kper[R, :].unsqueeze(1).broadcast_to([32, 8, 512])),
                                 reads=[t_kper, t_zero], writes=[])
                            for h in range(8):
                                bq, bs = nb(), nb()
                                for (bank, col0) in ((bq, h * 96), (bs, 768 + h * 96)):
                                    for c in range(3):
                                        S.op("pe", C("matmul",
                                            PS[bank][0:96, :], lhsT=wuq[:, c, col0:col0 + 96], rhs=cqn[:, c, :], start=(c == 0), stop=(c == 2)),
                                            reads=[t_wA, t_cqn], writes=[tPS[bank]])
                                S.op("act", C("copy", out=QT[0:64, h, cs], in_=PS[bq][0:64, :]), reads=[tPS[bq]], writes=[])
                                rope(bq, bs, QT[R, h, cs], None)
                                bank = nb()
                                for c in range(2):
                                    S.op("pe", C("matmul",
                                        PS[bank][0:64, :], lhsT=wukv[:, c, h * 64:(h + 1) * 64], rhs=ckvn[:, c, :], start=(c == 0), stop=(c == 1)),
                                        reads=[t_wA, t_ckvn], writes=[tPS[bank]])
                                S.op("act", C("copy", out=KT[0:64, h, cs], in_=PS[bank][0:64, :]), reads=[tPS[bank]], writes=[])
                            for tt in range(4):
                                bank = nb()
                                for c in range(2):
                                    S.op("pe", C("matmul",
                                        PS[bank][:], lhsT=ckvn[:, c, tt * 128:(tt + 1) * 128], rhs=wukv[:, c, 512:1024], start=(c == 0), stop=(c == 1)),
                                        reads=[t_wA, t_ckvn], writes=[tPS[bank]])
                                S.op("dve", C("tensor_copy", out=Vm[:, ch * 4 + tt, :], in_=PS[bank][:]), reads=[tPS[bank]], writes=[])
                    A.hi_ptr = mkA
                    S.barrier()
                    if stage == 1 and dbg:
                        dbg_dump("cqf", cqf[:], [128, 3, 512], F32)
                        dbg_dump("wuq", wuq[:], [128, 3, 1536], BF16)
                        dbg_dump("wukv", wukv[:], [128, 2, 1024], BF16)
                        dbg_dump("sqf", sqf[:], [128, 3, 512], F32)
                        dbg_dump("rstd", rstd[:], [128, 512], F32)
                        dbg_dump("cqn", cqn[:], [128, 3, 512], BF16)
                        dbg_dump("ckvn", ckvn[:], [128, 2, 512], BF16)
                        dbg_dump("cosT", cosT[64:96, :], [32, 512], F32)
                        dbg_dump("sinS", sinS[64:96, :], [32, 512], F32)
                    if stage == 1:
                        dbg_dump("QT", QT[0:96, :, :], [96, 8, SEQ], BF16)
                        dbg_dump("KT", KT[0:96, :, :], [96, 8, SEQ], BF16)
                        dbg_dump("Vm", Vm[:], [128, NT, 512], BF16)
                        dbg_dump("xT", xT[:], [128, 8, SEQ], BF16)
                    if stage >= 2:
                        if True:
                            if b == 0:
                                casts(1)
                            S.dma("sp", C("dma_start", out=xT_s, in_=xT[:]), "xTs", reads=[t_xTs], writes=[t_xTs])
                            attention(S, nc, "hi", PS, tPS, mode="mla", QT=QT, KT=KT, V=Vm, OT=OmT, ones_b=ones_b, tri=tri,
                                      t_const=t_const, sbuf=sbuf)
                        S.barrier()
                A.hi_ptr = HI_TOP
                if stage == 2:
                    dbg_dump("OmT", OmT[:], [128, 4, SEQ], BF16)
                if stage < 3:
                    continue
                if True:
                    dv = sbuf("hi", "dv", [128, NT, 1024], BF16)
                    dqz = sbuf("hi", "dqz", [128, 8, 8, 512], BF16)
                    dkT = sbuf("hi", "dkT", [128, 8, SEQ], BF16)
                    if True:
                        mkC = A.hi_ptr
                        if b == 0:
                            casts(2)
                        wdv = sbuf("hi", "wdv", [128, 8, 1024], BF16)
                        t_wdv = S.tok()
                        S.dma("sp", C("dma_start", out=wdv[:], in_=wdv_s), "wdv", reads=[tw["dv"]], writes=[t_wdv])
                        t_dv = S.tok()
                        t_dqz = S.tok()
                        S.op("pool", C("memset", dqz[0:64, :, :, 256:512], 0.0), writes=[])
                        S.op("pool", C("memset", dqz[64:128, :, :, 0:256], 0.0), writes=[])
                        k = 0
                        for t in range(NT):
                            for j in range(2):
                                bank = k % 8
                                k += 1
                                for c in range(8):
                                    S.op("pe", C("matmul",
                                        PS[bank][:], lhsT=xT[:, c, t * 128:(t + 1) * 128], rhs=wdv[:, c, j * 512:(j + 1) * 512],
                                        start=(c == 0), stop=(c == 7)), reads=[t_wdv], writes=[tPS[bank]])
                                dst = dv[:, t, j * 512:(j + 1) * 512]
                                if k % 2:
                                    S.op("act", C("copy", out=dst, in_=PS[bank][:]), reads=[tPS[bank]], writes=[])
                                else:
                                    S.op("dve", C("tensor_copy", out=dst, in_=PS[bank][:]), reads=[tPS[bank]], writes=[])
                        A.hi_ptr = mkC
                        S.barrier()
                        wq = [sbuf("hi", "wq%d" % i, [128, 8, 256], BF16) for i in range(2)]
                        t_wq = [S.tok(), S.tok()]
                        for h in range(2):
                            S.dma("sp", C("dma_start", out=wq[h][:], in_=wdqk_s[h]), "wq%d" % h, reads=[tw["dqk"]], writes=[t_wq[h]])
                        for h in range(8):
                            hb = h % 2
                            for ch in range(NCH):
                                cs = slice(ch * 512, (ch + 1) * 512)
                                for which in range(2):
                                    bank = k % 8
                                    k += 1
                                    for c in range(8):
                                        S.op("pe", C("matmul", PS[bank][:], lhsT=wq[hb][:, c, which * 128:(which + 1) * 128], rhs=xT[:, c, cs],
                                                     start=(c == 0), stop=(c == 7)), reads=[t_wq[hb]], writes=[tPS[bank]])
                                    if which == 0:
                                        top_src = PS[bank][0:64, :].rearrange("p (b n) -> p b n", b=2)
                                        bot_src = PS[bank][64:128, :].rearrange("p (b n) -> p b n", b=2)
                                        S.op("act", C("copy", out=dqz[0:64, h, 2 * ch:2 * ch + 2, 0:256], in_=top_src), reads=[tPS[bank]], writes=[])
                                        S.op("dve", C("tensor_copy", out=dqz[64:128, h, 2 * ch:2 * ch + 2, 256:512], in_=bot_src), reads=[tPS[bank]], writes=[])
                                    elif k % 2:
                                        S.op("act", C("copy", out=dkT[:, h, cs], in_=PS[bank][:]), reads=[tPS[bank]], writes=[])
                                    else:
                                        S.op("dve", C("tensor_copy", out=dkT[:, h, cs], in_=PS[bank][:]), reads=[tPS[bank]], writes=[])
                            if h + 2 < 8:
                                S.dma("sp", C("dma_start", out=wq[hb][:], in_=wdqk_s[h + 2]), "wq%d" % hb, reads=[tw["dqk"]], writes=[t_wq[hb]])
                    A.hi_ptr = mkC
                    S.barrier()
                    if b == 0:
                        casts(3)
                    save = (A.lo_ptr, A.hi_ptr)
                    A.lo_ptr, A.hi_ptr = XT_OFF, XT_OFF + 32768
                    xt_tmps = {"distc": sbuf("lo", "distc", [128, 16, 256], mybir.dt.int16),
                               "sbq": [sbuf("lo", "sbq%d" % i, [128, 512], F32) for i in range(6)],
                               "pt": [sbuf("lo", "pt%d" % i, [128, 512], BF16) for i in range(8)],
                               "e_r": [sbuf("lo", "e_r0", [128, 512], F32)],
                               "ost": [sbuf("lo", "ost%d" % i, [128, 256], BF16) for i in range(2)]}
                    A.lo_ptr, A.hi_ptr = save
                    xt_tmps["e_r"].append(sbuf("hi", "e_r1", [128, 512], F32))
                    xt_tmps["e_t"] = [sbuf("hi", "e_t%d" % i, [128, 512], F32) for i in range(2)]
                    xt_tmps["e_od"] = [sbuf("hi", "e_od%d" % i, [128, 256], F32) for i in range(2)]
                    xt_tmps["e_sq"] = [sbuf("hi", "e_sq%d" % i, [128, 256], F32) for i in range(2)]
                    xt_tmps["e_rs"] = [sbuf("hi", "e_rs%d" % i, [128, 256], F32) for i in range(2)]
                    attention(S, nc, "hi", PS, tPS, mode="diff", dqz=dqz, dkT=dkT, V=dv, OT=OdT_s, ones_b=ones_b, ones_f=ones_f, tri=tri,
                              t_const=t_const, sbuf=sbuf, posf=posf, npk=npk, nlam=nlam, gsub=gsub,
                              slopes=slopes, eps_rms=eps_rms, tmps=xt_tmps, t_ods=t_ods)
                    S.barrier()
                A.hi_ptr = HI_TOP
                if stage == 3 and dbg:
                    dbg_dump("dqz", dqz[:], [128, 8, 8, 512], BF16)
                    dbg_dump("dkT", dkT[:], [128, 8, SEQ], BF16)
                    dbg_dump("distc", xt_tmps["distc"][:], [128, 16, 256], mybir.dt.int16)
                OdT = sbuf("lo", "OdT", [128, 8, SEQ], BF16)
                t_odr = S.tok()
                S.dma("sp", C("dma_start", out=OdT[:], in_=OdT_s), "odr", reads=[t_ods], writes=[t_odr])
                if stage == 3:
                    dbg_dump("OdT", OdT[:], [128, 8, SEQ], BF16)
                if stage < 4:
                    continue
                x1 = None
                if True:
                    mixT = sbuf("hi", "mixT", [128, 8, SEQ], BF16)
                    if True:
                        mkD = A.hi_ptr
                        wm = [sbuf("hi", "wm%d" % i, [128, 28, 128], BF16) for i in range(2)]
                        t_wm = [S.tok(), S.tok()]
                        g0 = sbuf("hi", "g0", [128, 512], F32)
                        g1 = sbuf("hi", "g1", [128, 512], F32)
                        u0 = sbuf("hi", "u0", [128, 512], F32)
                        u1 = sbuf("hi", "u1", [128, 512], F32)
                        t_g0, t_g1, t_u0, t_u1, t_mix = S.tok(), S.tok(), S.tok(), S.tok(), S.tok()
                        S.dma("sp", C("dma_start", out=wm[0][:], in_=wmg_s[0]), "wm0", reads=[tw["mg"]], writes=[t_wm[0]])
                        t_xTr = S.tok()
                        S.dma("sp", C("dma_start", out=xT[:], in_=xT_s), "xTr", reads=[t_xTs], writes=[t_xTr])
                        k = 0
                        for m in range(8):
                            w = wm[m % 2]
                            if m + 1 < 8:
                                S.dma("sp", C("dma_start", out=wm[(m + 1) % 2][:], in_=wmg_s[m + 1]), "wm%d" % ((m + 1) % 2),
                                      reads=[tw["mg"]], writes=[t_wm[(m + 1) % 2]])
                            for ch in range(NCH):
                                cs = slice(ch * 512, (ch + 1) * 512)
                                b_ym, b_yd, b_g0, b_g1 = [(k * 4 + i) % 8 for i in range(4)]
                                k += 1
                                for c in range(4):
                                    S.op("pe", C("matmul", PS[b_ym][:], lhsT=w[:, c, :], rhs=OmT[:, c, cs],
                                                                                         start=(c == 0), stop=(c == 3)),
                                         reads=[t_wm[m % 2]], writes=[tPS[b_ym]])
                                for c in range(8):
                                    S.op("pe", C("matmul", PS[b_yd][:], lhsT=w[:, 4 + c, :], rhs=OdT[:, c, cs],
                                                                                         start=(c == 0), stop=(c == 7)),
                                         reads=[t_wm[m % 2], t_odr], writes=[tPS[b_yd]])
                                for c in range(8):
                                    S.op("pe", C("matmul", PS[b_g0][:], lhsT=w[:, 12 + c, :], rhs=xT[:, c, cs],
                                                                                         start=(c == 0), stop=(c == 7)),
                                         reads=[t_wm[m % 2], t_xTr], writes=[tPS[b_g0]])
                                for c in range(8):
                                    S.op("pe", C("matmul", PS[b_g1][:], lhsT=w[:, 20 + c, :], rhs=xT[:, c, cs],
                                                                                         start=(c == 0), stop=(c == 7)),
                                         reads=[t_wm[m % 2], t_xTr], writes=[tPS[b_g1]])
                                S.op("act", C("activation", out=g0[:], in_=PS[b_g0][:], func=AF.Sigmoid, bias=bg[:, m, :]),
                                     reads=[tPS[b_g0], t_small], writes=[t_g0])
                                S.op("act", C("activation", out=g1[:], in_=PS[b_g1][:], func=AF.Sigmoid, bias=bg[:, 8 + m, :]),
                                     reads=[tPS[b_g1], t_small], writes=[t_g1])
                                S.op("dve", C("tensor_tensor", out=u0[:], in0=g0[:], in1=PS[b_ym][:], op=ALU.mult),
                                     reads=[t_g0, tPS[b_ym]], writes=[t_u0])
                                S.op("dve", C("tensor_tensor", out=u1[:], in0=g1[:], in1=PS[b_yd][:], op=ALU.mult),
                                     reads=[t_g1, tPS[b_yd]], writes=[t_u1])
                                S.op("pool", C("tensor_tensor", out=mixT[:, m, cs], in0=u0[:], in1=u1[:], op=ALU.add),
                                     reads=[t_u0, t_u1], writes=[t_mix])
                    A.hi_ptr = mkD
                    A.lo_ptr = LO_BASE
                    S.barrier()
                    if stage == 4:
                        dbg_dump("mixT", mixT[:], [128, 8, SEQ], BF16)
                        continue
                    x1 = sbuf("lo", "x1", [128, NT, D], F32)
                    lnp = sbuf("lo", "lnp", [128, 4, D], F32)
                    for i in range(4):
                        S.dma("sp", C("dma_start", out=lnp[:, i, :], in_=ln_in[i].partition_broadcast(128)), "lnp", writes=[t_small])
                    if True:
                        wo = sbuf("hi", "wo", [128, 8, 1024], BF16)
                        t_wo = S.tok()
                        S.dma("sp", C("dma_start", out=wo[:], in_=wo_s), "wo", reads=[tw["o"]], writes=[t_wo])
                        xin = [sbuf("hi", "xin2_%d" % i, [128, D], F32) for i in range(4)]
                        t_xin = [S.tok() for _ in range(4)]
                        t_x1 = [S.tok() for _ in range(NT)]
                        lnw = ln_work(S, nc, "hi", sbuf)
                        prev_st2 = None
                        for t in range(NT):
                            xi = xin[t % 4]
                            S.dma("sp", C("dma_start", out=xi[:], in_=x[b, t * 128:(t + 1) * 128, :]),
                                  "xin2_%d" % (t % 4), writes=[t_xin[t % 4]])
                            banks = [(2 * t) % 8, (2 * t + 1) % 8]
                            for j in range(2):
                                for c in range(8):
                                    S.op("pe", C("matmul",
                                        PS[banks[j]][:], lhsT=mixT[:, c, t * 128:(t + 1) * 128], rhs=wo[:, c, j * 512:(j + 1) * 512],
                                        start=(c == 0), stop=(c == 7)), reads=[t_wo], writes=[tPS[banks[j]]])
                            st2 = resid_ln(S, lnw, xi, t_xin[t % 4], [PS[banks[0]], PS[banks[1]]], [tPS[banks[0]], tPS[banks[1]]],
                                           lnp, 0, x1[:, t, :], t_x1[t], t_small, eps_ln)
                            if prev_st2 is not None:
                                prev_st2()
                            prev_st2 = st2
                        prev_st2()
                    A.hi_ptr = HI_TOP
                    S.barrier()
                    if stage == 5:
                        dbg_dump("x1", x1[:], [128, NT, D], F32)
                        continue
                    if True:
                        ring = [sbuf("hi", "ring%d" % i, [128, 4096], BF16) for i in range(NRING)]
                        t_ring = [S.tok() for _ in range(NRING)]
                        x1T = sbuf("hi", "x1T", [128, 8, 512], BF16)
                        hT = sbuf("hi", "hT", [128, 32, 512], BF16)
                        rl = [sbuf("hi", "rl%d" % i, [128, 512], F32) for i in range(4)]
                        ost = [sbuf("hi", "ost%d" % i, [128, D], F32) for i in range(2)]
                        t_x1T, t_hT, t_hT2 = S.tok(), S.tok(), S.tok()
                        t_rl = [S.tok() for _ in range(4)]
                        t_ost = [S.tok(), S.tok()]
                        lnw = ln_work(S, nc, "hi", sbuf)
                        t_x1c = S.tok()

                        def ring_load(src, key):
                            i = ring_ctr[0] % NRING
                            ring_ctr[0] += 1
                            S.dma("sp", C("dma_start", out=ring[i][:], in_=src), "ring%d" % i, reads=[tw[key]], writes=[t_ring[i]])
                            return i

                        oc = 0
                        PD = 3
                        blocks = []
                        for _ch in range(NCH):
                            blocks += [(wup_s[fb].rearrange("p c n -> p (c n)"), "up") for fb in range(8)]
                            blocks += [(wdn_s[fb].rearrange("p c n -> p (c n)"), "dn") for fb in range(8)]
                        loaded = []
                        nxt = [0]

                        def next_block():
                            while nxt[0] < len(blocks) and len(loaded) < PD:
                                loaded.append(ring_load(*blocks[nxt[0]]))
                                nxt[0] += 1
                            ri = loaded.pop(0)
                            while nxt[0] < len(blocks) and len(loaded) < PD:
                                loaded.append(ring_load(*blocks[nxt[0]]))
                                nxt[0] += 1
                            return ri

                        for ch in range(NCH):
                            for tt in range(4):
                                t = ch * 4 + tt
                                for half in range(2):
                                    bank = (2 * tt + half) % 8
                                    for c4 in range(4):
                                        c = half * 4 + c4
                                        S.op("pe", C("transpose",
                                            out=PS[bank][:, c4 * 128:(c4 + 1) * 128], in_=x1[:, t, c * 128:(c + 1) * 128], identity=ident[:]),
                                            reads=[t_x1c, t_const], writes=[tPS[bank]])
                                    dst = x1T[:, half * 4:(half + 1) * 4, tt * 128:(tt + 1) * 128]
                                    src = PS[bank][:].rearrange("p (c n) -> p c n", c=4)
                                    if half == 0:
                                        S.op("act", C("copy", out=dst, in_=src), reads=[tPS[bank]], writes=[t_x1T])
                                    else:
                                        S.op("dve", C("tensor_copy", out=dst, in_=src), reads=[tPS[bank]], writes=[t_x1T])
                            k = 0
                            for fb in range(8):
                                ri = next_block()
                                wv = ring[ri][:].rearrange("p (c n) -> p c n", c=8)
                                for f4 in range(4):
                                    f = fb * 4 + f4
                                    bank = k % 8
                                    k += 1
                                    for c in range(8):
                                        S.op("pe", C("matmul",
                                            PS[bank][:], lhsT=wv[:, c, f4 * 128:(f4 + 1) * 128], rhs=x1T[:, c, :], start=(c == 0), stop=(c == 7)),
                                            reads=[t_ring[ri], t_x1T], writes=[tPS[bank]])
                                    r = rl[f % 4]
                                    S.op("act", C("activation", out=r[:], in_=PS[bank][:], func=AF.Relu),
                                         reads=[tPS[bank]], writes=[t_rl[f % 4]])
                                    if f % 4 != 3:
                                        S.op("dve", C("tensor_tensor", out=hT[:, f, :], in0=r[:], in1=r[:], op=ALU.mult),
                                             reads=[t_rl[f % 4]], writes=[t_hT])
                                    else:
                                        S.op("pool", C("tensor_tensor", out=hT[:, f, :], in0=r[:], in1=r[:], op=ALU.mult),
                                             reads=[t_rl[f % 4]], writes=[t_hT2])
                            for fb in range(8):
                                ri = next_block()
                                wv = ring[ri][:].rearrange("p (c n) -> p c n", c=4)
                                for tt in range(4):
                                    for j in range(2):
                                        bank = tt * 2 + j
                                        for fi in range(4):
                                            f = fb * 4 + fi
                                            S.op("pe", C("matmul",
                                                PS[bank][:], lhsT=hT[:, f, tt * 128:(tt + 1) * 128], rhs=wv[:, fi, j * 512:(j + 1) * 512],
                                                start=(f == 0), stop=(f == 31)), reads=[t_ring[ri], t_hT, t_hT2], writes=[tPS[bank]])
                            prev_fin = None
                            for tt in range(4):
                                t = ch * 4 + tt
                                o = ost[oc % 2]
                                t_o = t_ost[oc % 2]
                                slot = "ost%d" % (oc % 2)
                                oc += 1
                                st2 = resid_ln(S, lnw, x1[:, t, :], t_x1c, [PS[tt * 2], PS[tt * 2 + 1]], [tPS[tt * 2], tPS[tt * 2 + 1]],
                                               lnp, 2, o[:], t_o, t_small, eps_ln, src_is_ap=True)

                                def fin(st2=st2, o=o, t=t, t_o=t_o, slot=slot):
                                    st2()
                                    S.dma("sp", C("dma_start", out=out[b, t * 128:(t + 1) * 128, :], in_=o[:]), slot, reads=[t_o])
                                if prev_fin is not None:
                                    prev_fin()
                                prev_fin = fin
                            prev_fin()
                    S.barrier()
        S.emit(top)
    return nc, dbg_out


def ln_work(S, nc, st, sbuf):
    w = {"i": 0, "sets": []}
    for i in range(2):
        d = {
            "r": sbuf(st, "ln_r%d" % i, [128, D], F32),
            "st": sbuf(st, "ln_st%d" % i, [128, 2, 6], F32),
            "mv": sbuf(st, "ln_mv%d" % i, [128, 2], F32),
            "rs": sbuf(st, "ln_rs%d" % i, [128, 1], F32),
            "nm": sbuf(st, "ln_nm%d" % i, [128, 1], F32),
            "t": sbuf(st, "ln_t%d" % i, [128, D], F32),
            "tok": S.tok(), "tok2": S.tok(),
        }
        w["sets"].append(d)
    return w


def resid_ln(S, lnw, xsrc, t_x, banks, t_banks, lnp, gi, dst, t_dst, t_small, eps_ln, src_is_ap=False):
    d = lnw["sets"][lnw["i"] % 2]
    lnw["i"] += 1
    r, stt, mv, rs, tmp, tk, tk2 = d["r"], d["st"], d["mv"], d["rs"], d["t"], d["tok"], d["tok2"]
    for j in range(2):
        xs = xsrc[:, j * 512:(j + 1) * 512]
        S.op("dve", C("scalar_tensor_tensor", out=r[:, j * 512:(j + 1) * 512], in0=xs, scalar=ALPHA,
                                                                in1=banks[j][:], op0=ALU.mult, op1=ALU.add),
             reads=[t_x, t_banks[j]], writes=[tk])
        S.op("dve", C("bn_stats", out=stt[:, j, :], in_=r[:, j * 512:(j + 1) * 512]), reads=[tk], writes=[tk])
    S.op("dve", C("bn_aggr", out=mv[:], in_=stt[:].rearrange("p a b -> p (a b)")), reads=[tk], writes=[tk])
    S.op("act", C("activation", out=rs[:], in_=mv[:, 1:2], func=AF.Sqrt, bias=eps_ln[:, 0:1]), reads=[tk], writes=[tk])
    S.op("dve", C("reciprocal", out=rs[:], in_=rs[:]), reads=[tk], writes=[tk])
    S.op("dve", C("scalar_tensor_tensor", out=d["nm"][:], in0=mv[:, 0:1], scalar=-1.0, in1=rs[:], op0=ALU.mult, op1=ALU.mult), reads=[tk], writes=[tk])
    S.op("act", C("activation", out=tmp[:], in_=r[:], func=AF.Identity, scale=rs[:, 0:1], bias=d["nm"][:, 0:1]), reads=[tk], writes=[tk2])
    def stage2():
        S.op("dve", C("tensor_tensor", out=tmp[:], in0=tmp[:], in1=lnp[:, gi, :], op=ALU.mult), reads=[tk2, t_small], writes=[tk2])
        S.op("pool", C("tensor_tensor", out=dst, in0=tmp[:], in1=lnp[:, gi + 1, :], op=ALU.add), reads=[tk2, t_small], writes=[t_dst])
    return stage2


def attention(S, nc, st, PS, tPS, mode, sbuf, ones_b, tri, t_const, V, OT, **kw):
    NPT = 4 if mode == "mla" else 8
    LAG = 3 if mode == "mla" else 5
    NSBQ = 6
    sctr = [0]
    ictr = [0]

    def sbank():
        sctr[0] += 1
        return sctr[0] % 4

    if mode == "mla":
        QT, KT = kw["QT"], kw["KT"]
        pt = [sbuf(st, "pt%d" % i, [128, 512], BF16) for i in range(NPT)]
        rlt = [sbuf(st, "rlt%d" % i, [128, 512], F32) for i in range(2)]
        t_rlt = [S.tok(), S.tok()]
    else:
        dqz, dkT = kw["dqz"], kw["dkT"]
        posf, npk, nlam, gsub, slopes = kw["posf"], kw["npk"], kw["nlam"], kw["gsub"], kw["slopes"]
        ones_f, eps_rms, t_ods = kw["ones_f"], kw["eps_rms"], kw["t_ods"]
        tm = kw["tmps"]
        pt, sbq, distc = tm["pt"], tm["sbq"], tm["distc"]
        e_r2, e_t2, e_od2, e_sq2, e_rs2, ost = tm["e_r"], tm["e_t"], tm["e_od"], tm["e_sq"], tm["e_rs"], tm["ost"]
        t_e2 = [S.tok(), S.tok()]
        t_er2 = [S.tok(), S.tok()]
        t_et2 = [S.tok(), S.tok()]
        t_sbq = [S.tok() for _ in range(NSBQ)]
        t_dist = [S.tok() for _ in range(16)]
        t_ost = [S.tok(), S.tok()]
        octr = [0]
    t_pt = [S.tok() for _ in range(NPT)]
    pending = []

    def issue_S(it):
        i = ictr[0]
        ictr[0] += 1
        it["pi"] = i % NPT
        bk = sbank()
        n0 = it["n0"]
        S.op("pe", C("matmul", PS[bk][:, n0:512], lhsT=it["lhsT"], rhs=it["rhs"], start=True, stop=True),
             reads=it["s_reads"], writes=[tPS[bk]])
        p = pt[it["pi"]]
        tp = t_pt[it["pi"]]
        if mode == "mla":
            S.op("act", C("activation", out=p[:, n0:512], in_=PS[bk][:, n0:512], func=AF.Exp, scale=MLA_SCALE),
                 reads=[tPS[bk]], writes=[tp])
        else:
            q = i % NSBQ
            sb_, tsb = sbq[q], t_sbq[q]
            kt = it["kt"]
            v3 = lambda ap: ap.rearrange("p (c n) -> p c n", c=2)
            dbc = distc[:, kt, :].unsqueeze(1).broadcast_to([128, 2, 256])
            S.op("dve", C("scalar_tensor_tensor", out=v3(sb_[:]), in0=dbc, scalar=it["bscale"], in1=v3(PS[bk][:]),
                          op0=ALU.mult, op1=ALU.add), reads=[t_dist[kt], tPS[bk]], writes=[tsb])
            S.op("act", C("activation", out=p[:], in_=sb_[:], func=AF.Exp, scale=DIFF_SCALE), reads=[tsb], writes=[tp])
        if mode == "mla" and it["diag"]:
            S.op("pool", C("tensor_tensor", out=p[:, n0:n0 + 128], in0=p[:, n0:n0 + 128], in1=tri[:], op=ALU.mult),
                 reads=[tp, t_const], writes=[tp])
        if it.get("after_S") is not None:
            it["after_S"]()

    deferred = []
    POST_LAG = 3
    STAGE_LAG = 2

    def defer(cnt, fn, tag):
        deferred.append({"cnt": cnt, "fn": fn, "tag": tag})

    def run_deferred():
        for d in deferred:
            d["cnt"] -= 1
        while deferred and deferred[0]["cnt"] <= 0:
            deferred.pop(0)["fn"]()

    def force_tag(tag):
        while any(d["tag"] == tag for d in deferred):
            deferred.pop(0)["fn"]()

    def issue_PV(it):
        p = pt[it["pi"]]
        tp = t_pt[it["pi"]]
        n0 = it["n0"]
        aO, aL = it["accO"], it["accL"]
        r0, r1 = it["rows"]
        if it["first"]:
            force_tag(("acc", aO))
        S.op("pe", C("matmul", PS[aO][r0:r1, n0:512], lhsT=it["vT"], rhs=p[:, n0:512], start=it["first"], stop=it["last"]),
             reads=[tp], writes=[tPS[aO]])
        S.op("pe", C("matmul", PS[aL][r0:r1, n0:512], lhsT=ones_b[:, 0:r1 - r0], rhs=p[:, n0:512], start=it["first"], stop=it["last"]),
             reads=[tp, t_const], writes=[tPS[aL]])
        if it["post"] is not None:
            defer(POST_LAG, it["post"], ("acc", aO))

    def push(it):
        issue_S(it)
        pending.append(it)
        if len(pending) > LAG:
            issue_PV(pending.pop(0))
        run_deferred()

    def flush():
        while pending:
            issue_PV(pending.pop(0))
        while deferred:
            deferred.pop(0)["fn"]()

    grp = 0
    if mode == "mla":
        for h in range(8):
            ho = (h % 2) * 64
            for qc in range(NCH):
                aO, aL = (4, 5) if grp % 2 == 0 else (6, 7)
                rl_, trl = rlt[grp % 2], t_rlt[grp % 2]
                grp += 1
                nk = 4 * qc + 4

                def post(h=h, qc=qc, aO=aO, aL=aL, ho=ho, rl_=rl_, trl=trl):
                    force_tag(("acc", aO))
                    S.op("act", C("activation", out=rl_[ho:ho + 64, :], in_=PS[aL][ho:ho + 64, :], func=AF.Ln), reads=[tPS[aL]], writes=[trl])
                    S.op("act", C("activation", out=rl_[ho:ho + 64, :], in_=rl_[ho:ho + 64, :], func=AF.Exp, scale=-1.0), reads=[trl], writes=[trl])

                    def post_b():
                        S.op("dve", C("tensor_tensor", out=OT[ho:ho + 64, h // 2, qc * 512:(qc + 1) * 512], in0=PS[aO][ho:ho + 64, :],
                                      in1=rl_[ho:ho + 64, :], op=ALU.mult), reads=[tPS[aO], trl], writes=[])
                    defer(STAGE_LAG, post_b, ("acc", aO))
                for kt in range(nk):
                    j = kt - 4 * qc
                    n0 = max(0, j) * 128
                    push({"n0": n0, "diag": j >= 0,
                          "lhsT": KT[:, h, kt * 128:(kt + 1) * 128], "rhs": QT[:, h, qc * 512 + n0:(qc + 1) * 512],
                          "s_reads": [], "vT": V[:, kt, h * 64:(h + 1) * 64],
                          "accO": aO, "accL": aL, "rows": (ho, ho + 64), "first": kt == 0, "last": kt == nk - 1,
                          "post": post if kt == nk - 1 else None})
        flush()
        return

    def abs_tile(qb, kt):
        S.op("act", C("activation", out=distc[:, kt, :], in_=posf[:, qb * 256:(qb + 1) * 256], func=AF.Abs, bias=npk[:, kt, :]),
             reads=[], writes=[t_dist[kt]])
        if kt >= 2 * qb:
            S.op("pool", C("affine_select", out=distc[:, kt, :], in_=distc[:, kt, :], pattern=[[1, 256]], base=-(kt - 2 * qb) * 128,
                           channel_multiplier=-1, compare_op=ALU.is_ge, fill=32000.0), reads=[t_dist[kt]], writes=[t_dist[kt]])

    for kt in range(2):
        abs_tile(0, kt)
    NQB = SEQ // 256
    for qb in range(NQB):
        nk = 2 * qb + 2
        for h in range(8):
            bscale = -slopes[h] / DIFF_SCALE
            aO, aL = (4, 5) if grp % 2 == 0 else (6, 7)
            grp += 1

            def post(h=h, qb=qb, aO=aO, aL=aL, gi=grp % 2):
                e_r, e_t = e_r2[gi], e_t2[gi]
                e_od, e_sq, e_rs, t_e, t_er, t_et = e_od2[gi], e_sq2[gi], e_rs2[gi], t_e2[gi], t_er2[gi], t_et2[gi]
                force_tag(("acc", aO))
                force_tag(("tmp", gi))
                S.op("act", C("activation", out=e_r[:], in_=PS[aL][:], func=AF.Ln), reads=[tPS[aL]], writes=[t_er])
                S.op("act", C("activation", out=e_r[:], in_=e_r[:], func=AF.Exp, scale=-1.0), reads=[t_er], writes=[t_er])

                def post_b():
                    S.op("dve", C("tensor_tensor", out=e_t[:], in0=PS[aO][:], in1=e_r[:], op=ALU.mult), reads=[tPS[aO], t_er], writes=[t_et])
                    S.op("dve", C("scalar_tensor_tensor", out=e_od[:], in0=e_t[:, 256:512], scalar=nlam, in1=e_t[:, 0:256], op0=ALU.mult, op1=ALU.add),
                         reads=[t_et], writes=[t_e])
                    S.op("pool", C("tensor_tensor", out=e_sq[:], in0=e_od[:], in1=e_od[:], op=ALU.mult), reads=[t_e], writes=[t_e])
                    defer(STAGE_LAG + 1, tail_a, ("tmp", gi))

                def tail_a():
                    bk = sbank()
                    S.op("pe", C("matmul", PS[bk][:, 0:256], lhsT=ones_f[:], rhs=e_sq[:], start=True, stop=True), reads=[t_e, t_const], writes=[tPS[bk]])
                    defer(STAGE_LAG, lambda: tail_b(bk), ("tmp", gi))

                def tail_b(bk):
                    S.op("act", C("activation", out=e_rs[:], in_=PS[bk][:, 0:256], func=AF.Ln, scale=1.0 / 128.0, bias=eps_rms[:, 0:1]),
                         reads=[tPS[bk], t_const], writes=[t_e])
                    S.op("act", C("activation", out=e_rs[:], in_=e_rs[:], func=AF.Exp, scale=-0.5), reads=[t_e], writes=[t_e])
                    defer(STAGE_LAG, tail_c, ("tmp", gi))

                def tail_c():
                    oi = octr[0] % 2
                    octr[0] += 1
                    S.op("dve", C("scalar_tensor_tensor", out=ost[oi][:], in0=e_od[:], scalar=gsub[:, 0:1], in1=e_rs[:],
                                  op0=ALU.mult, op1=ALU.mult), reads=[t_e], writes=[t_ost[oi]])
                    S.dma("sp", C("dma_start", out=OT[:, h, qb * 256:(qb + 1) * 256], in_=ost[oi][:]), "ods%d" % oi, reads=[t_ost[oi]], writes=[t_ods])

                defer(STAGE_LAG, post_b, ("acc", aO))

            for kt in range(nk):
                j = kt - 2 * qb
                after = None
                if h == 7 and qb + 1 < NQB:
                    after = (lambda qb=qb, kt=kt: abs_tile(qb + 1, kt))
                push({"n0": 0, "diag": j >= 0, "d0": max(0, j) * 128, "kt": kt, "bscale": bscale, "after_S": after,
                      "lhsT": dkT[:, h, kt * 128:(kt + 1) * 128], "rhs": dqz[:, h, qb, :],
                      "s_reads": [], "vT": V[:, kt, h * 128:(h + 1) * 128],
                      "accO": aO, "accL": aL, "rows": (0, 128), "first": kt == 0, "last": kt == nk - 1,
                      "post": post if kt == nk - 1 else None})
        if qb + 1 < NQB:
            for kt in range(nk, nk + 2):
                abs_tile(qb + 1, kt)
    flush()


def _host_inputs(inputs):
    f = lambda a: np.ascontiguousarray(np.asarray(a, dtype=np.float32))
    w_in = f(inputs["w_in"][0])
    kpe = w_in[:, O2_:O3_]
    kpe_sw = np.concatenate([kpe[:, 16:32], kpe[:, 0:16]], axis=1)
    w_kpe = np.concatenate([kpe, kpe, kpe, kpe_sw, kpe_sw, kpe_sw], axis=1)
    w_uq = f(inputs["w_uq"][0])
    w_uqs = w_uq.copy()
    for h in range(8):
        b0 = h * 96 + 64
        w_uqs[:, b0:b0 + 16] = w_uq[:, b0 + 16:b0 + 32]
        w_uqs[:, b0 + 16:b0 + 32] = w_uq[:, b0:b0 + 16]
    w_ukv = f(inputs["w_ukv"][0]).reshape(256, 8, 128)
    w_ukv_r = np.concatenate([w_ukv[:, :, 0:64].reshape(256, 512), w_ukv[:, :, 64:128].reshape(256, 512)], axis=1)
    inv_freq = (1.0 / (10000.0 ** (np.arange(0, 32, 2, dtype=np.float32) / 32.0))).astype(np.float32)
    cvec = np.zeros((128, 2), np.float32)
    cvec[64:80, 0] = inv_freq
    cvec[80:96, 0] = inv_freq
    cvec[64:80, 1] = -1.0
    cvec[80:96, 1] = 1.0
    shared = {
        "w_in": w_in, "w_kpe": np.ascontiguousarray(w_kpe), "w_uq": w_uq, "w_uqs": np.ascontiguousarray(w_uqs),
        "w_ukv": np.ascontiguousarray(w_ukv_r), "w_omla": f(inputs["w_o_mla"][0]), "w_odiff": f(inputs["w_o_diff"][0]),
        "w_o": f(inputs["w_o"][0]), "w_up": f(inputs["w_up"][0]), "w_down": f(inputs["w_down"][0]),
        "b_gate": f(inputs["b_gate"][0]), "g_q": f(inputs["mla_q_norm"][0]), "g_kv": f(inputs["mla_kv_norm"][0]),
        "g_sub": f(inputs["diff_subln"][0]),
        "lam0": f(inputs["diff_lambda_q1"]), "lam1": f(inputs["diff_lambda_k1"]),
        "lam2": f(inputs["diff_lambda_q2"]), "lam3": f(inputs["diff_lambda_k2"]),
        "ln0": f(inputs["ln1_g"]), "ln1": f(inputs["ln1_b"]), "ln2": f(inputs["ln2_g"]), "ln3": f(inputs["ln2_b"]),
        "cvec": cvec,
    }
    return shared


_NC_CACHE = {}


def kernel(**inputs):
    x = np.ascontiguousarray(np.asarray(inputs["x"], dtype=np.float32))
    positions = np.ascontiguousarray(np.asarray(inputs["positions"], dtype=np.int32))
    nseq = x.shape[0] // N_CORES
    shared = _host_inputs(inputs)
    if "nc" not in _NC_CACHE:
        _NC_CACHE["nc"] = build(nseq)[0]
    nc = _NC_CACHE["nc"]
    in_maps = []
    for i in range(N_CORES):
        m = dict(shared)
        m["x"] = x[i * nseq:(i + 1) * nseq]
        m["pos"] = positions[i * nseq:(i + 1) * nseq]
        in_maps.append(m)
    res = run_bass_kernel_spmd(nc, in_maps, core_ids=list(range(N_CORES)))
    return np.concatenate([r["out"] for r in res.results], axis=0)
```
